# Optimizing a Trainium2 kernel written in Bass

```python
import math
import jax, jax.numpy as jnp
from jax import lax
import numpy as np

D_MODEL = 2048
BATCH = 32
SEQ = 256
DEPTH = 2
DEC_BATCH = 2
DEC_SEQ = 1024
PAST_LEN = 256

GRID_W = 64
N_EVEN = (DEPTH + 1) // 2
N_ODD = DEPTH // 2
N_MOD = 9
D_FF = 5632
ROPE_THETA = 10000.0
Q_BLOCK = 128
EPS = 1e-6

MLA_HEADS = 8
MLA_Q_RANK = 512
MLA_KV_RANK = 512
MLA_NOPE = 128
MLA_ROPE = 64
MLA_V = 128

GQA_HEADS = 8
GQA_KV_HEADS = 2
GQA_GROUP = GQA_HEADS // GQA_KV_HEADS
HEAD_DIM = 128

EVEN_SPLITS = (MLA_Q_RANK, MLA_KV_RANK, MLA_ROPE, GQA_HEADS * HEAD_DIM, GQA_KV_HEADS * HEAD_DIM, GQA_KV_HEADS * HEAD_DIM)
EVEN_IN = MLA_Q_RANK + MLA_KV_RANK + MLA_ROPE + GQA_HEADS * HEAD_DIM + 2 * GQA_KV_HEADS * HEAD_DIM
EVEN_MIX = MLA_HEADS * MLA_V + GQA_HEADS * HEAD_DIM

RET_HEADS = 8
RET_DK = 128
RET_DV = 128
RET_CHUNK = 128
HY_CH = 1024
HY_ORDER = 2
HY_BANDS = 16
HY_EMB = 1 + 2 * HY_BANDS
HY_FHID = 64
ODD_SPLITS = (RET_HEADS * RET_DK, RET_HEADS * RET_DK, RET_HEADS * RET_DV, RET_HEADS * RET_DV, (HY_ORDER + 1) * HY_CH)
ODD_IN = 2 * RET_HEADS * RET_DK + 2 * RET_HEADS * RET_DV + (HY_ORDER + 1) * HY_CH
ODD_MIX = RET_HEADS * RET_DV + HY_CH

kernel_name = 'hybrid_diffusion_mla_gqa_retention_hyena'


def _split(u, sizes):
    return jnp.split(u, np.cumsum(np.array(sizes))[:-1].tolist(), axis=-1)


def _rms(x, g=None):
    xf = x.astype(jnp.float32)
    y = xf * lax.rsqrt(jnp.mean(xf * xf, axis=-1, keepdims=True) + EPS)
    if g is not None:
        y = y * g.astype(jnp.float32)
    return y.astype(x.dtype)


def _axial_rope(length, dim, dtype):
    f32 = jnp.float32
    n_rows = length // GRID_W
    row = jnp.repeat(jnp.arange(n_rows, dtype=f32), GRID_W)
    col = jnp.tile(jnp.arange(GRID_W, dtype=f32), n_rows)
    half = dim // 2
    freq = ROPE_THETA ** (-jnp.arange(0, half, 2, dtype=f32) / half)
    ang = jnp.concatenate([row[:, None] * freq[None], col[:, None] * freq[None]], axis=-1)
    return jnp.cos(ang).astype(dtype), jnp.sin(ang).astype(dtype)


def _rope(x, cos, sin):
    x1, x2 = jnp.split(x, 2, axis=-1)
    c = cos[:, None, :]
    s = sin[:, None, :]
    return jnp.concatenate([x1 * c - x2 * s, x1 * s + x2 * c], axis=-1)


def _attention(q, k, v):
    b, sq, hk, g, d = q.shape
    scale = d ** -0.5
    nb = sq // Q_BLOCK
    qb = jnp.moveaxis(q.reshape(b, nb, Q_BLOCK, hk, g, d), 1, 0)

    def block(qi):
        s = jnp.einsum('bqhgd,bkhd->bhgqk', qi, k, preferred_element_type=jnp.float32) * scale
        p = jax.nn.softmax(s, axis=-1).astype(v.dtype)
        return jnp.einsum('bhgqk,bkhv->bqhgv', p, v)

    o = lax.map(block, qb)
    return jnp.moveaxis(o, 0, 1).reshape(b, sq, hk * g * v.shape[-1])


def _even_mixer(h, w_in, q_norm, w_qb, kv_norm, w_kvb, nope_g, rope_g, gqa_g, w_out, cache):
    b, L, _ = h.shape
    q_a, kv_a, kr, qg, kg, vg = _split(h @ w_in, EVEN_SPLITS)
    q = (_rms(q_a, q_norm) @ w_qb).reshape(b, L, MLA_HEADS, MLA_NOPE + MLA_ROPE)
    q_nope = _rms(q[..., :MLA_NOPE], nope_g[0])
    q_rope = _rms(q[..., MLA_NOPE:], rope_g[0])
    ckv = _rms(kv_a, kv_norm)
    k_rope = _rms(kr, rope_g[1])
    qg = _rms(qg.reshape(b, L, GQA_HEADS, HEAD_DIM), gqa_g[0])
    kg = _rms(kg.reshape(b, L, GQA_KV_HEADS, HEAD_DIM), gqa_g[1])
    vg = vg.reshape(b, L, GQA_KV_HEADS, HEAD_DIM)
    if cache is None:
        state = (ckv, k_rope, kg, vg)
        ckv_all, krope_all, kg_all, vg_all = state
    else:
        cos_m, sin_m = _axial_rope(L, MLA_ROPE, h.dtype)
        cos_g, sin_g = _axial_rope(L, HEAD_DIM, h.dtype)
        q_rope = _rope(q_rope, cos_m, sin_m)
        k_rope = _rope(k_rope[:, :, None, :], cos_m, sin_m)[:, :, 0]
        qg = _rope(qg, cos_g, sin_g)
        kg = _rope(kg, cos_g, sin_g)
        c_ckv, c_krope, c_k, c_v = cache
        ckv_all = jnp.concatenate([c_ckv, ckv], axis=1)
        krope_all = jnp.concatenate([c_krope, k_rope], axis=1)
        kg_all = jnp.concatenate([c_k, kg], axis=1)
        vg_all = jnp.concatenate([c_v, vg], axis=1)
        state = None
    sk = ckv_all.shape[1]
    kv = (ckv_all @ w_kvb).reshape(b, sk, MLA_HEADS, MLA_NOPE + MLA_V)
    k_m = jnp.concatenate([_rms(kv[..., :MLA_NOPE], nope_g[1]),
                           jnp.broadcast_to(krope_all[:, :, None, :], (b, sk, MLA_HEADS, MLA_ROPE))], axis=-1)
    q_m = jnp.concatenate([q_nope, q_rope], axis=-1)[:, :, :, None, :]
    o_m = _attention(q_m, k_m, kv[..., MLA_NOPE:])
    o_g = _attention(qg.reshape(b, L, GQA_KV_HEADS, GQA_GROUP, HEAD_DIM), kg_all, vg_all)
    return jnp.concatenate([o_m, o_g], axis=-1) @ w_out, state


def _retention(q, k, v, log_g, s0):
    f32 = jnp.float32
    b, nh, L, _ = q.shape
    dv = v.shape[-1]
    n = L // RET_CHUNK
    idx = jnp.arange(RET_CHUNK, dtype=f32)
    lg = log_g.astype(f32)
    diff = idx[:, None] - idx[None, :]
    intra = jnp.where(diff >= 0, jnp.exp(jnp.maximum(diff, 0.0)[None] * lg[:, None, None]), 0.0)
    q_dec = jnp.exp((idx + 1.0)[None] * lg[:, None])
    k_dec = jnp.exp((RET_CHUNK - 1.0 - idx)[None] * lg[:, None])
    c_dec = jnp.exp(RET_CHUNK * lg)

    def chunks(x):
        return jnp.moveaxis(x.astype(f32).reshape(b, nh, n, RET_CHUNK, x.shape[-1]), 2, 0)

    def step(s, inp):
        qc, kc, vc = inp
        att = jnp.einsum('bhnd,bhmd->bhnm', qc, kc) * intra
        o = jnp.einsum('bhnm,bhmv->bhnv', att, vc) + jnp.einsum('bhnd,bhdv->bhnv', qc, s) * q_dec[..., None]
        s = s * c_dec[:, None, None] + jnp.einsum('bhmd,bhmv->bhdv', kc * k_dec[..., None], vc)
        return s, o

    s_fin, o = lax.scan(step, s0, (chunks(q), chunks(k), chunks(v)))
    return jnp.moveaxis(o, 0, 2).reshape(b, nh, L, dv), s_fin


def _bidir_retention(q, k, v, log_g, s0):
    o_f, s_f = _retention(q, k, v, log_g[0], s0[:, 0])
    flip = lambda x: jnp.flip(x, axis=2)
    o_b, s_b = _retention(flip(q), flip(k), flip(v), log_g[1], s0[:, 1])
    return o_f + flip(o_b), jnp.stack([s_f, s_b], axis=1)


def _short_conv(x, w, bias):
    xp = jnp.pad(x, ((0, 0), (1, 1), (0, 0)))
    return xp[:, :-2] * w[0] + xp[:, 1:-1] * w[1] + xp[:, 2:] * w[2] + bias


def _hyena_filters(L, w1, b1, w2, b2, freq, w3, decay):
    f32 = jnp.float32
    t = jnp.arange(L, dtype=f32)
    tn = t / L
    bands = jnp.arange(1, HY_BANDS + 1, dtype=f32)
    ang = 2.0 * math.pi * tn[:, None] * bands[None, :]
    feat = jnp.concatenate([tn[:, None], jnp.sin(ang), jnp.cos(ang)], axis=-1)
    z = jnp.sin(freq[0].astype(f32) * (feat @ w1.astype(f32) + b1.astype(f32)))
    z = jnp.sin(freq[1].astype(f32) * (z @ w2.astype(f32) + b2.astype(f32)))
    filt = z @ w3.astype(f32)
    r = jnp.abs(t - L // 2) / (L / 2)
    return filt * jnp.exp(-r[:, None] * jnp.abs(decay.astype(f32))[None, :])


def _fft_conv(z, h):
    L = z.shape[1]
    n = 2 * L
    zf = jnp.fft.rfft(z.astype(jnp.float32), n=n, axis=1)
    hf = jnp.fft.rfft(h, n=n, axis=0)
    y = jnp.fft.irfft(zf * hf[None], n=n, axis=1)
    return y[:, L // 2: L // 2 + L].astype(z.dtype)


def _hyena(u, conv_w, conv_b, w1, b1, w2, b2, freq, w3, decay, skip):
    L = u.shape[1]
    v, x1, x2 = jnp.split(_short_conv(u, conv_w, conv_b), HY_ORDER + 1, axis=-1)
    filt = _hyena_filters(L, w1, b1, w2, b2, freq, w3, decay)
    z = x1 * (_fft_conv(v, filt[:, :HY_CH]) + skip[:HY_CH] * v)
    return x2 * (_fft_conv(z, filt[:, HY_CH:]) + skip[HY_CH:] * z)


def _odd_mixer(h, w_in, decay_logit, ret_g, conv_w, conv_b, w1, b1, w2, b2, freq, w3, decay, skip, w_out, s0):
    b, L, _ = h.shape
    q, k, v, g, hy = _split(h @ w_in, ODD_SPLITS)
    q = q.reshape(b, L, RET_HEADS, RET_DK).transpose(0, 2, 1, 3)
    k = (k * (RET_DK ** -0.5)).reshape(b, L, RET_HEADS, RET_DK).transpose(0, 2, 1, 3)
    v = v.reshape(b, L, RET_HEADS, RET_DV).transpose(0, 2, 1, 3)
    log_g = jax.nn.log_sigmoid(decay_logit.astype(jnp.float32))
    o, state = _bidir_retention(q, k, v, log_g, s0.astype(jnp.float32))
    o = _rms(o.astype(h.dtype).transpose(0, 2, 1, 3)).reshape(b, L, RET_HEADS * RET_DV) * ret_g
    o_ret = o * jax.nn.silu(g)
    o_hy = _hyena(hy, conv_w, conv_b, w1, b1, w2, b2, freq, w3, decay, skip)
    return jnp.concatenate([o_ret, o_hy], axis=-1) @ w_out, state.astype(h.dtype)


def _modulate(x, mod, i):
    shift, scale, gate = mod[:, 3 * i], mod[:, 3 * i + 1], mod[:, 3 * i + 2]
    return _rms(x) * (1.0 + scale[:, None]) + shift[:, None], gate[:, None]


def _swiglu(h, wg, wu, wd):
    return (jax.nn.silu(h @ wg) * (h @ wu)) @ wd


def setup_inputs(seed: int = 0) -> dict:
    key = jax.random.key(seed)
    keys = jax.random.split(key, 64)
    counter = [0]
    f32 = jnp.float32

    def nrm(shape, scale=1.0):
        kk = keys[counter[0]]
        counter[0] += 1
        return jax.random.normal(kk, shape, f32) * scale

    def gain(shape):
        return 1.0 + nrm(shape, 0.02)

    base_logit = jnp.log(2.0 ** (5.0 + jnp.arange(RET_HEADS, dtype=f32)) - 1.0)
    decay_lo = -math.log(1e-2) / 1.5
    decay_hi = -math.log(1e-2) / 0.3
    return {
        'x_prompt': nrm((BATCH, SEQ, D_MODEL)),
        'x_sample': nrm((DEC_BATCH, DEC_SEQ, D_MODEL)),
        'cache_mla_ckv': nrm((DEC_BATCH, N_EVEN, PAST_LEN, MLA_KV_RANK)),
        'cache_mla_krope': nrm((DEC_BATCH, N_EVEN, PAST_LEN, MLA_ROPE)),
        'cache_gqa_k': nrm((DEC_BATCH, N_EVEN, PAST_LEN, GQA_KV_HEADS, HEAD_DIM)),
        'cache_gqa_v': nrm((DEC_BATCH, N_EVEN, PAST_LEN, GQA_KV_HEADS, HEAD_DIM)),
        'state_ret': nrm((DEC_BATCH, N_ODD, 2, RET_HEADS, RET_DK, RET_DV), 0.5),
        'c': nrm((DEC_BATCH, D_MODEL)),
        'c_ctx': nrm((D_MODEL,)),
        'mod_w': nrm((DEPTH, D_MODEL, N_MOD * D_MODEL), 0.5 * D_MODEL ** -0.5),
        'mod_b': nrm((DEPTH, N_MOD * D_MODEL), 0.02),
        'ffn_w_gate': nrm((DEPTH, 2, D_MODEL, D_FF), D_MODEL ** -0.5),
        'ffn_w_up': nrm((DEPTH, 2, D_MODEL, D_FF), D_MODEL ** -0.5),
        'ffn_w_down': nrm((DEPTH, 2, D_FF, D_MODEL), D_FF ** -0.5),
        'ev_w_in': nrm((N_EVEN, D_MODEL, EVEN_IN), D_MODEL ** -0.5),
        'mla_q_norm': gain((N_EVEN, MLA_Q_RANK)),
        'mla_w_qb': nrm((N_EVEN, MLA_Q_RANK, MLA_HEADS * (MLA_NOPE + MLA_ROPE)), MLA_Q_RANK ** -0.5),
        'mla_kv_norm': gain((N_EVEN, MLA_KV_RANK)),
        'mla_w_kvb': nrm((N_EVEN, MLA_KV_RANK, MLA_HEADS * (MLA_NOPE + MLA_V)), MLA_KV_RANK ** -0.5),
        'mla_nope_norm': gain((N_EVEN, 2, MLA_NOPE)),
        'mla_rope_norm': gain((N_EVEN, 2, MLA_ROPE)),
        'gqa_qk_norm': gain((N_EVEN, 2, HEAD_DIM)),
        'ev_w_out': nrm((N_EVEN, EVEN_MIX, D_MODEL), EVEN_MIX ** -0.5),
        'od_w_in': nrm((N_ODD, D_MODEL, ODD_IN), D_MODEL ** -0.5),
        'ret_decay_logit': base_logit[None, None, :] + nrm((N_ODD, 2, RET_HEADS), 0.1),
        'ret_norm': gain((N_ODD, RET_HEADS * RET_DV)),
        'hy_conv_w': nrm((N_ODD, 3, (HY_ORDER + 1) * HY_CH), 3 ** -0.5),
        'hy_conv_b': nrm((N_ODD, (HY_ORDER + 1) * HY_CH), 0.02),
        'hy_filt_w1': nrm((N_ODD, HY_EMB, HY_FHID), HY_EMB ** -0.5),
        'hy_filt_b1': nrm((N_ODD, HY_FHID), 0.02),
        'hy_filt_w2': nrm((N_ODD, HY_FHID, HY_FHID), HY_FHID ** -0.5),
        'hy_filt_b2': nrm((N_ODD, HY_FHID), 0.02),
        'hy_filt_freq': gain((N_ODD, 2, HY_FHID)),
        'hy_filt_w3': nrm((N_ODD, HY_FHID, HY_ORDER * HY_CH), 0.1 * HY_FHID ** -0.5),
        'hy_decay': jnp.linspace(decay_lo, decay_hi, HY_ORDER * HY_CH, dtype=f32)[None] + nrm((N_ODD, HY_ORDER * HY_CH), 0.1),
        'hy_skip': nrm((N_ODD, HY_ORDER * HY_CH), 0.5),
        'od_w_out': nrm((N_ODD, ODD_MIX, D_MODEL), ODD_MIX ** -0.5),
    }


def reference(x_prompt, x_sample, cache_mla_ckv, cache_mla_krope, cache_gqa_k, cache_gqa_v, state_ret,
              c, c_ctx, mod_w, mod_b, ffn_w_gate, ffn_w_up, ffn_w_down,
              ev_w_in, mla_q_norm, mla_w_qb, mla_kv_norm, mla_w_kvb, mla_nope_norm, mla_rope_norm, gqa_qk_norm, ev_w_out,
              od_w_in, ret_decay_logit, ret_norm, hy_conv_w, hy_conv_b, hy_filt_w1, hy_filt_b1, hy_filt_w2, hy_filt_b2,
              hy_filt_freq, hy_filt_w3, hy_decay, hy_skip, od_w_out):
    xc, xs = x_prompt, x_sample
    new_ckv, new_krope, new_k, new_v, new_ret = [], [], [], [], []
    for l in range(DEPTH):
        m_ctx = (jax.nn.silu(c_ctx)[None, :] @ mod_w[l] + mod_b[l]).reshape(1, N_MOD, D_MODEL)
        m_lat = (jax.nn.silu(c) @ mod_w[l] + mod_b[l]).reshape(c.shape[0], N_MOD, D_MODEL)

        def ffn(x, mod, i, j):
            h, g = _modulate(x, mod, i)
            return x + 0.5 * g * _swiglu(h, ffn_w_gate[l, j], ffn_w_up[l, j], ffn_w_down[l, j])

        xc = ffn(xc, m_ctx, 0, 0)
        xs = ffn(xs, m_lat, 0, 0)
        hc, gc = _modulate(xc, m_ctx, 1)
        hs, gs = _modulate(xs, m_lat, 1)
        if l % 2 == 0:
            e = l // 2
            w = (ev_w_in[e], mla_q_norm[e], mla_w_qb[e], mla_kv_norm[e], mla_w_kvb[e],
                 mla_nope_norm[e], mla_rope_norm[e], gqa_qk_norm[e], ev_w_out[e])
            oc, st = _even_mixer(hc, *w, None)
            os_, _ = _even_mixer(hs, *w, (cache_mla_ckv[:, e], cache_mla_krope[:, e], cache_gqa_k[:, e], cache_gqa_v[:, e]))
            new_ckv.append(st[0])
            new_krope.append(st[1])
            new_k.append(st[2])
            new_v.append(st[3])
        else:
            o = l // 2
            w = (od_w_in[o], ret_decay_logit[o], ret_norm[o], hy_conv_w[o], hy_conv_b[o], hy_filt_w1[o], hy_filt_b1[o],
                 hy_filt_w2[o], hy_filt_b2[o], hy_filt_freq[o], hy_filt_w3[o], hy_decay[o], hy_skip[o], od_w_out[o])
            s_zero = jnp.zeros((hc.shape[0], 2, RET_HEADS, RET_DK, RET_DV), hc.dtype)
            oc, st = _odd_mixer(hc, *w, s_zero)
            os_, _ = _odd_mixer(hs, *w, state_ret[:, o])
            new_ret.append(st)
        xc = xc + gc * oc
        xs = xs + gs * os_
        xc = ffn(xc, m_ctx, 2, 1)
        xs = ffn(xs, m_lat, 2, 1)
    return (xc, xs, jnp.stack(new_ckv, axis=1), jnp.stack(new_krope, axis=1), jnp.stack(new_k, axis=1),
            jnp.stack(new_v, axis=1), jnp.stack(new_ret, axis=1))
```

```python
import math
from contextlib import ExitStack
import numpy as np
import ml_dtypes
import concourse.bass as bass
import concourse.mybir as mybir
from concourse.bass_utils import run_bass_kernel_spmd

F32 = mybir.dt.float32
BF16 = mybir.dt.bfloat16
AF = mybir.ActivationFunctionType
ALU = mybir.AluOpType

NCORES = 8
D = 2048
KC = 16
T = 1280
NSLOT = 5
L = 256
DFF = 5632
NJ = DFF // 128
EPS = 1e-6
TBS = [(0, 512), (512, 512), (1024, 256)]
STOP_AFTER = None


class Sched:
    def __init__(self, nc, es):
        self.nc = nc
        self.es = es
        self.prog = {e: [] for e in ('pe', 'act', 'dve', 'pool', 'sp')}
        self.esem = {e: es.enter_context(nc.semaphore('sem_' + e)) for e in ('pe', 'act', 'dve', 'pool')}
        self.ecnt = {e: 0 for e in self.esem}
        self.dsem = {}
        self.dcnt = {}
        self.waited = {e: {} for e in self.prog}
        self.lastw = {}
        self.readers = {}
        self.bank = 0
        self.nops = 0
        self.pending = {e: [] for e in self.prog}

    def _deps(self, eng, reads, writes):
        toks = self.pending[eng]
        self.pending[eng] = []
        for r in reads:
            if r in self.lastw:
                toks.append(self.lastw[r])
        for w in writes:
            if w in self.lastw:
                toks.append(self.lastw[w])
            toks.extend(self.readers.get(w, ()))
        need = {}
        for (sid, sem, val) in toks:
            if self.waited[eng].get(sid, 0) >= val:
                continue
            if need.get(sid, (None, 0))[1] < val:
                need[sid] = (sem, val)
        for sid, (sem, val) in need.items():
            self.waited[eng][sid] = val
        return list(need.values())

    def fence(self):
        toks = [(e, self.esem[e], self.ecnt[e]) for e in self.esem if self.ecnt[e] > 0]
        toks += [('d_' + key, self.dsem[key], self.dcnt[key]) for key in self.dsem]
        for e in self.prog:
            self.pending[e] = list(toks)

    def _commit(self, tok, reads, writes):
        for r in reads:
            self.readers.setdefault(r, []).append(tok)
        for w in writes:
            self.lastw[w] = tok
            self.readers[w] = []

    def op(self, eng, fn, reads=(), writes=()):
        waits = self._deps(eng, reads, writes)
        self.ecnt[eng] += 1
        val = self.ecnt[eng]
        sem = self.esem[eng]
        tok = (eng, sem, val)
        def run(e, waits=waits, fn=fn, sem=sem):
            for (s, v) in waits:
                e.wait_ge(s, v)
            ins = fn(e)
            ins.then_inc(sem, 1)
        self.prog[eng].append(run)
        self._commit(tok, reads, writes)
        self.nops += 1

    def dma(self, q, out, in_, reads=(), writes=(), key=None):
        if key is None:
            key = writes[0]
        if key not in self.dsem:
            self.dsem[key] = self.es.enter_context(self.nc.semaphore('d_' + str(len(self.dsem))))
            self.dcnt[key] = 0
        waits = self._deps(q, reads, writes)
        self.dcnt[key] += 16
        sem = self.dsem[key]
        tok = ('d_' + key, sem, self.dcnt[key])
        def run(e, waits=waits, sem=sem, out=out, in_=in_):
            for (s, v) in waits:
                e.wait_ge(s, v)
            e.dma_start(out=out, in_=in_).then_inc(sem, 16)
        self.prog[q].append(run)
        self._commit(tok, reads, writes)

    def final_wait(self, q, keys):
        waits = self._deps(q, keys, ())
        def run(e, waits=waits):
            for (s, v) in waits:
                e.wait_ge(s, v)
        self.prog[q].append(run)

    def next_bank(self):
        b = self.bank
        self.bank = (self.bank + 1) % 8
        return b


def _mm(e, out, lhsT, rhs, start, stop):
    return e.matmul(out, lhsT, rhs, start=start, stop=stop)


def build_program(dbg=None):
    nc = bass.Bass("TRN2", target_bir_lowering=False)
    es = ExitStack()
    with es:
        def din(name, shape, dt=F32):
            return nc.dram_tensor(name, list(shape), dt, kind="ExternalInput").ap()

        def dout(name, shape, dt=F32):
            return nc.dram_tensor(name, list(shape), dt, kind="ExternalOutput").ap()

        xs = din("xs", [T, D])
        csT = din("csT", [128, KC, NSLOT])
        ident = din("ident", [128, 128])
        mod_w = din("mod_w", [2, D, 9 * D])
        mod_b = din("mod_b", [2, 9 * D])
        wg = din("ffn_w_gate", [2, 2, D, DFF])
        wu = din("ffn_w_up", [2, 2, D, DFF])
        wd = din("ffn_w_down", [2, 2, DFF, D])
        vec_in = din("vec", [128, 64])
        ev_w_in = din("ev_w_in", [1, D, 2624])
        mla_w_qb = din("mla_w_qb", [1, 512, 1536])
        mla_w_kvb = din("mla_w_kvb", [1, 512, 2048])
        ev_w_out = din("ev_w_out", [1, D, D])
        c_ckv = din("c_ckv", [256, 512])
        c_kr = din("c_kr", [256, 64])
        c_k = din("c_k", [256, 256])
        c_v = din("c_v", [256, 256])
        rt128_in = din("rt128", [128, 128])
        rt64_in = din("rt64", [64, 64])
        cos64_in = din("cos64", [64, 1024])
        sin64_in = din("sin64", [64, 1024])
        cos128_in = din("cos128", [128, 1024])
        sin128_in = din("sin128", [128, 1024])
        od_w_in = din("od_w_in", [1, D, 7168])
        od_w_out = din("od_w_out", [1, D, D])
        rc_in = din("rc", [6, 128, 128])
        pc_in = din("pc", [128, 4])
        lgt_in = din("lgt", [128, 16])
        s0_in = din("s0", [2, 8, 128, 128])
        identb_in = din("identb", [128, 128])
        dft_in = din("dft", [8, 256, 256])
        feat_in = din("feat", [33, 1280])
        hy_w1 = din("hy_filt_w1", [1, 33, 64])
        hy_w2 = din("hy_filt_w2", [1, 64, 64])
        hy_w3 = din("hy_filt_w3", [1, 64, 2048])
        fb_in = din("fb", [64, 4])
        vec2_in = din("vec2", [128, 128])
        negr_in = din("negr", [128, 10])
        decb_in = din("decb", [128, 2048])
        y = dout("y", [T, D])
        o_ret = dout("o_ret", [NSLOT, 2, 8, 128, 128])
        o_ckv = dout("o_ckv", [T, 512])
        o_kr = dout("o_kr", [T, 64])
        o_k = dout("o_k", [T, 256])
        o_v = dout("o_v", [T, 256])

        def sb(name, shape, dt):
            return es.enter_context(nc.sbuf_tensor(name, list(shape), dt))

        X = sb("X", [128, KC, T], F32)
        H = sb("H", [128, KC, T], BF16)
        MODV = sb("MODV", [128, 144, NSLOT], F32)
        IDF = sb("IDF", [128, 128], F32)
        IDB = sb("IDB", [128, 128], BF16)
        ONESB = sb("ONESB", [128, 128], BF16)
        ONESF = sb("ONESF", [128, 128], F32)
        SC = sb("SC", [128, KC, NSLOT], BF16)
        SCF = sb("SCF", [128, KC, NSLOT], F32)
        EPSD = sb("EPSD", [128, 4], F32)
        NV = 64
        VEC = sb("VEC", [128, NV], F32)
        ARENA_ELEMS = 41 * 1024 + 256
        ARENA = sb("ARENA", [128, ARENA_ELEMS], BF16)
        PS = [es.enter_context(nc.psum_tensor("ps%d" % i, [128, 512], F32)) for i in range(8)]

        k = Sched(nc, es)
        out_keys = []

        class Carver:
            def __init__(self, off=0, base=None, limit=None):
                self.off = off
                self.base = ARENA if base is None else base
                self.limit = ARENA_ELEMS if limit is None else limit
            def take(self, shape, dt):
                n = int(np.prod(shape))
                nb = n * (4 if dt == F32 else 2)
                ne = (nb + 1) // 2
                ne = (ne + 15) // 16 * 16
                a = self.base[:, self.off:self.off + ne]
                self.off += ne
                assert self.off <= self.limit, (self.off, self.limit)
                if dt == F32:
                    a = a.bitcast(F32)
                a = a[:, 0:n]
                if len(shape) == 2:
                    return a.rearrange("p (a b) -> p a b", a=shape[0])
                if len(shape) == 3:
                    return a.rearrange("p (a b c) -> p a b c", a=shape[0], b=shape[1])
                return a

        k.dma('sp', IDF[:], ident, writes=['IDF'])
        k.dma('pool', IDB[:], identb_in, writes=['IDB'])
        k.op('dve', lambda e: e.memset(ONESB[:], 1.0), writes=['ONESB'])
        k.op('dve', lambda e: e.memset(ONESF[:], 1.0), writes=['ONESF'])
        k.op('dve', lambda e: e.memset(EPSD[:, 0:1], float(D * EPS)), writes=['EPSD'])
        k.dma('sp', SCF[:], csT, writes=['SCF'])
        k.dma('sp', VEC[:], vec_in, writes=['VEC'])
        k.op('dve', lambda e: e.memset(EPSD[:, 1:2], float(EPS)), writes=['EPSD'])
        k.op('act', lambda e: e.activation(SC[:], SCF[:], AF.Silu), reads=['SCF'], writes=['SC'])

        cv = Carver()
        XT = [cv.take([KC * 128], F32) for _ in range(2)]
        for tt in range(T // 128):
            st = XT[tt % 2]
            k.dma('sp', st, xs[tt * 128:(tt + 1) * 128, :], writes=['XT%d' % (tt % 2)])
            for k4 in range(KC // 4):
                b = k.next_bank()
                def f(e, st=st, k4=k4, b=b):
                    ins = None
                    for q in range(4):
                        kk = k4 * 4 + q
                        ins = e.transpose(PS[b][:, q * 128:(q + 1) * 128], st[:, kk * 128:(kk + 1) * 128], IDF[:])
                    return ins
                k.op('pe', f, reads=['XT%d' % (tt % 2), 'IDF'], writes=['ps%d' % b])
                k.op('dve' if k4 % 2 == 0 else 'act',
                     (lambda e, b=b, k4=k4, tt=tt: e.tensor_copy(
                         X[:, k4 * 4:(k4 + 1) * 4, tt * 128:(tt + 1) * 128],
                         PS[b][:].rearrange("p (q c) -> p q c", q=4))) if k4 % 2 == 0 else
                     (lambda e, b=b, k4=k4, tt=tt: e.activation(
                         X[:, k4 * 4:(k4 + 1) * 4, tt * 128:(tt + 1) * 128],
                         PS[b][:].rearrange("p (q c) -> p q c", q=4), AF.Copy)),
                     reads=['ps%d' % b], writes=['X'])

        def mod_phase(l):
            k.fence()
            cvm = Carver()
            WM = [cvm.take([KC, 512], BF16) for _ in range(2)]
            BM = [cvm.take([512], BF16) for _ in range(2)]
            for g in range(36):
                s = g % 2
                k.dma('pool', WM[s], mod_w[l][:, g * 512:(g + 1) * 512].rearrange("(k p) c -> p k c", p=128),
                      writes=['WM%d' % s])
                k.dma('pool', BM[s][0:1, :], mod_b[l:l + 1, g * 512:(g + 1) * 512], writes=['BM%d' % s])
                b = k.next_bank()
                def f(e, s=s, b=b):
                    ins = None
                    for fc in range(4):
                        o = PS[b][:, fc * 8:fc * 8 + NSLOT]
                        for kk in range(KC):
                            e.matmul(o, WM[s][:, kk, fc * 128:(fc + 1) * 128], SC[:, kk, :], start=(kk == 0), stop=False)
                        ins = e.matmul(o, BM[s][0:1, fc * 128:(fc + 1) * 128], ONESB[0:1, 0:NSLOT], start=False, stop=True)
                    return ins
                k.op('pe', f, reads=['WM%d' % s, 'BM%d' % s, 'SC', 'ONESB'], writes=['ps%d' % b])
                k.op('dve', lambda e, b=b, g=g: e.tensor_copy(
                    MODV[:, g * 4:(g + 1) * 4, :],
                    PS[b][:, 0:32].rearrange("p (a c) -> p a c", a=4)[:, :, 0:NSLOT]),
                    reads=['ps%d' % b], writes=['MODV'])
            for i in (1, 4, 7):
                k.op('dve', lambda e, i=i: e.tensor_scalar_add(MODV[:, i * 16:(i + 1) * 16, :], MODV[:, i * 16:(i + 1) * 16, :], 1.0),
                     reads=['MODV'], writes=['MODV'])
            for i in (2, 8):
                k.op('dve', lambda e, i=i: e.tensor_scalar_mul(MODV[:, i * 16:(i + 1) * 16, :], MODV[:, i * 16:(i + 1) * 16, :], 0.5),
                     reads=['MODV'], writes=['MODV'])

        def prenorm(i, cvp):
            SQ = [cvp.take([512], BF16) for _ in range(2)]
            TMP = [cvp.take([T], F32) for _ in range(2)]
            RS = cvp.take([T], F32)
            for ti, (t0, tn) in enumerate(TBS):
                b = k.next_bank()
                for kk in range(KC):
                    s = kk % 2
                    k.op('act', lambda e, s=s, kk=kk, t0=t0, tn=tn: e.activation(SQ[s][:, 0:tn], X[:, kk, t0:t0 + tn], AF.Square),
                         reads=['X'], writes=['SQ%d' % s])
                    k.op('pe', lambda e, s=s, kk=kk, tn=tn, b=b: e.matmul(PS[b][:, 0:tn], ONESB[:], SQ[s][:, 0:tn], start=(kk == 0), stop=(kk == KC - 1)),
                         reads=['SQ%d' % s, 'ONESB'], writes=['ps%d' % b])
                k.op('act', lambda e, b=b, t0=t0, tn=tn: e.activation(RS[:, t0:t0 + tn], PS[b][:, 0:tn], AF.Sqrt, bias=EPSD[:, 0:1]),
                     reads=['ps%d' % b, 'EPSD'], writes=['RS'])
                k.op('dve', lambda e, t0=t0, tn=tn: e.reciprocal(RS[:, t0:t0 + tn], RS[:, t0:t0 + tn]),
                     reads=['RS'], writes=['RS'])
            for kk in range(KC):
                s = kk % 2
                k.op('dve', lambda e, s=s, kk=kk: e.scalar_tensor_tensor(TMP[s][:], X[:, kk, :], float(math.sqrt(D)), RS[:], ALU.mult, ALU.mult),
                     reads=['X', 'RS'], writes=['TMP%d' % s])
                for sl in range(NSLOT):
                    if sl in (1, 3):
                        k.op('dve', lambda e, s=s, kk=kk, sl=sl: e.tensor_scalar(
                            H[:, kk, sl * L:(sl + 1) * L], TMP[s][:, sl * L:(sl + 1) * L],
                            MODV[:, (3 * i + 1) * 16 + kk, sl:sl + 1], MODV[:, (3 * i) * 16 + kk, sl:sl + 1], ALU.mult, ALU.add),
                            reads=['TMP%d' % s, 'MODV'], writes=['H'])
                    else:
                        k.op('act', lambda e, s=s, kk=kk, sl=sl: e.activation(
                            H[:, kk, sl * L:(sl + 1) * L], TMP[s][:, sl * L:(sl + 1) * L], AF.Identity,
                            bias=MODV[:, (3 * i) * 16 + kk, sl:sl + 1], scale=MODV[:, (3 * i + 1) * 16 + kk, sl:sl + 1]),
                            reads=['TMP%d' % s, 'MODV'], writes=['H'])

        def ffn(l, j, i):
            k.fence()
            cvf = Carver()
            WG = [cvf.take([KC, 256], BF16) for _ in range(2)]
            WU = [cvf.take([KC, 256], BF16) for _ in range(2)]
            WD = [cvf.take([4, D], BF16) for _ in range(2)]
            a_off = cvf.off
            A = [cvf.take([2, T], BF16) for _ in range(3)]
            SIL = [cvf.take([512], F32)]
            NG = NJ // 2
            silc = [0]

            def gu_load(g):
                s = g % 2
                k.dma('pool', WG[s], wg[l, j][:, g * 256:(g + 1) * 256].rearrange("(k p) c -> p k c", p=128), writes=['WG%d' % s])
                k.dma('pool', WU[s], wu[l, j][:, g * 256:(g + 1) * 256].rearrange("(k p) c -> p k c", p=128), writes=['WU%d' % s])

            def dn_load(p):
                s = p % 2
                k.dma('pool', WD[s], wd[l, j][p * 512:(p + 1) * 512, :].rearrange("(c p) n -> p c n", p=128), writes=['WD%d' % s])

            gu_load(0)
            gu_load(1)
            dn_load(0)
            prenorm(i, Carver(off=a_off))
            k.fence()

            def gu(g):
                s = g % 2
                a3 = g % 3
                if g >= 2:
                    gu_load(g)
                for jj in range(2):
                    for (t0, tn) in TBS:
                        bg = k.next_bank()
                        bu = k.next_bank()
                        def fg(e, W=WG[s], b=bg, jj=jj, t0=t0, tn=tn):
                            ins = None
                            for kk in range(KC):
                                ins = e.matmul(PS[b][:, 0:tn], W[:, kk, jj * 128:(jj + 1) * 128], H[:, kk, t0:t0 + tn], start=(kk == 0), stop=(kk == KC - 1))
                            return ins
                        k.op('pe', fg, reads=['WG%d' % s, 'H'], writes=['ps%d' % bg])
                        def fu(e, W=WU[s], b=bu, jj=jj, t0=t0, tn=tn):
                            ins = None
                            for kk in range(KC):
                                ins = e.matmul(PS[b][:, 0:tn], W[:, kk, jj * 128:(jj + 1) * 128], H[:, kk, t0:t0 + tn], start=(kk == 0), stop=(kk == KC - 1))
                            return ins
                        k.op('pe', fu, reads=['WU%d' % s, 'H'], writes=['ps%d' % bu])
                        k.op('act', lambda e, b=bg, tn=tn: e.activation(SIL[0][:, 0:tn], PS[b][:, 0:tn], AF.Silu),
                             reads=['ps%d' % bg], writes=['SIL0'])
                        k.op('dve', lambda e, b=bu, a3=a3, jj=jj, t0=t0, tn=tn: e.tensor_tensor(
                            A[a3][:, jj, t0:t0 + tn], SIL[0][:, 0:tn], PS[b][:, 0:tn], ALU.mult),
                            reads=['SIL0', 'ps%d' % bu], writes=['A%d' % a3])

            def dn(p):
                s = p % 2
                if p >= 1:
                    dn_load(p)
                for oc in range(KC):
                    for (t0, tn) in TBS:
                        b = k.next_bank()
                        def fd(e, s=s, b=b, oc=oc, t0=t0, tn=tn):
                            ins = None
                            for q in range(4):
                                g_ = 2 * p + q // 2
                                ins = e.matmul(PS[b][:, 0:tn], WD[s][:, q, oc * 128:(oc + 1) * 128], A[g_ % 3][:, q % 2, t0:t0 + tn], start=(q == 0), stop=(q == 3))
                            return ins
                        k.op('pe', fd, reads=['WD%d' % s, 'A%d' % ((2 * p) % 3), 'A%d' % ((2 * p + 1) % 3)], writes=['ps%d' % b])
                        for sl in range(t0 // L, (t0 + tn) // L):
                            c0 = sl * L - t0
                            k.op('dve', lambda e, b=b, oc=oc, sl=sl, c0=c0: e.scalar_tensor_tensor(
                                X[:, oc, sl * L:(sl + 1) * L], PS[b][:, c0:c0 + L],
                                MODV[:, (3 * i + 2) * 16 + oc, sl:sl + 1], X[:, oc, sl * L:(sl + 1) * L], ALU.mult, ALU.add),
                                reads=['ps%d' % b, 'MODV', 'X'], writes=['X'])

            gu(0)
            gu(1)
            for p in range(NG // 2):
                if 2 * p + 2 < NG:
                    gu(2 * p + 2)
                dn(p)
                if 2 * p + 3 < NG:
                    gu(2 * p + 3)

        def load_w(slot, dram2d, ncols, key):
            k.dma('pool', slot[:, :, 0:ncols], dram2d.rearrange("(k p) c -> p k c", p=128), writes=[key])

        def fm_unit(W, c0, m, src, nk, t0, tn, rkeys):
            b = k.next_bank()
            def f(e, b=b):
                ins = None
                for kk in range(nk):
                    ins = e.matmul(PS[b][0:m, 0:tn], W[:, kk, c0:c0 + m], src[:, kk, t0:t0 + tn], start=(kk == 0), stop=(kk == nk - 1))
                return ins
            k.op('pe', f, reads=rkeys, writes=['ps%d' % b])
            return b

        def rstd_from(bs, m, tn, nfeat, RSQ, rkey):
            k.op('act', lambda e: e.activation(RSQ[0:m, 0:tn], PS[bs][0:m, 0:tn], AF.Sqrt, bias=EPSD[0:m, 1:2], scale=1.0 / nfeat),
                 reads=['ps%d' % bs, 'EPSD'], writes=[rkey])
            k.op('dve', lambda e: e.reciprocal(RSQ[0:m, 0:tn], RSQ[0:m, 0:tn]), reads=[rkey], writes=[rkey])

        def sumsq_unit(bs, b, m, tn, SQs, sqkey, start, stop):
            k.op('act', lambda e: e.activation(SQs[0:m, 0:tn], PS[b][0:m, 0:tn], AF.Square), reads=['ps%d' % b], writes=[sqkey])
            k.op('pe', lambda e: e.matmul(PS[bs][0:m, 0:tn], ONESB[0:m, 0:m], SQs[0:m, 0:tn], start=start, stop=stop),
                 reads=[sqkey, 'ONESB'], writes=['ps%d' % bs])

        def tok_major_out(srcs, m, tn, t0, dst, width, OSTG, okey, oname):
            for tt in range(tn // 128):
                b = k.next_bank()
                def f(e, b=b, tt=tt):
                    ins = None
                    for c, sr in enumerate(srcs):
                        ins = e.transpose(PS[b][:, c * m:(c + 1) * m], sr[0:m, tt * 128:(tt + 1) * 128], IDF[0:m, 0:m])
                    return ins
                k.op('pe', f, reads=[oname + 'CF', 'IDF'], writes=['ps%d' % b])
                s2 = tok_major_out.cnt % 2
                tok_major_out.cnt += 1
                k.op('dve', lambda e, b=b, s2=s2: e.tensor_copy(OSTG[s2][:, 0:width], PS[b][:, 0:width]),
                     reads=['ps%d' % b], writes=[okey + str(s2)])
                r0 = t0 + tt * 128
                k.dma('sp', dst[r0:r0 + 128, :], OSTG[s2][:, 0:width], reads=[okey + str(s2)], writes=[oname + 'o%d' % r0], key=oname + 'out%d' % s2)
                out_keys.append(oname + 'o%d' % r0)
        tok_major_out.cnt = 0

        def even_attention(l, QA, CKVb, KRb, KG, VG, QG, p_end):
            ei = l // 2
            k.fence()
            HF = H[:].rearrange("p k t -> p (k t)")
            ca = Carver(off=p_end)
            ch = Carver(base=HF, limit=KC * T)
            OM = ch.take([4, T], BF16)
            WO = [ch.take([4, 512], BF16) for _ in range(2)]
            COS = ch.take([1024], F32)
            SIN = ch.take([1024], F32)
            RT = ch.take([128], BF16)
            CKc = ch.take([4, 256], BF16)
            KRc = ch.take([256], BF16)
            KGc = ch.take([2, 256], BF16)
            VGc = ch.take([2, 256], BF16)
            CST = ch.take([2, 512], F32)
            WQ = ch.take([4, 192], BF16)
            WKV = ch.take([4, 256], BF16)
            QN = ca.take([T], BF16)
            QR = ca.take([T], BF16)
            QRr = ca.take([1024], BF16)
            KN = ca.take([T], BF16)
            KNc = ca.take([256], BF16)
            KRr = ca.take([1024], BF16)
            VM = ca.take([12, 128], BF16)
            PT = [ca.take([512], BF16) for _ in range(2)]
            OP = ca.take([T], BF16)
            OS = ca.take([1024], BF16)
            RSQ = ca.take([512], F32)
            SQ = ca.take([512], BF16)
            RD = ca.take([512], F32)
            TMPF = ca.take([512], F32)
            GATE = (3 * 1 + 2) * 16
            st = {'sb': 0, 'acc': 0, 'pt': 0, 'wo': 0}

            def sbank():
                st['sb'] = (st['sb'] + 1) % 4
                return st['sb']

            def accbanks():
                st['acc'] ^= 1
                return (4, 5) if st['acc'] else (6, 7)

            k.dma('sp', CST[:, :, 0:512], c_ckv.rearrange("(t p) f -> p t f", p=128), writes=['CST'])
            for t in range(2):
                b = k.next_bank()
                def f(e, b=b, t=t):
                    ins = None
                    for c in range(4):
                        ins = e.transpose(PS[b][:, c * 128:(c + 1) * 128], CST[:, t, c * 128:(c + 1) * 128], IDF[:])
                    return ins
                k.op('pe', f, reads=['CST', 'IDF'], writes=['ps%d' % b])
                k.op('dve', lambda e, b=b, t=t: e.tensor_copy(CKc[:, :, t * 128:(t + 1) * 128], PS[b][:].rearrange("p (c q) -> p c q", c=4)),
                     reads=['ps%d' % b], writes=['CKc'])
            k.dma('sp', CST[:, :, 0:64], c_kr.rearrange("(t p) f -> p t f", p=128), writes=['CST'])
            b = k.next_bank()
            def f(e, b=b):
                ins = None
                for t in range(2):
                    ins = e.transpose(PS[b][0:64, t * 128:(t + 1) * 128], CST[:, t, 0:64], IDF[:])
                return ins
            k.op('pe', f, reads=['CST', 'IDF'], writes=['ps%d' % b])
            k.op('dve', lambda e, b=b: e.tensor_copy(KRc[0:64, 0:256], PS[b][0:64, 0:256]), reads=['ps%d' % b], writes=['KRc'])
            k.dma('sp', CST[:, :, 0:256], c_k.rearrange("(t p) f -> p t f", p=128), writes=['CST'])
            b = k.next_bank()
            def f(e, b=b):
                ins = None
                for kvh in range(2):
                    for t in range(2):
                        ins = e.transpose(PS[b][:, (kvh * 2 + t) * 128:(kvh * 2 + t + 1) * 128], CST[:, t, kvh * 128:(kvh + 1) * 128], IDF[:])
                return ins
            k.op('pe', f, reads=['CST', 'IDF'], writes=['ps%d' % b])
            k.op('dve', lambda e, b=b: e.tensor_copy(KGc[:, :, :], PS[b][:].rearrange("p (c q) -> p c q", c=2)), reads=['ps%d' % b], writes=['KGc'])
            k.dma('pool', VGc[:, :, :], c_v.rearrange("(t p) f -> p t f", p=128), writes=['VGc'])

            def load_tables(m, cos_in, sin_in, rt_in):
                k.dma('sp', COS[0:m, :], cos_in, writes=['COS'])
                k.dma('sp', SIN[0:m, :], sin_in, writes=['SIN'])
                k.dma('pool', RT[0:m, 0:m], rt_in, writes=['RT'])

            def rope(dst, src, m, rkeys, wkey):
                for blk in range(2):
                    c0 = blk * 512
                    b = k.next_bank()
                    k.op('pe', lambda e, b=b, c0=c0: e.matmul(PS[b][0:m, 0:512], RT[0:m, 0:m], src[0:m, c0:c0 + 512], start=True, stop=True),
                         reads=rkeys + ['RT'], writes=['ps%d' % b])
                    k.op('dve', lambda e, b=b, c0=c0: e.tensor_tensor(TMPF[0:m, :], PS[b][0:m, 0:512], SIN[0:m, c0:c0 + 512], ALU.mult),
                         reads=['ps%d' % b, 'SIN'], writes=['TMPF'])
                    k.op('dve', lambda e, c0=c0: e.tensor_tensor(RD[0:m, :], src[0:m, c0:c0 + 512], COS[0:m, c0:c0 + 512], ALU.mult),
                         reads=rkeys + ['COS'], writes=['RD'])
                    k.op('dve', lambda e, c0=c0: e.tensor_tensor(dst[0:m, c0:c0 + 512], TMPF[0:m, :], RD[0:m, :], ALU.add),
                         reads=['TMPF', 'RD'], writes=[wkey])

            def normed(W, c0, m, src, nk, ntok_list, gcol, nfeat, dst, rkeys, wkey):
                for (t0, tn) in ntok_list:
                    b = fm_unit(W, c0, m, src, nk, t0, tn, rkeys)
                    bs = k.next_bank()
                    sumsq_unit(bs, b, m, tn, SQ, 'SQa', True, True)
                    rstd_from(bs, m, tn, nfeat, RSQ, 'RSQa')
                    k.op('dve', lambda e, b=b, t0=t0, tn=tn: e.scalar_tensor_tensor(dst[0:m, t0:t0 + tn], PS[b][0:m, 0:tn], VEC[0:m, gcol:gcol + 1], RSQ[0:m, 0:tn], ALU.mult, ALU.mult),
                         reads=['ps%d' % b, 'VEC', 'RSQa'], writes=[wkey])

            def attend(nq, q0, parts, ktiles, scale, dst, dkey):
                bo, bd = accbanks()
                nkt = len(ktiles)
                pend = None
                for kt in range(nkt):
                    bS = sbank()
                    def fS(e, bS=bS, kt=kt):
                        ins = None
                        for pi, (lfn, rhs, kd, _) in enumerate(parts):
                            ins = e.matmul(PS[bS][:, 0:nq], lfn(kt), rhs[0:kd, q0:q0 + nq], start=(pi == 0), stop=(pi == len(parts) - 1))
                        return ins
                    rk = []
                    for p_ in parts:
                        rk += p_[3]
                    k.op('pe', fS, reads=rk, writes=['ps%d' % bS])
                    pi_ = st['pt'] % 2
                    st['pt'] += 1
                    k.op('act', lambda e, bS=bS, pi_=pi_: e.activation(PT[pi_][:, 0:nq], PS[bS][:, 0:nq], AF.Exp, scale=float(scale)),
                         reads=['ps%d' % bS], writes=['PT%d' % pi_])
                    V, vkeys = ktiles[kt]
                    def fO(e, V=V, pi_=pi_, kt=kt):
                        e.matmul(PS[bo][:, 0:nq], V, PT[pi_][:, 0:nq], start=(kt == 0), stop=(kt == nkt - 1))
                        return e.matmul(PS[bd][:, 0:nq], ONESB[:], PT[pi_][:, 0:nq], start=(kt == 0), stop=(kt == nkt - 1))
                    if pend is not None:
                        k.op('pe', pend[0], reads=pend[1], writes=['ps%d' % bo, 'ps%d' % bd])
                    pend = (fO, ['PT%d' % pi_, 'ONESB'] + vkeys)
                k.op('pe', pend[0], reads=pend[1], writes=['ps%d' % bo, 'ps%d' % bd])
                k.op('dve', lambda e: e.reciprocal(RD[:, 0:nq], PS[bd][:, 0:nq]), reads=['ps%d' % bd], writes=['RD'])
                k.op('dve', lambda e: e.tensor_tensor(dst, PS[bo][:, 0:nq], RD[:, 0:nq], ALU.mult), reads=['ps%d' % bo, 'RD'], writes=[dkey])

            def combine(hh):
                k.op('dve', lambda e: e.tensor_copy(OM[:, hh, 1024:1280], OP[:, 1024:1280]), reads=['OP'], writes=['OM'])
                k.op('dve', lambda e: e.tensor_scalar_mul(OM[:, hh, 0:1024], OP[:, 0:1024], VEC[:, 14:15]), reads=['OP', 'VEC'], writes=['OM'])
                k.op('dve', lambda e: e.scalar_tensor_tensor(OM[:, hh, 0:1024], OS[:, 0:1024], VEC[:, 15:16], OM[:, hh, 0:1024], ALU.mult, ALU.add),
                     reads=['OS', 'VEC', 'OM'], writes=['OM'])

            def out_proj(w_out, r0):
                for og in range(4):
                    s = st['wo'] % 2
                    st['wo'] += 1
                    k.dma('pool', WO[s], w_out[r0:r0 + 512, og * 512:(og + 1) * 512].rearrange("(h p) c -> p h c", p=128), writes=['WO%d' % s])
                    for oc4 in range(4):
                        oc = og * 4 + oc4
                        for (t0, tn) in TBS:
                            b = k.next_bank()
                            def f(e, s=s, b=b, oc4=oc4, t0=t0, tn=tn):
                                ins = None
                                for hh in range(4):
                                    ins = e.matmul(PS[b][:, 0:tn], WO[s][:, hh, oc4 * 128:(oc4 + 1) * 128], OM[:, hh, t0:t0 + tn], start=(hh == 0), stop=(hh == 3))
                                return ins
                            k.op('pe', f, reads=['WO%d' % s, 'OM'], writes=['ps%d' % b])
                            for sl in range(t0 // L, (t0 + tn) // L):
                                c0 = sl * L - t0
                                k.op('dve', lambda e, b=b, oc=oc, sl=sl, c0=c0: e.scalar_tensor_tensor(
                                    X[:, oc, sl * L:(sl + 1) * L], PS[b][:, c0:c0 + L], MODV[:, GATE + oc, sl:sl + 1], X[:, oc, sl * L:(sl + 1) * L], ALU.mult, ALU.add),
                                    reads=['ps%d' % b, 'MODV', 'X'], writes=['X'])

            load_tables(64, cos64_in, sin64_in, rt64_in)
            rope(KRr, KRb, 64, ['KRb'], 'KRr')
            sc_m = 192.0 ** -0.5
            for h in range(8):
                k.dma('pool', WQ, mla_w_qb[ei][:, h * 192:(h + 1) * 192].rearrange("(k p) c -> p k c", p=128), writes=['WQ'])
                k.dma('pool', WKV, mla_w_kvb[ei][:, h * 256:(h + 1) * 256].rearrange("(k p) c -> p k c", p=128), writes=['WKV'])
                normed(WQ, 0, 128, QA, 4, TBS, 8, 128, QN, ['WQ', 'QA'], 'QN')
                normed(WQ, 128, 64, QA, 4, TBS, 10, 64, QR, ['WQ', 'QA'], 'QR')
                rope(QRr, QR, 64, ['QR'], 'QRr')
                normed(WKV, 0, 128, CKVb, 4, TBS, 9, 128, KN, ['WKV', 'CKVb'], 'KN')
                normed(WKV, 0, 128, CKc, 4, [(0, 256)], 9, 128, KNc, ['WKV', 'CKc'], 'KNc')
                for g3 in range(3):
                    b = k.next_bank()
                    def f(e, b=b, g3=g3):
                        ins = None
                        for q in range(4):
                            ti = g3 * 4 + q
                            for c in range(4):
                                lh = CKc[:, c, ti * 128:(ti + 1) * 128] if ti < 2 else CKVb[:, c, (ti - 2) * 128:(ti - 1) * 128]
                                ins = e.matmul(PS[b][:, q * 128:(q + 1) * 128], lh, WKV[:, c, 128:256], start=(c == 0), stop=(c == 3))
                        return ins
                    k.op('pe', f, reads=['CKc', 'CKVb', 'WKV'], writes=['ps%d' % b])
                    k.op('act', lambda e, b=b, g3=g3: e.activation(VM[:, g3 * 4:(g3 + 1) * 4, :], PS[b][:].rearrange("p (q c) -> p q c", q=4), AF.Copy),
                         reads=['ps%d' % b], writes=['VM'])
                for s_ in range(NSLOT):
                    attend(L, s_ * L,
                           [(lambda kt, s_=s_: KN[:, s_ * L + kt * 128:s_ * L + (kt + 1) * 128], QN, 128, ['KN', 'QN']),
                            (lambda kt, s_=s_: KRb[0:64, s_ * L + kt * 128:s_ * L + (kt + 1) * 128], QR, 64, ['KRb', 'QR'])],
                           [(VM[:, 2 + s_ * 2 + kt, :], ['VM']) for kt in range(2)], sc_m, OP[:, s_ * L:(s_ + 1) * L], 'OP')
                for qb in range(2):
                    attend(512, qb * 512,
                           [(lambda kt: KNc[:, kt * 128:(kt + 1) * 128] if kt < 2 else KN[:, (kt - 2) * 128:(kt - 1) * 128], QN, 128, ['KNc', 'KN', 'QN']),
                            (lambda kt: KRc[0:64, kt * 128:(kt + 1) * 128] if kt < 2 else KRr[0:64, (kt - 2) * 128:(kt - 1) * 128], QRr, 64, ['KRc', 'KRr', 'QRr'])],
                           [(VM[:, kt, :], ['VM']) for kt in range(10)], sc_m, OS[:, qb * 512:(qb + 1) * 512], 'OS')
                combine(h % 4)
                if h % 4 == 3:
                    out_proj(ev_w_out[ei], (h // 4) * 512)
            if dbg == 'e2':
                return
            k.fence()
            load_tables(128, cos128_in, sin128_in, rt128_in)
            KGr = KRr
            QGr = QRr
            sc_g = 128.0 ** -0.5
            for kvh in range(2):
                rope(KGr, KG[:, kvh, :], 128, ['KG'], 'KGr')
                for g in range(4):
                    h = kvh * 4 + g
                    rope(QGr, QG[:, h, :], 128, ['QG'], 'QGr')
                    for s_ in range(NSLOT):
                        attend(L, s_ * L,
                               [(lambda kt, s_=s_, kvh=kvh: KG[:, kvh, s_ * L + kt * 128:s_ * L + (kt + 1) * 128], QG[:, h, :], 128, ['KG', 'QG'])],
                               [(VG[:, s_ * 2 + kt, kvh * 128:(kvh + 1) * 128], ['VG']) for kt in range(2)], sc_g, OP[:, s_ * L:(s_ + 1) * L], 'OP')
                    for qb in range(2):
                        attend(512, qb * 512,
                               [(lambda kt, kvh=kvh: KGc[:, kvh, kt * 128:(kt + 1) * 128] if kt < 2 else KGr[:, (kt - 2) * 128:(kt - 1) * 128], QGr, 128, ['KGc', 'KGr', 'QGr'])],
                               [((VGc[:, kt, kvh * 128:(kvh + 1) * 128] if kt < 2 else VG[:, kt - 2, kvh * 128:(kvh + 1) * 128]), ['VGc', 'VG']) for kt in range(10)],
                               sc_g, OS[:, qb * 512:(qb + 1) * 512], 'OS')
                    combine(g)
                out_proj(ev_w_out[ei], 1024 + kvh * 512)

        def even_mixer(l):
            ei = l // 2
            w_in = ev_w_in[ei]
            k.fence()
            cv0 = Carver()
            prenorm(1, cv0)
            k.fence()
            cv = Carver()
            QA = cv.take([4, T], BF16)
            CKVb = cv.take([4, T], BF16)
            KRb = cv.take([T], BF16)
            KG = cv.take([2, T], BF16)
            VG = cv.take([10, 256], BF16)
            qg_off = cv.off
            QG = cv.take([8, T], BF16)
            p_end = cv.off
            WS = cv.take([KC, 512], BF16)
            SQ = [cv.take([512], BF16) for _ in range(2)]
            RSQ = cv.take([512], F32)
            cva = Carver(off=qg_off)
            CF = [cva.take([512], F32) for _ in range(4)]
            OSTG = [cva.take([512], F32) for _ in range(2)]
            assert cva.off <= p_end

            def joint(col0, g0, outb, f32dst, oname):
                load_w(WS, w_in[:, col0:col0 + 512], 512, 'WS')
                for (t0, tn) in TBS:
                    bs = k.next_bank()
                    bc = []
                    for c in range(4):
                        b = fm_unit(WS, c * 128, 128, H, KC, t0, tn, ['WS', 'H'])
                        bc.append(b)
                        sumsq_unit(bs, b, 128, tn, SQ[c % 2], 'SQ%d' % (c % 2), c == 0, c == 3)
                    rstd_from(bs, 128, tn, 512, RSQ, 'RSQ')
                    for c in range(4):
                        if f32dst is not None:
                            k.op('dve', lambda e, c=c, b=bc[c], tn=tn: e.scalar_tensor_tensor(CF[c][:, 0:tn], PS[b][:, 0:tn], VEC[:, g0 + c:g0 + c + 1], RSQ[:, 0:tn], ALU.mult, ALU.mult),
                                 reads=['ps%d' % bc[c], 'VEC', 'RSQ'], writes=[oname + 'CF'])
                            k.op('act', lambda e, c=c, t0=t0, tn=tn: e.activation(outb[:, c, t0:t0 + tn], CF[c][:, 0:tn], AF.Copy),
                                 reads=[oname + 'CF'], writes=[oname])
                        else:
                            k.op('dve', lambda e, c=c, b=bc[c], t0=t0, tn=tn: e.scalar_tensor_tensor(outb[:, c, t0:t0 + tn], PS[b][:, 0:tn], VEC[:, g0 + c:g0 + c + 1], RSQ[:, 0:tn], ALU.mult, ALU.mult),
                                 reads=['ps%d' % bc[c], 'VEC', 'RSQ'], writes=[oname])
                    if f32dst is not None:
                        tok_major_out(CF, 128, tn, t0, f32dst, 512, OSTG, 'OSTG', oname)

            joint(0, 0, QA, None, 'QA')
            joint(512, 4, CKVb, o_ckv, 'CKVb')
            load_w(WS, w_in[:, 1024:1088], 64, 'WS')
            for (t0, tn) in TBS:
                b = fm_unit(WS, 0, 64, H, KC, t0, tn, ['WS', 'H'])
                bs = k.next_bank()
                sumsq_unit(bs, b, 64, tn, SQ[0], 'SQ0', True, True)
                rstd_from(bs, 64, tn, 64, RSQ, 'RSQ')
                k.op('dve', lambda e, b=b, tn=tn: e.scalar_tensor_tensor(CF[0][0:64, 0:tn], PS[b][0:64, 0:tn], VEC[0:64, 11:12], RSQ[0:64, 0:tn], ALU.mult, ALU.mult),
                     reads=['ps%d' % b, 'VEC', 'RSQ'], writes=['KRbCF'])
                k.op('act', lambda e, t0=t0, tn=tn: e.activation(KRb[0:64, t0:t0 + tn], CF[0][0:64, 0:tn], AF.Copy), reads=['KRbCF'], writes=['KRb'])
                tok_major_out([CF[0]], 64, tn, t0, o_kr, 64, OSTG, 'OSTG', 'KRb')
            k.fence()
            for g in range(2):
                load_w(WS, w_in[:, 1088 + g * 512:1088 + (g + 1) * 512], 512, 'WS')
                for c in range(4):
                    for (t0, tn) in TBS:
                        b = fm_unit(WS, c * 128, 128, H, KC, t0, tn, ['WS', 'H'])
                        bs = k.next_bank()
                        sumsq_unit(bs, b, 128, tn, SQ[0], 'SQ0', True, True)
                        rstd_from(bs, 128, tn, 128, RSQ, 'RSQ')
                        k.op('dve', lambda e, b=b, h=g * 4 + c, t0=t0, tn=tn: e.scalar_tensor_tensor(QG[:, h, t0:t0 + tn], PS[b][:, 0:tn], VEC[:, 12:13], RSQ[:, 0:tn], ALU.mult, ALU.mult),
                             reads=['ps%d' % b, 'VEC', 'RSQ'], writes=['QG'])
            cvb = Carver(off=cv.off)
            CF2 = [cvb.take([512], F32) for _ in range(2)]
            OSTG2 = [cvb.take([512], F32) for _ in range(2)]
            load_w(WS, w_in[:, 2112:2624], 512, 'WS')
            for (t0, tn) in TBS:
                for c in range(2):
                    b = fm_unit(WS, c * 128, 128, H, KC, t0, tn, ['WS', 'H'])
                    bs = k.next_bank()
                    sumsq_unit(bs, b, 128, tn, SQ[0], 'SQ0', True, True)
                    rstd_from(bs, 128, tn, 128, RSQ, 'RSQ')
                    k.op('dve', lambda e, b=b, c=c, tn=tn: e.scalar_tensor_tensor(CF2[c][:, 0:tn], PS[b][:, 0:tn], VEC[:, 13:14], RSQ[:, 0:tn], ALU.mult, ALU.mult),
                         reads=['ps%d' % b, 'VEC', 'RSQ'], writes=['KGCF'])
                    k.op('act', lambda e, c=c, t0=t0, tn=tn: e.activation(KG[:, c, t0:t0 + tn], CF2[c][:, 0:tn], AF.Copy), reads=['KGCF'], writes=['KG'])
                tok_major_out(CF2, 128, tn, t0, o_k, 256, OSTG2, 'OSTGb', 'KG')
            for tile in range(T // 128):
                b = k.next_bank()
                def f(e, b=b, tile=tile):
                    ins = None
                    for kk in range(KC):
                        ins = e.matmul(PS[b][:, 0:256], H[:, kk, tile * 128:(tile + 1) * 128], WS[:, kk, 256:512], start=(kk == 0), stop=(kk == KC - 1))
                    return ins
                k.op('pe', f, reads=['WS', 'H'], writes=['ps%d' % b])
                s2 = tile % 2
                k.op('dve', lambda e, b=b, s2=s2: e.tensor_copy(OSTG2[s2][:, 0:256], PS[b][:, 0:256]), reads=['ps%d' % b], writes=['OSTGb%d' % s2])
                k.op('act', lambda e, s2=s2, tile=tile: e.activation(VG[:, tile, :], OSTG2[s2][:, 0:256], AF.Copy), reads=['OSTGb%d' % s2], writes=['VG'])
                k.dma('sp', o_v[tile * 128:(tile + 1) * 128, :], OSTG2[s2][:, 0:256], reads=['OSTGb%d' % s2], writes=['VGo%d' % tile], key='VGout%d' % s2)
                out_keys.append('VGo%d' % tile)
            if dbg == 'e1':
                return
            even_attention(l, QA, CKVb, KRb, KG, VG, QG, p_end)

        USCR = nc.dram_tensor("u_scr", [56, 128, T], BF16)

        def odd_mixer(l):
            oi = l // 2
            k.fence()
            prenorm(1, Carver())
            k.fence()
            cv = Carver()
            WSo = [cv.take([KC, 512], BF16) for _ in range(2)]
            STG = [cv.take([T], BF16) for _ in range(4)]
            kscale = 128.0 ** -0.5
            ev = [0]
            for g in range(14):
                s = g % 2
                load_w(WSo[s], od_w_in[oi][:, g * 512:(g + 1) * 512], 512, 'WSo%d' % s)
                for c in range(4):
                    chn = g * 4 + c
                    sg = STG[chn % 4]
                    sc = kscale if 8 <= chn < 16 else 1.0
                    for (t0, tn) in TBS:
                        b = fm_unit(WSo[s], c * 128, 128, H, KC, t0, tn, ['WSo%d' % s, 'H'])
                        ev[0] += 1
                        if ev[0] % 2 == 0:
                            k.op('act', lambda e, b=b, sg=sg, t0=t0, tn=tn, sc=sc: e.activation(sg[:, t0:t0 + tn], PS[b][:, 0:tn], AF.Copy, scale=float(sc)),
                                 reads=['ps%d' % b], writes=['STG%d' % (chn % 4)])
                        else:
                            k.op('dve', lambda e, b=b, sg=sg, t0=t0, tn=tn, sc=sc: e.tensor_scalar_mul(sg[:, t0:t0 + tn], PS[b][:, 0:tn], float(sc)),
                                 reads=['ps%d' % b], writes=['STG%d' % (chn % 4)])
                    k.dma('sp', USCR[chn], sg, reads=['STG%d' % (chn % 4)], writes=['U%d' % chn], key='Uw%d' % (chn % 4))
            k.fence()
            OMIX = H
            retention(oi, OMIX)
            if dbg in ('odr', 'odr_only'):
                k.op('dve', lambda e: e.memset(OMIX[:, 8:16, :], 0.0), reads=['OMIX'], writes=['OMIX'])
            else:
                hyena(oi, OMIX)
            k.fence()
            cw = Carver()
            WOo = [cw.take([KC, 256], BF16) for _ in range(2)]
            GATE = (3 * 1 + 2) * 16
            for g in range(8):
                s = g % 2
                k.dma('pool', WOo[s], od_w_out[oi][:, g * 256:(g + 1) * 256].rearrange("(k p) c -> p k c", p=128), writes=['WOo%d' % s])
                for c in range(2):
                    oc = g * 2 + c
                    for (t0, tn) in TBS:
                        b = fm_unit(WOo[s], c * 128, 128, OMIX, KC, t0, tn, ['WOo%d' % s, 'OMIX'])
                        for sl in range(t0 // L, (t0 + tn) // L):
                            c0 = sl * L - t0
                            k.op('dve', lambda e, b=b, oc=oc, sl=sl, c0=c0: e.scalar_tensor_tensor(
                                X[:, oc, sl * L:(sl + 1) * L], PS[b][:, c0:c0 + L], MODV[:, GATE + oc, sl:sl + 1], X[:, oc, sl * L:(sl + 1) * L], ALU.mult, ALU.add),
                                reads=['ps%d' % b, 'MODV', 'X'], writes=['X'])

        def hyena(oi, OMIX):
            k.fence()
            ch_ = Carver()
            DFTT = ch_.take([8, 512], BF16)
            FB = ch_.take([8], F32)
            Z2 = ch_.take([1280], F32)
            cm_ = Carver(off=ch_.off + 2048)
            FE = cm_.take([1280], F32)
            W1 = cm_.take([64], F32)
            W2 = cm_.take([64], F32)
            Z1 = cm_.take([1280], F32)
            RTMP = cm_.take([512], F32)
            VEC2 = ch_.take([128], F32)
            NEGR = ch_.take([16], F32)
            UIN = ch_.take([T], BF16)
            V = ch_.take([T], F32)
            VS = ch_.take([1024], F32)
            XX = ch_.take([T], F32)
            XXS = ch_.take([1024], F32)
            VTKp = ch_.take([10, 128], BF16)
            VTKs = ch_.take([8, 128], BF16)
            HP = ch_.take([4, 128], F32)
            HS = ch_.take([16, 128], F32)
            FTp = ch_.take([2, 128], BF16)
            FTs = ch_.take([8, 128], BF16)
            W3c = ch_.take([128], F32)
            ABSD = ch_.take([128], F32)
            WIN = [ch_.take([128], F32) for _ in range(2)]
            T1 = [ch_.take([4, 128], F32) for _ in range(2)]
            T2 = [ch_.take([4, 128], F32) for _ in range(2)]
            YCp = ch_.take([10, 128], BF16)
            YSp = ch_.take([10, 128], BF16)
            YCf = ch_.take([10, 128], F32)
            YSf = ch_.take([10, 128], F32)
            YCs = ch_.take([10, 128], BF16)
            YSs = ch_.take([10, 128], BF16)
            DT = DFTT.rearrange("p a (r n) -> p a r n", r=2)

            for a_ in range(8):
                k.dma('pool', DT[:, a_, :, :], dft_in[a_].rearrange("(r p) n -> p r n", p=128), writes=['DFTT'])
            k.dma('sp', FE[0:33, :], feat_in, writes=['FE'])
            k.dma('sp', W1[0:33, 0:64], hy_w1[oi], writes=['W1'])
            k.dma('sp', W2[0:64, 0:64], hy_w2[oi], writes=['W2'])
            k.dma('sp', FB[0:64, 0:4], fb_in, writes=['FB'])
            k.dma('sp', VEC2, vec2_in, writes=['VEC2'])
            k.dma('sp', NEGR[:, 0:10], negr_in, writes=['NEGR'])
            k.op('dve', lambda e: e.tensor_tensor(FB[0:64, 4:5], FB[0:64, 0:1], FB[0:64, 2:3], ALU.mult), reads=['FB'], writes=['FB'])
            k.op('dve', lambda e: e.tensor_tensor(FB[0:64, 5:6], FB[0:64, 1:2], FB[0:64, 3:4], ALU.mult), reads=['FB'], writes=['FB'])
            k.op('dve', lambda e: e.memset(FB[0:64, 6:7], -math.pi), reads=['FB'], writes=['FB'])
            for (src, W, kin, fcol, dst, skey, dkey) in ((FE, W1, 33, 2, Z1, 'FE', 'Z1'), (Z1, W2, 64, 3, Z2, 'Z1', 'Z2')):
                for (c0, cn) in ((0, 256), (256, 512), (768, 512)):
                    b = k.next_bank()
                    k.op('pe', lambda e, b=b, W=W, kin=kin, src=src, c0=c0, cn=cn: e.matmul(PS[b][0:64, 0:cn], W[0:kin, 0:64], src[0:kin, c0:c0 + cn], start=True, stop=True),
                         reads=[skey, 'W1', 'W2'], writes=['ps%d' % b])
                    k.op('dve', lambda e, b=b, dst=dst, c0=c0, cn=cn, fcol=fcol: e.tensor_scalar(dst[0:64, c0:c0 + cn], PS[b][0:64, 0:cn], FB[0:64, fcol:fcol + 1], FB[0:64, fcol + 2:fcol + 3], ALU.mult, ALU.add),
                         reads=['ps%d' % b, 'FB'], writes=[dkey])
                    for rep in range(2):
                        for (cmp_, thr, adj) in ((ALU.is_gt, math.pi, -2.0 * math.pi), (ALU.is_lt, -math.pi, 2.0 * math.pi)):
                            k.op('dve', lambda e, dst=dst, c0=c0, cn=cn, cmp_=cmp_, thr=thr, adj=adj: e.tensor_scalar(RTMP[0:64, 0:cn], dst[0:64, c0:c0 + cn], float(thr), float(adj), cmp_, ALU.mult),
                                 reads=[dkey], writes=['RTMP'])
                            k.op('dve', lambda e, dst=dst, c0=c0, cn=cn: e.tensor_tensor(dst[0:64, c0:c0 + cn], dst[0:64, c0:c0 + cn], RTMP[0:64, 0:cn], ALU.add),
                                 reads=[dkey, 'RTMP'], writes=[dkey])
                    k.op('act', lambda e, dst=dst, c0=c0, cn=cn: e.activation(dst[0:64, c0:c0 + cn], dst[0:64, c0:c0 + cn], AF.Sin),
                         reads=[dkey, 'FB'], writes=[dkey])
            cnt = {'w': 0, 't': 0}
            k.fence()

            def short_conv(cidx, OUTP, OUTS, pkey, skey):
                w0 = VEC2[:, cidx:cidx + 1]
                w1 = VEC2[:, 24 + cidx:25 + cidx]
                w2 = VEC2[:, 48 + cidx:49 + cidx]
                bb = VEC2[:, 72 + cidx:73 + cidx]
                k.dma('sp', UIN, USCR[32 + cidx], reads=['U%d' % (32 + cidx)], writes=['UIN'])
                k.op('dve', lambda e: e.tensor_scalar(OUTP, UIN, w1, bb, ALU.mult, ALU.add), reads=['UIN', 'VEC2'], writes=[pkey])
                for s_ in range(NSLOT):
                    a, z_ = s_ * L, (s_ + 1) * L
                    k.op('dve', lambda e, a=a, z_=z_: e.scalar_tensor_tensor(OUTP[:, a + 1:z_], UIN[:, a:z_ - 1], w0, OUTP[:, a + 1:z_], ALU.mult, ALU.add),
                         reads=['UIN', 'VEC2', pkey], writes=[pkey])
                    k.op('dve', lambda e, a=a, z_=z_: e.scalar_tensor_tensor(OUTP[:, a:z_ - 1], UIN[:, a + 1:z_], w2, OUTP[:, a:z_ - 1], ALU.mult, ALU.add),
                         reads=['UIN', 'VEC2', pkey], writes=[pkey])
                k.op('dve', lambda e: e.tensor_scalar(OUTS, UIN[:, 0:1024], w1, bb, ALU.mult, ALU.add), reads=['UIN', 'VEC2'], writes=[skey])
                k.op('dve', lambda e: e.scalar_tensor_tensor(OUTS[:, 1:1024], UIN[:, 0:1023], w0, OUTS[:, 1:1024], ALU.mult, ALU.add), reads=['UIN', 'VEC2', skey], writes=[skey])
                k.op('dve', lambda e: e.scalar_tensor_tensor(OUTS[:, 0:1023], UIN[:, 1:1024], w2, OUTS[:, 0:1023], ALU.mult, ALU.add), reads=['UIN', 'VEC2', skey], writes=[skey])

            def filters(o, c):
                chn = o * 1024 + c * 128
                k.dma('sp', W3c[0:64, :], hy_w3[oi][:, chn:chn + 128], writes=['W3c'])
                k.dma('sp', ABSD, decb_in[:, chn:chn + 128], writes=['ABSD'])
                k.op('act', lambda e: e.activation(ABSD, ABSD, AF.Abs), reads=['ABSD'], writes=['ABSD'])
                for tile in range(10):
                    zc0 = tile * 128
                    if tile < 2:
                        dst = FTp[:, tile, :]
                        dkey = 'FTp'
                    else:
                        t_idx = tile - 2
                        dst = FTs[:, (t_idx % 2) * 4 + t_idx // 2, :]
                        dkey = 'FTs'
                    b = k.next_bank()
                    k.op('pe', lambda e, b=b, zc0=zc0: e.matmul(PS[b][:, 0:128], Z2[0:64, zc0:zc0 + 128], W3c[0:64, :], start=True, stop=True),
                         reads=['Z2', 'W3c'], writes=['ps%d' % b])
                    wi = cnt['w'] % 2
                    cnt['w'] += 1
                    k.op('act', lambda e, wi=wi, tile=tile: e.activation(WIN[wi], ABSD, AF.Exp, scale=NEGR[:, tile:tile + 1]), reads=['ABSD', 'NEGR'], writes=['WIN%d' % wi])
                    k.op('dve', lambda e, b=b, wi=wi, dst=dst: e.tensor_tensor(dst, PS[b][:, 0:128], WIN[wi], ALU.mult), reads=['ps%d' % b, 'WIN%d' % wi], writes=[dkey])
                b = k.next_bank()
                def f(e, b=b):
                    ins = None
                    for cs in range(2):
                        for kc in range(2):
                            o_ = PS[b][:, (cs * 2 + kc) * 128:(cs * 2 + kc + 1) * 128]
                            for tt in range(2):
                                ins = e.matmul(o_, DT[:, cs, tt, kc * 128:(kc + 1) * 128], FTp[:, tt, :], start=(tt == 0), stop=(tt == 1))
                    return ins
                k.op('pe', f, reads=['DFTT', 'FTp'], writes=['ps%d' % b])
                k.op('act', lambda e, b=b: e.activation(HP, PS[b][:].rearrange("p (a c) -> p a c", a=4), AF.Copy), reads=['ps%d' % b], writes=['HP'])
                for cs in range(2):
                    for kc in range(2):
                        b = k.next_bank()
                        def f(e, b=b, cs=cs, kc=kc):
                            ins = None
                            for tt in range(2):
                                ins = e.matmul(PS[b][:, 0:512], DT[:, cs, tt, kc * 128:(kc + 1) * 128], FTs[:, tt * 4:(tt + 1) * 4, :], start=(tt == 0), stop=(tt == 1))
                            return ins
                        k.op('pe', f, reads=['DFTT', 'FTs'], writes=['ps%d' % b])
                        k.op('act', lambda e, b=b, cs=cs, kc=kc: e.activation(HS[:, (cs * 2 + kc) * 4:(cs * 2 + kc + 1) * 4, :], PS[b][:].rearrange("p (a c) -> p a c", a=4), AF.Copy),
                             reads=['ps%d' % b], writes=['HS'])

            def to_tokmajor(SRC, nblk, DST, skey, dkey):
                blocks = [(tt, bl) for tt in range(2) for bl in range(nblk)]
                for g0 in range(0, len(blocks), 4):
                    grp = blocks[g0:g0 + 4]
                    b = k.next_bank()
                    def f(e, b=b, grp=grp):
                        ins = None
                        for q, (tt, bl) in enumerate(grp):
                            ins = e.transpose(PS[b][:, q * 128:(q + 1) * 128], SRC[:, bl * 256 + tt * 128:bl * 256 + (tt + 1) * 128], IDF[:])
                        return ins
                    k.op('pe', f, reads=[skey, 'IDF'], writes=['ps%d' % b])
                    n = len(grp)
                    k.op('act', lambda e, b=b, g0=g0, n=n: e.activation(DST[:, g0:g0 + n, :], PS[b][:, 0:n * 128].rearrange("p (q c) -> p q c", q=n), AF.Copy),
                         reads=['ps%d' % b], writes=[dkey])

            def tmp():
                i = cnt['t'] % 2
                cnt['t'] += 1
                return i

            def cmul(Ap, Bp, Ah, Bh, n, psk, hkey, outC, outS, ckey, accumulate):
                AhB = Ah[:, None, :].broadcast_to([128, n, 128])
                BhB = Bh[:, None, :].broadcast_to([128, n, 128])
                i = tmp()
                t1, t2 = T1[i][:, 0:n, :], T2[i][:, 0:n, :]
                k.op('dve', lambda e: e.tensor_tensor(t1, Ap, AhB, ALU.mult), reads=psk + [hkey], writes=['T1%d' % i])
                k.op('dve', lambda e: e.tensor_tensor(t2, Bp, BhB, ALU.mult), reads=psk + [hkey], writes=['T2%d' % i])
                if accumulate:
                    k.op('dve', lambda e: e.tensor_tensor(t1, t1, t2, ALU.subtract), reads=['T1%d' % i, 'T2%d' % i], writes=['T1%d' % i])
                    k.op('dve', lambda e: e.tensor_tensor(outC, outC, t1, ALU.add), reads=['T1%d' % i, ckey], writes=[ckey])
                else:
                    k.op('dve', lambda e: e.tensor_tensor(outC, t1, t2, ALU.subtract), reads=['T1%d' % i, 'T2%d' % i], writes=[ckey])
                i2 = tmp()
                u1, u2 = T1[i2][:, 0:n, :], T2[i2][:, 0:n, :]
                k.op('dve', lambda e: e.tensor_tensor(u1, Ap, BhB, ALU.mult), reads=psk + [hkey], writes=['T1%d' % i2])
                k.op('dve', lambda e: e.tensor_tensor(u2, Bp, AhB, ALU.mult), reads=psk + [hkey], writes=['T2%d' % i2])
                if accumulate:
                    k.op('dve', lambda e: e.tensor_tensor(u1, u1, u2, ALU.add), reads=['T1%d' % i2, 'T2%d' % i2], writes=['T1%d' % i2])
                    k.op('dve', lambda e: e.tensor_tensor(outS, outS, u1, ALU.add), reads=['T1%d' % i2, ckey], writes=[ckey])
                else:
                    k.op('dve', lambda e: e.tensor_tensor(outS, u1, u2, ALU.add), reads=['T1%d' % i2, 'T2%d' % i2], writes=[ckey])

            def long_conv(skipcol):
                sk = VEC2[:, skipcol:skipcol + 1]
                to_tokmajor(V, NSLOT, VTKp, 'V', 'VTKp')
                to_tokmajor(VS, 4, VTKs, 'VS', 'VTKs')
                for kc in range(2):
                    banks = []
                    for cs in range(2):
                        b1 = k.next_bank()
                        b2 = k.next_bank()
                        def f(e, b1=b1, b2=b2, cs=cs, kc=kc):
                            ins = None
                            for tt in range(2):
                                e.matmul(PS[b1][:, 0:512], DT[:, cs, tt, kc * 128:(kc + 1) * 128], VTKp[:, tt * 5:tt * 5 + 4, :], start=(tt == 0), stop=(tt == 1))
                            for tt in range(2):
                                ins = e.matmul(PS[b2][:, 0:128], DT[:, cs, tt, kc * 128:(kc + 1) * 128], VTKp[:, tt * 5 + 4, :], start=(tt == 0), stop=(tt == 1))
                            return ins
                        k.op('pe', f, reads=['DFTT', 'VTKp'], writes=['ps%d' % b1, 'ps%d' % b2])
                        banks.append((b1, b2))
                    psk = ['ps%d' % banks[0][0], 'ps%d' % banks[0][1], 'ps%d' % banks[1][0], 'ps%d' % banks[1][1]]
                    v4 = lambda bq: PS[bq][:, 0:512].rearrange("p (a c) -> p a c", c=128)
                    v1 = lambda bq: PS[bq][:, 0:128].rearrange("p (a c) -> p a c", c=128)
                    cmul(v4(banks[0][0]), v4(banks[1][0]), HP[:, kc, :], HP[:, 2 + kc, :], 4, psk, 'HP', YCp[:, kc * 5:kc * 5 + 4, :], YSp[:, kc * 5:kc * 5 + 4, :], 'YP', False)
                    cmul(v1(banks[0][1]), v1(banks[1][1]), HP[:, kc, :], HP[:, 2 + kc, :], 1, psk, 'HP', YCp[:, kc * 5 + 4:kc * 5 + 5, :], YSp[:, kc * 5 + 4:kc * 5 + 5, :], 'YP', False)
                for s_ in range(NSLOT):
                    b = k.next_bank()
                    def f(e, b=b, s_=s_):
                        ins = None
                        for kc in range(2):
                            e.matmul(PS[b][:, 0:256], YCp[:, kc * 5 + s_, :], DT[:, 2, kc, :], start=(kc == 0), stop=False)
                            ins = e.matmul(PS[b][:, 0:256], YSp[:, kc * 5 + s_, :], DT[:, 3, kc, :], start=False, stop=(kc == 1))
                        return ins
                    k.op('pe', f, reads=['YP', 'DFTT'], writes=['ps%d' % b])
                    k.op('dve', lambda e, b=b, s_=s_: e.scalar_tensor_tensor(V[:, s_ * L:(s_ + 1) * L], V[:, s_ * L:(s_ + 1) * L], sk, PS[b][:, 0:256], ALU.mult, ALU.add),
                         reads=['ps%d' % b, 'V', 'VEC2'], writes=['V'])
                for kc in range(2):
                    bA = k.next_bank()
                    bB = k.next_bank()
                    def f(e, bA=bA, bB=bB, kc=kc):
                        ins = None
                        for cs, bq in ((0, bA), (1, bB)):
                            for tt in range(2):
                                ins = e.matmul(PS[bq][:, 0:512], DT[:, cs, tt, kc * 128:(kc + 1) * 128], VTKs[:, tt * 4:(tt + 1) * 4, :], start=(tt == 0), stop=(tt == 1))
                        return ins
                    k.op('pe', f, reads=['DFTT', 'VTKs'], writes=['ps%d' % bA, 'ps%d' % bB])
                    if kc == 0:
                        k.op('dve', lambda e: e.memset(YCf, 0.0), reads=['YF'], writes=['YF'])
                        k.op('dve', lambda e: e.memset(YSf, 0.0), reads=['YF'], writes=['YF'])
                    for j_ in range(4):
                        i_lo, i_hi = max(0, 1 - j_), min(3, 5 - j_)
                        n_ = i_hi - i_lo + 1
                        o0 = kc * 5 + (i_lo + j_) - 1
                        va = lambda bq: PS[bq][:, i_lo * 128:(i_hi + 1) * 128].rearrange("p (a c) -> p a c", c=128)
                        cmul(va(bA), va(bB), HS[:, kc * 4 + j_, :], HS[:, (2 + kc) * 4 + j_, :], n_, ['ps%d' % bA, 'ps%d' % bB], 'HS',
                             YCf[:, o0:o0 + n_, :], YSf[:, o0:o0 + n_, :], 'YF', True)
                    k.op('act', lambda e, kc=kc: e.activation(YCs[:, kc * 5:kc * 5 + 5, :], YCf[:, kc * 5:kc * 5 + 5, :], AF.Copy), reads=['YF'], writes=['YS_'])
                    k.op('act', lambda e, kc=kc: e.activation(YSs[:, kc * 5:kc * 5 + 5, :], YSf[:, kc * 5:kc * 5 + 5, :], AF.Copy), reads=['YF'], writes=['YS_'])
                for r in range(4):
                    b = k.next_bank()
                    def f(e, b=b, r=r):
                        ins = None
                        first = True
                        for kc in range(2):
                            for (Y, s_, tab) in ((YCs, r + 2, 4), (YSs, r + 2, 5), (YCs, r + 1, 6), (YSs, r + 1, 7)):
                                ins = e.matmul(PS[b][:, 0:256], Y[:, kc * 5 + s_ - 1, :], DT[:, tab, kc, :], start=first, stop=(kc == 1 and tab == 7))
                                first = False
                        return ins
                    k.op('pe', f, reads=['YS_', 'DFTT'], writes=['ps%d' % b])
                    k.op('dve', lambda e, b=b, r=r: e.scalar_tensor_tensor(VS[:, r * L:(r + 1) * L], VS[:, r * L:(r + 1) * L], sk, PS[b][:, 0:256], ALU.mult, ALU.add),
                         reads=['ps%d' % b, 'VS', 'VEC2'], writes=['VS'])

            for c in range(8):
                short_conv(c, V, VS, 'V', 'VS')
                short_conv(8 + c, XX, XXS, 'XX', 'XXS')
                filters(0, c)
                long_conv(96 + c)
                k.op('dve', lambda e: e.tensor_tensor(V, V, XX, ALU.mult), reads=['V', 'XX'], writes=['V'])
                k.op('dve', lambda e: e.tensor_tensor(VS, VS, XXS, ALU.mult), reads=['VS', 'XXS'], writes=['VS'])
                short_conv(16 + c, XX, XXS, 'XX', 'XXS')
                filters(1, c)
                long_conv(104 + c)
                k.op('dve', lambda e: e.tensor_tensor(V, V, XX, ALU.mult), reads=['V', 'XX'], writes=['V'])
                k.op('dve', lambda e: e.tensor_tensor(VS, VS, XXS, ALU.mult), reads=['VS', 'XXS'], writes=['VS'])
                k.op('dve', lambda e: e.tensor_scalar_mul(V[:, 0:1024], V[:, 0:1024], VEC[:, 14:15]), reads=['V', 'VEC'], writes=['V'])
                k.op('dve', lambda e: e.scalar_tensor_tensor(V[:, 0:1024], VS, VEC[:, 15:16], V[:, 0:1024], ALU.mult, ALU.add), reads=['V', 'VS', 'VEC'], writes=['V'])
                k.op('act', lambda e, c=c: e.activation(OMIX[:, 8 + c, :], V, AF.Copy), reads=['V'], writes=['OMIX'])

        def retention(oi, OMIX):
            cr = Carver()
            RC = cr.take([6, 128], F32)
            PC = cr.take([4], F32)
            LG = cr.take([16], F32)
            k.dma('sp', RC, rc_in.rearrange("a p n -> p a n"), writes=['RC'])
            k.dma('sp', PC, pc_in, writes=['PC'])
            k.dma('sp', LG, lgt_in, writes=['LG'])
            k.op('act', lambda e: e.activation(LG, LG, AF.Exp, scale=-1.0), reads=['LG'], writes=['LG'])
            k.op('dve', lambda e: e.tensor_scalar_add(LG, LG, 1.0), reads=['LG'], writes=['LG'])
            k.op('act', lambda e: e.activation(LG, LG, AF.Ln), reads=['LG'], writes=['LG'])
            k.op('dve', lambda e: e.tensor_scalar_mul(LG, LG, -1.0), reads=['LG'], writes=['LG'])
            QT = cr.take([T], BF16)
            KT = cr.take([T], BF16)
            GT = cr.take([T], BF16)
            VT = cr.take([T], BF16)
            KTOK = cr.take([10, 128], BF16)
            VTOK = cr.take([10, 128], BF16)
            Mh = cr.take([128], F32)
            E2 = cr.take([128], F32)
            QDF = cr.take([128], F32)
            QDB = cr.take([128], F32)
            KD = cr.take([4], F32)
            PST = cr.take([10, 256], F32)
            PF0b = cr.take([5, 128], BF16)
            PB1b = cr.take([5, 128], BF16)
            SOUT = [cr.take([128], F32) for _ in range(2)]
            SFb = cr.take([8, 128], BF16)
            SBb = cr.take([8, 128], BF16)
            SF = cr.take([128], F32)
            SB = cr.take([128], F32)
            KFB = [cr.take([2, 128], BF16) for _ in range(2)]
            AM = [cr.take([128], BF16) for _ in range(2)]
            QFB = [cr.take([2, 128], BF16) for _ in range(2)]
            OR = cr.take([T], F32)
            ORS = cr.take([1024], F32)
            SQ = cr.take([512], BF16)
            RSQ = cr.take([512], F32)
            SG = cr.take([T], F32)
            for h in range(8):
                for (tl, chn, key) in ((QT, h, 'QT'), (KT, 8 + h, 'KT'), (VT, 16 + h, 'VT'), (GT, 24 + h, 'GT')):
                    k.dma('sp', tl, USCR[chn], reads=['U%d' % chn], writes=[key])
                for (src, dst, skey, dkey) in ((KT, KTOK, 'KT', 'KTOK'), (VT, VTOK, 'VT', 'VTOK')):
                    for g3 in range(3):
                        nt = 4 if g3 < 2 else 2
                        b = k.next_bank()
                        pb = PS[b][:].bitcast(BF16)
                        def f(e, pb=pb, src=src, g3=g3, nt=nt):
                            ins = None
                            for q in range(nt):
                                ti = g3 * 4 + q
                                ins = e.transpose(pb[:, q * 128:(q + 1) * 128], src[:, ti * 128:(ti + 1) * 128], IDB[:])
                            return ins
                        k.op('pe', f, reads=[skey, 'IDB'], writes=['ps%d' % b])
                        k.op('dve', lambda e, pb=pb, dst=dst, g3=g3, nt=nt: e.tensor_copy(dst[:, g3 * 4:g3 * 4 + nt, :], pb[:, 0:nt * 128].rearrange("p (q c) -> p q c", q=nt)),
                             reads=['ps%d' % b], writes=[dkey])
                lgf = LG[:, h:h + 1]
                lgb = LG[:, 8 + h:9 + h]
                k.op('act', lambda e, lgf=lgf: e.activation(Mh, RC[:, 0, :], AF.Exp, scale=lgf), reads=['RC', 'LG'], writes=['Mh'])
                k.op('dve', lambda e: e.tensor_tensor(Mh, Mh, RC[:, 2, :], ALU.mult), reads=['Mh', 'RC'], writes=['Mh'])
                k.op('act', lambda e, lgb=lgb: e.activation(E2, RC[:, 1, :], AF.Exp, scale=lgb), reads=['RC', 'LG'], writes=['E2'])
                k.op('dve', lambda e: e.tensor_tensor(E2, E2, RC[:, 3, :], ALU.mult), reads=['E2', 'RC'], writes=['E2'])
                k.op('dve', lambda e: e.tensor_tensor(Mh, Mh, E2, ALU.add), reads=['E2', 'Mh'], writes=['Mh'])
                k.op('act', lambda e, lgf=lgf: e.activation(QDF, RC[:, 4, :], AF.Exp, scale=lgf), reads=['RC', 'LG'], writes=['QDF'])
                k.op('act', lambda e, lgb=lgb: e.activation(QDB, RC[:, 5, :], AF.Exp, scale=lgb), reads=['RC', 'LG'], writes=['QDB'])
                k.op('act', lambda e, lgf=lgf: e.activation(KD[:, 0:1], PC[:, 0:1], AF.Exp, scale=lgf), reads=['PC', 'LG'], writes=['KD'])
                k.op('act', lambda e, lgb=lgb: e.activation(KD[:, 1:2], PC[:, 1:2], AF.Exp, scale=lgb), reads=['PC', 'LG'], writes=['KD'])
                k.op('act', lambda e, lgf=lgf: e.activation(KD[:, 2:3], PC[:, 2:3], AF.Exp, scale=lgf), reads=['PC', 'LG'], writes=['KD'])
                k.op('act', lambda e, lgb=lgb: e.activation(KD[:, 3:4], PC[:, 2:3], AF.Exp, scale=lgb), reads=['PC', 'LG'], writes=['KD'])
                for ci in range(10):
                    s2 = ci % 2
                    k.op('dve', lambda e, ci=ci, s2=s2: e.tensor_scalar_mul(KFB[s2][:, 0, :], KTOK[:, ci, :], KD[:, 0:1]), reads=['KTOK', 'KD'], writes=['KFB%d' % s2])
                    k.op('dve', lambda e, ci=ci, s2=s2: e.tensor_scalar_mul(KFB[s2][:, 1, :], KTOK[:, ci, :], KD[:, 1:2]), reads=['KTOK', 'KD'], writes=['KFB%d' % s2])
                    b = k.next_bank()
                    def f(e, b=b, ci=ci, s2=s2):
                        e.matmul(PS[b][:, 0:128], KFB[s2][:, 0, :], VTOK[:, ci, :], start=True, stop=True)
                        return e.matmul(PS[b][:, 128:256], KFB[s2][:, 1, :], VTOK[:, ci, :], start=True, stop=True)
                    k.op('pe', f, reads=['KFB%d' % s2, 'VTOK'], writes=['ps%d' % b])
                    k.op('act', lambda e, b=b, ci=ci: e.activation(PST[:, ci, :], PS[b][:, 0:256], AF.Copy), reads=['ps%d' % b], writes=['PST'])
                for s_ in range(NSLOT):
                    c0_, c1_ = 2 * s_, 2 * s_ + 1
                    k.op('act', lambda e, s_=s_, c0_=c0_: e.activation(PF0b[:, s_, :], PST[:, c0_, 0:128], AF.Copy), reads=['PST'], writes=['PF0b'])
                    k.op('act', lambda e, s_=s_, c1_=c1_: e.activation(PB1b[:, s_, :], PST[:, c1_, 128:256], AF.Copy), reads=['PST'], writes=['PB1b'])
                    k.op('dve', lambda e, c0_=c0_, c1_=c1_: e.scalar_tensor_tensor(SOUT[0], PST[:, c0_, 0:128], KD[:, 2:3], PST[:, c1_, 0:128], ALU.mult, ALU.add),
                         reads=['PST', 'KD'], writes=['SOUT0'])
                    k.dma('sp', o_ret[s_, 0, h], SOUT[0], reads=['SOUT0'], writes=['ret%d_0_%d' % (s_, h)], key='retout0')
                    k.op('dve', lambda e, c0_=c0_, c1_=c1_: e.scalar_tensor_tensor(SOUT[1], PST[:, c1_, 128:256], KD[:, 3:4], PST[:, c0_, 128:256], ALU.mult, ALU.add),
                         reads=['PST', 'KD'], writes=['SOUT1'])
                    k.dma('sp', o_ret[s_, 1, h], SOUT[1], reads=['SOUT1'], writes=['ret%d_1_%d' % (s_, h)], key='retout1')
                    out_keys.append('ret%d_0_%d' % (s_, h))
                    out_keys.append('ret%d_1_%d' % (s_, h))
                k.dma('sp', SF, s0_in[0, h], writes=['SF'])
                k.dma('sp', SB, s0_in[1, h], writes=['SB'])
                for j in range(8):
                    k.op('act', lambda e, j=j: e.activation(SFb[:, j, :], SF, AF.Copy), reads=['SF'], writes=['SFb'])
                    k.op('dve', lambda e, j=j: e.scalar_tensor_tensor(SF, SF, KD[:, 2:3], PST[:, j, 0:128], ALU.mult, ALU.add), reads=['SF', 'KD', 'PST'], writes=['SF'])
                for j in range(7, -1, -1):
                    k.op('act', lambda e, j=j: e.activation(SBb[:, j, :], SB, AF.Copy), reads=['SB'], writes=['SBb'])
                    k.op('dve', lambda e, j=j: e.scalar_tensor_tensor(SB, SB, KD[:, 3:4], PST[:, j, 128:256], ALU.mult, ALU.add), reads=['SB', 'KD', 'PST'], writes=['SB'])
                pendD = None
                for ci in range(10):
                    s2 = ci % 2
                    cs_ = slice(ci * 128, (ci + 1) * 128)
                    sl_, jj = ci // 2, ci % 2
                    bA = k.next_bank()
                    k.op('pe', lambda e, bA=bA, cs_=cs_: e.matmul(PS[bA][:, 0:128], KT[:, cs_], QT[:, cs_], start=True, stop=True), reads=['KT', 'QT'], writes=['ps%d' % bA])
                    k.op('dve', lambda e, bA=bA, s2=s2: e.tensor_tensor(AM[s2], PS[bA][:, 0:128], Mh, ALU.mult), reads=['ps%d' % bA, 'Mh'], writes=['AM%d' % s2])
                    k.op('dve', lambda e, s2=s2, cs_=cs_: e.tensor_tensor(QFB[s2][:, 0, :], QT[:, cs_], QDF, ALU.mult), reads=['QT', 'QDF'], writes=['QFB%d' % s2])
                    k.op('dve', lambda e, s2=s2, cs_=cs_: e.tensor_tensor(QFB[s2][:, 1, :], QT[:, cs_], QDB, ALU.mult), reads=['QT', 'QDB'], writes=['QFB%d' % s2])
                    if pendD is not None:
                        pendD()
                    def emitO(ci=ci, s2=s2, sl_=sl_, jj=jj, cs_=cs_):
                      bO = k.next_bank()
                      def f(e, bO=bO, ci=ci, s2=s2, sl_=sl_, jj=jj):
                          e.matmul(PS[bO][:, 0:128], VTOK[:, ci, :], AM[s2], start=True, stop=False)
                          if jj == 1:
                              ins = e.matmul(PS[bO][:, 0:128], PF0b[:, sl_, :], QFB[s2][:, 0, :], start=False, stop=True)
                          else:
                              ins = e.matmul(PS[bO][:, 0:128], PB1b[:, sl_, :], QFB[s2][:, 1, :], start=False, stop=True)
                          if ci < 8:
                              e.matmul(PS[bO][:, 128:256], VTOK[:, ci, :], AM[s2], start=True, stop=False)
                              e.matmul(PS[bO][:, 128:256], SFb[:, ci, :], QFB[s2][:, 0, :], start=False, stop=False)
                              ins = e.matmul(PS[bO][:, 128:256], SBb[:, ci, :], QFB[s2][:, 1, :], start=False, stop=True)
                          return ins
                      k.op('pe', f, reads=['VTOK', 'AM%d' % s2, 'QFB%d' % s2, 'PF0b', 'PB1b', 'SFb', 'SBb'], writes=['ps%d' % bO])
                      k.op('act', lambda e, bO=bO, cs_=cs_: e.activation(OR[:, cs_], PS[bO][:, 0:128], AF.Copy), reads=['ps%d' % bO], writes=['OR'])
                      if ci < 8:
                          k.op('act', lambda e, bO=bO, cs_=cs_: e.activation(ORS[:, cs_], PS[bO][:, 128:256], AF.Copy), reads=['ps%d' % bO], writes=['ORS'])
                    pendD = emitO
                pendD()
                k.op('dve', lambda e: e.tensor_scalar_mul(OR[:, 0:1024], OR[:, 0:1024], VEC[:, 14:15]), reads=['OR', 'VEC'], writes=['OR'])
                k.op('dve', lambda e: e.scalar_tensor_tensor(OR[:, 0:1024], ORS[:, 0:1024], VEC[:, 15:16], OR[:, 0:1024], ALU.mult, ALU.add), reads=['ORS', 'OR', 'VEC'], writes=['OR'])
                k.op('act', lambda e: e.activation(SG, GT, AF.Silu), reads=['GT'], writes=['SG'])
                for (t0, tn) in TBS:
                    bs = k.next_bank()
                    k.op('act', lambda e, t0=t0, tn=tn: e.activation(SQ[:, 0:tn], OR[:, t0:t0 + tn], AF.Square), reads=['OR'], writes=['SQr'])
                    k.op('pe', lambda e, bs=bs, tn=tn: e.matmul(PS[bs][:, 0:tn], ONESB[:], SQ[:, 0:tn], start=True, stop=True), reads=['SQr', 'ONESB'], writes=['ps%d' % bs])
                    rstd_from(bs, 128, tn, 128, RSQ, 'RSQr')
                    k.op('dve', lambda e, t0=t0, tn=tn, h=h: e.scalar_tensor_tensor(OR[:, t0:t0 + tn], OR[:, t0:t0 + tn], VEC[:, 16 + h:17 + h], RSQ[:, 0:tn], ALU.mult, ALU.mult),
                         reads=['OR', 'VEC', 'RSQr'], writes=['OR'])
                    k.op('dve', lambda e, t0=t0, tn=tn, h=h: e.tensor_tensor(OMIX[:, h, t0:t0 + tn], OR[:, t0:t0 + tn], SG[:, t0:t0 + tn], ALU.mult),
                         reads=['OR', 'SG'], writes=['OMIX'])

        nl = 2
        for l in ([1] if dbg in ('odr_only', 'od_only') else range(nl)):
            if dbg == 'load':
                break
            mod_phase(l)
            if dbg == 'mod':
                for sl in range(NSLOT):
                    k.op('dve', lambda e, sl=sl: e.tensor_copy(X[:, 0:9, sl * L:sl * L + 16], MODV[:, :, sl].rearrange("p (a b) -> p a b", a=9)),
                         reads=['MODV', 'X'], writes=['X'])
                break
            if dbg == 'pre':
                cvq = Carver()
                prenorm(0, cvq)
                k.op('dve', lambda e: e.tensor_copy(X[:], H[:]), reads=['H', 'X'], writes=['X'])
                break
            ffn(l, 0, 0)
            if dbg == 'ffn0':
                break
            if l % 2 == 0:
                even_mixer(l)
            else:
                odd_mixer(l)
            if dbg in ('odr_only', 'od_only', 'odr', 'od'):
                break
            if dbg in ('e1', 'e2', 'ev'):
                break
            ffn(l, 1, 2)

        k.fence()
        cvo = Carver()
        YT = [cvo.take([D], F32) for _ in range(2)]
        for tt in range(T // 128):
            st = YT[tt % 2]
            for k4 in range(KC // 4):
                b = k.next_bank()
                def f(e, tt=tt, k4=k4, b=b):
                    ins = None
                    for q in range(4):
                        kk = k4 * 4 + q
                        ins = e.transpose(PS[b][:, q * 128:(q + 1) * 128], X[:, kk, tt * 128:(tt + 1) * 128], IDF[:])
                    return ins
                k.op('pe', f, reads=['X', 'IDF'], writes=['ps%d' % b])
                if k4 % 2 == 0:
                    k.op('dve', lambda e, b=b, k4=k4, st=st: e.tensor_copy(st[:, k4 * 512:(k4 + 1) * 512], PS[b][:]),
                         reads=['ps%d' % b], writes=['YT%d' % (tt % 2)])
                else:
                    k.op('act', lambda e, b=b, k4=k4, st=st: e.activation(st[:, k4 * 512:(k4 + 1) * 512], PS[b][:], AF.Copy),
                         reads=['ps%d' % b], writes=['YT%d' % (tt % 2)])
            k.dma('sp', y[tt * 128:(tt + 1) * 128, :], st, reads=['YT%d' % (tt % 2)], writes=['y%d' % tt], key='yout%d' % (tt % 2))
        k.final_wait('sp', ['y%d' % tt for tt in range(T // 128)] + out_keys)

        with nc.Block() as block:
            @block.tensor
            def _(e):
                for f in k.prog['pe']:
                    f(e)

            @block.scalar
            def _(e):
                for f in k.prog['act']:
                    f(e)

            @block.vector
            def _(e):
                for f in k.prog['dve']:
                    f(e)

            @block.gpsimd
            def _(e):
                for f in k.prog['pool']:
                    f(e)

            @block.sync
            def _(e):
                for f in k.prog['sp']:
                    f(e)
    return nc


def _slot_assign():
    out = []
    for c in range(NCORES):
        if c < 2:
            out.append([('s', c, q) for q in range(4)] + [('p', c)])
        else:
            out.append([('p', 2 + 5 * (c - 2) + s) for s in range(5)])
    return out


def _rope_tables(dim):
    f32 = np.float32
    n_rows = 1024 // 64
    row = np.repeat(np.arange(n_rows, dtype=f32), 64)
    col = np.tile(np.arange(64, dtype=f32), n_rows)
    half = dim // 2
    freq = (f32(10000.0) ** (-np.arange(0, half, 2, dtype=f32) / f32(half))).astype(f32)
    ang = np.concatenate([row[:, None] * freq[None], col[:, None] * freq[None]], axis=-1).astype(f32)
    cos = np.cos(ang).astype(f32)
    sin = np.sin(ang).astype(f32)
    cosT = np.ascontiguousarray(np.concatenate([cos, cos], axis=1).T)
    sinT = np.ascontiguousarray(np.concatenate([sin, sin], axis=1).T)
    rt = np.zeros((dim, dim), f32)
    for m_ in range(dim):
        if m_ < half:
            rt[m_ + half, m_] = -1.0
        else:
            rt[m_ - half, m_] = 1.0
    return cosT, sinT, rt


def _const_tables():
    c64, s64, rt64 = _rope_tables(64)
    c128, s128, rt128 = _rope_tables(128)
    f32 = np.float32
    ii = np.arange(128, dtype=f32)
    dm = ii[None, :] - ii[:, None]
    rc = np.stack([np.maximum(dm, 0), np.maximum(-dm, 0), (dm >= 0).astype(f32), (dm <= 0).astype(f32),
                   np.broadcast_to(ii[None, :] + 1, (128, 128)), np.broadcast_to(128 - ii[None, :], (128, 128))]).astype(f32)
    pc = np.stack([127 - ii, ii, np.full(128, 128.0, f32), np.zeros(128, f32)], axis=1).astype(f32)
    tt_ = np.arange(256, dtype=np.float64)
    om = 2.0 * np.pi * (np.arange(256, dtype=np.float64) + 0.5) / 512.0
    fc = np.cos(np.outer(tt_, om)); fs = np.sin(np.outer(tt_, om))
    def g(off):
        ph = np.outer(om, tt_ + off)
        return (2.0 / 512.0) * np.cos(ph), (2.0 / 512.0) * np.sin(ph)
    gcm, gsm = g(128); gcl, gsl = g(0); gch, gsh = g(256)
    dft = np.stack([fc, fs, gcm, gsm, gcl, gsl, gch, gsh]).astype(f32)
    def feat(Lq):
        t = np.arange(Lq, dtype=f32); tn = (t / f32(Lq)).astype(f32)
        bands = np.arange(1, 17, dtype=f32)
        ang = (f32(2.0 * math.pi) * tn[:, None] * bands[None, :]).astype(f32)
        return np.concatenate([tn[:, None], np.sin(ang), np.cos(ang)], axis=-1).astype(f32).T
    featT = np.ascontiguousarray(np.concatenate([feat(256), feat(1024)], axis=1))
    def negr(Lq):
        t = np.arange(Lq, dtype=f32)
        return (-(np.abs(t - Lq // 2) / f32(Lq / 2))).astype(f32).reshape(Lq // 128, 128).T
    negr_t = np.ascontiguousarray(np.concatenate([negr(256), negr(1024)], axis=1))
    return {'dft': dft, 'feat': featT, 'negr': negr_t, 'rc': np.ascontiguousarray(rc), 'pc': np.ascontiguousarray(pc), 'identb': np.eye(128, dtype=f32), 'cos64': c64, 'sin64': s64, 'rt64': rt64, 'cos128': c128, 'sin128': s128, 'rt128': rt128}


def make_in_maps(inp):
    f32 = np.float32
    assign = _slot_assign()
    maps = []
    ident = np.eye(128, dtype=f32)
    consts = _const_tables()
    for c in range(NCORES):
        xs = np.empty((T, D), f32)
        cs = np.empty((NSLOT, D), f32)
        for s, a in enumerate(assign[c]):
            if a[0] == 's':
                xs[s * L:(s + 1) * L] = inp['x_sample'][a[1], a[2] * L:(a[2] + 1) * L]
                cs[s] = inp['c'][a[1]]
            else:
                xs[s * L:(s + 1) * L] = inp['x_prompt'][a[1]]
                cs[s] = inp['c_ctx']
        csT = np.ascontiguousarray(cs.reshape(NSLOT, KC, 128).transpose(2, 1, 0))
        b_ = c if c < 2 else 0
        vec = np.zeros((128, 64), f32)
        vec[:, 0:4] = inp['mla_q_norm'][0].reshape(4, 128).T
        vec[:, 4:8] = inp['mla_kv_norm'][0].reshape(4, 128).T
        vec[:, 8] = inp['mla_nope_norm'][0, 0]
        vec[:, 9] = inp['mla_nope_norm'][0, 1]
        vec[0:64, 10] = inp['mla_rope_norm'][0, 0]
        vec[0:64, 11] = inp['mla_rope_norm'][0, 1]
        vec[:, 12] = inp['gqa_qk_norm'][0, 0]
        vec[:, 13] = inp['gqa_qk_norm'][0, 1]
        vec[:, 14] = 0.0 if c < 2 else 1.0
        vec[:, 15] = 1.0 if c < 2 else 0.0
        vec[:, 16:24] = inp['ret_norm'][0].reshape(8, 128).T
        vec2 = np.zeros((128, 128), f32)
        cw = inp['hy_conv_w'][0]
        for kk in range(3):
            vec2[:, kk * 24:(kk + 1) * 24] = cw[kk].reshape(24, 128).T
        vec2[:, 72:96] = inp['hy_conv_b'][0].reshape(24, 128).T
        vec2[:, 96:112] = inp['hy_skip'][0].reshape(16, 128).T
        fb = np.ascontiguousarray(np.stack([inp['hy_filt_b1'][0], inp['hy_filt_b2'][0], inp['hy_filt_freq'][0, 0], inp['hy_filt_freq'][0, 1]], axis=1))
        m = {
            'vec2': vec2, 'fb': fb, 'decb': np.ascontiguousarray(np.broadcast_to(inp['hy_decay'][0][None, :], (128, 2048))),
            'hy_filt_w1': inp['hy_filt_w1'], 'hy_filt_w2': inp['hy_filt_w2'], 'hy_filt_w3': inp['hy_filt_w3'],
            'od_w_in': inp['od_w_in'], 'od_w_out': inp['od_w_out'],
            'lgt': np.ascontiguousarray(np.broadcast_to(inp['ret_decay_logit'][0].reshape(1, 16), (128, 16))),
            's0': np.ascontiguousarray(inp['state_ret'][b_, 0]),
            'vec': vec,
            'ev_w_in': inp['ev_w_in'], 'mla_w_qb': inp['mla_w_qb'], 'mla_w_kvb': inp['mla_w_kvb'], 'ev_w_out': inp['ev_w_out'],
            'c_ckv': np.ascontiguousarray(inp['cache_mla_ckv'][b_, 0]), 'c_kr': np.ascontiguousarray(inp['cache_mla_krope'][b_, 0]),
            'c_k': np.ascontiguousarray(inp['cache_gqa_k'][b_, 0].reshape(256, 256)), 'c_v': np.ascontiguousarray(inp['cache_gqa_v'][b_, 0].reshape(256, 256)),
            **consts,
            'xs': xs, 'csT': csT, 'ident': ident,
            'mod_w': inp['mod_w'], 'mod_b': inp['mod_b'],
            'ffn_w_gate': inp['ffn_w_gate'], 'ffn_w_up': inp['ffn_w_up'], 'ffn_w_down': inp['ffn_w_down'],
        }
        maps.append(m)
    return maps


def kernel(**inputs):
    inp = {k_: np.asarray(v) for k_, v in inputs.items()}
    nc = build_program(dbg=STOP_AFTER)
    maps = make_in_maps(inp)
    res = run_bass_kernel_spmd(nc, maps, core_ids=list(range(NCORES)))
    r = res.results
    f32 = np.float32
    assign = _slot_assign()
    y_prompt = np.zeros((32, 256, D), f32)
    y_sample = np.zeros((2, 1024, D), f32)
    n_ckv = np.zeros((32, 1, 256, 512), f32)
    n_kr = np.zeros((32, 1, 256, 64), f32)
    n_k = np.zeros((32, 1, 256, 2, 128), f32)
    n_v = np.zeros((32, 1, 256, 2, 128), f32)
    n_ret = np.zeros((32, 1, 2, 8, 128, 128), f32)
    for c in range(NCORES):
        yc = np.asarray(r[c]['y'])
        for s_, a in enumerate(assign[c]):
            rows = slice(s_ * L, (s_ + 1) * L)
            if a[0] == 's':
                y_sample[a[1], a[2] * L:(a[2] + 1) * L] = yc[rows]
            else:
                y_prompt[a[1]] = yc[rows]
                for name, dst, shp in (('o_ckv', n_ckv, (256, 512)), ('o_kr', n_kr, (256, 64)),
                                       ('o_k', n_k, (256, 2, 128)), ('o_v', n_v, (256, 2, 128))):
                    if name in r[c]:
                        dst[a[1], 0] = np.asarray(r[c][name])[rows].reshape(shp)
                if 'o_ret' in r[c]:
                    n_ret[a[1], 0] = np.asarray(r[c]['o_ret'])[s_]
    return (y_prompt, y_sample, n_ckv, n_kr, n_k, n_v, n_ret)
```

```python
import math
from contextlib import ExitStack
import numpy as np
import ml_dtypes
import concourse.bass as bass
import concourse.mybir as mybir
from concourse.bass_utils import run_bass_kernel_spmd

F32 = mybir.dt.float32
BF16 = mybir.dt.bfloat16
AF = mybir.ActivationFunctionType
ALU = mybir.AluOpType

NCORES = 8
D = 2048
KC = 16
T = 1280
NSLOT = 5
L = 256
DFF = 5632
NJ = DFF // 128
EPS = 1e-6
TBS = [(0, 512), (512, 512), (1024, 256)]
STOP_AFTER = None


class Sched:
    def __init__(self, nc, es):
        self.nc = nc
        self.es = es
        self.prog = {e: [] for e in ('pe', 'act', 'dve', 'pool', 'sp')}
        self.esem = {e: es.enter_context(nc.semaphore('sem_' + e)) for e in ('pe', 'act', 'dve', 'pool')}
        self.ecnt = {e: 0 for e in self.esem}
        self.dsem = {}
        self.dcnt = {}
        self.waited = {e: {} for e in self.prog}
        self.lastw = {}
        self.readers = {}
        self.bank = 0
        self.nops = 0
        self.pending = {e: [] for e in self.prog}

    def _deps(self, eng, reads, writes):
        toks = self.pending[eng]
        self.pending[eng] = []
        for r in reads:
            if r in self.lastw:
                toks.append(self.lastw[r])
        for w in writes:
            if w in self.lastw:
                toks.append(self.lastw[w])
            toks.extend(self.readers.get(w, ()))
        need = {}
        for (sid, sem, val) in toks:
            if self.waited[eng].get(sid, 0) >= val:
                continue
            if need.get(sid, (None, 0))[1] < val:
                need[sid] = (sem, val)
        for sid, (sem, val) in need.items():
            self.waited[eng][sid] = val
        return list(need.values())

    def fence(self):
        toks = [(e, self.esem[e], self.ecnt[e]) for e in self.esem if self.ecnt[e] > 0]
        toks += [('d_' + key, self.dsem[key], self.dcnt[key]) for key in self.dsem]
        for e in self.prog:
            self.pending[e] = list(toks)

    def _commit(self, tok, reads, writes):
        for r in reads:
            self.readers.setdefault(r, []).append(tok)
        for w in writes:
            self.lastw[w] = tok
            self.readers[w] = []

    def op(self, eng, fn, reads=(), writes=()):
        waits = self._deps(eng, reads, writes)
        self.ecnt[eng] += 1
        val = self.ecnt[eng]
        sem = self.esem[eng]
        tok = (eng, sem, val)
        def run(e, waits=waits, fn=fn, sem=sem):
            for (s, v) in waits:
                e.wait_ge(s, v)
            ins = fn(e)
            ins.then_inc(sem, 1)
        self.prog[eng].append(run)
        self._commit(tok, reads, writes)
        self.nops += 1

    def dma(self, q, out, in_, reads=(), writes=(), key=None):
        if key is None:
            key = writes[0]
        if key not in self.dsem:
            self.dsem[key] = self.es.enter_context(self.nc.semaphore('d_' + str(len(self.dsem))))
            self.dcnt[key] = 0
        waits = self._deps(q, reads, writes)
        self.dcnt[key] += 16
        sem = self.dsem[key]
        tok = ('d_' + key, sem, self.dcnt[key])
        def run(e, waits=waits, sem=sem, out=out, in_=in_):
            for (s, v) in waits:
                e.wait_ge(s, v)
            e.dma_start(out=out, in_=in_).then_inc(sem, 16)
        self.prog[q].append(run)
        self._commit(tok, reads, writes)

    def final_wait(self, q, keys):
        waits = self._deps(q, keys, ())
        def run(e, waits=waits):
            for (s, v) in waits:
                e.wait_ge(s, v)
        self.prog[q].append(run)

    def next_bank(self):
        b = self.bank
        self.bank = (self.bank + 1) % 8
        return b


def _mm(e, out, lhsT, rhs, start, stop):
    return e.matmul(out, lhsT, rhs, start=start, stop=stop)


def build_program(dbg=None):
    nc = bass.Bass("TRN2", target_bir_lowering=False)
    es = ExitStack()
    with es:
        def din(name, shape, dt=F32):
            return nc.dram_tensor(name, list(shape), dt, kind="ExternalInput").ap()

        def dout(name, shape, dt=F32):
            return nc.dram_tensor(name, list(shape), dt, kind="ExternalOutput").ap()

        xs = din("xs", [T, D])
        csT = din("csT", [128, KC, NSLOT])
        ident = din("ident", [128, 128])
        mod_w = din("mod_w", [2, D, 9 * D])
        mod_b = din("mod_b", [2, 9 * D])
        wg = din("ffn_w_gate", [2, 2, D, DFF])
        wu = din("ffn_w_up", [2, 2, D, DFF])
        wd = din("ffn_w_down", [2, 2, DFF, D])
        vec_in = din("vec", [128, 64])
        ev_w_in = din("ev_w_in", [1, D, 2624])
        mla_w_qb = din("mla_w_qb", [1, 512, 1536])
        mla_w_kvb = din("mla_w_kvb", [1, 512, 2048])
        ev_w_out = din("ev_w_out", [1, D, D])
        c_ckv = din("c_ckv", [256, 512])
        c_kr = din("c_kr", [256, 64])
        c_k = din("c_k", [256, 256])
        c_v = din("c_v", [256, 256])
        rt128_in = din("rt128", [128, 128])
        rt64_in = din("rt64", [64, 64])
        cos64_in = din("cos64", [64, 1024])
        sin64_in = din("sin64", [64, 1024])
        cos128_in = din("cos128", [128, 1024])
        sin128_in = din("sin128", [128, 1024])
        od_w_in = din("od_w_in", [1, D, 7168])
        od_w_out = din("od_w_out", [1, D, D])
        rc_in = din("rc", [6, 128, 128])
        pc_in = din("pc", [128, 4])
        lgt_in = din("lgt", [128, 16])
        s0_in = din("s0", [2, 8, 128, 128])
        identb_in = din("identb", [128, 128])
        dft_in = din("dft", [8, 256, 256])
        feat_in = din("feat", [33, 1280])
        hy_w1 = din("hy_filt_w1", [1, 33, 64])
        hy_w2 = din("hy_filt_w2", [1, 64, 64])
        hy_w3 = din("hy_filt_w3", [1, 64, 2048])
        fb_in = din("fb", [64, 4])
        vec2_in = din("vec2", [128, 128])
        negr_in = din("negr", [128, 10])
        decb_in = din("decb", [128, 2048])
        y = dout("y", [T, D])
        o_ret = dout("o_ret", [NSLOT, 2, 8, 128, 128])
        o_ckv = dout("o_ckv", [T, 512])
        o_kr = dout("o_kr", [T, 64])
        o_k = dout("o_k", [T, 256])
        o_v = dout("o_v", [T, 256])

        def sb(name, shape, dt):
            return es.enter_context(nc.sbuf_tensor(name, list(shape), dt))

        X = sb("X", [128, KC, T], F32)
        H = sb("H", [128, KC, T], BF16)
        MODV = sb("MODV", [128, 144, NSLOT], F32)
        IDF = sb("IDF", [128, 128], F32)
        IDB = sb("IDB", [128, 128], BF16)
        ONESB = sb("ONESB", [128, 128], BF16)
        ONESF = sb("ONESF", [128, 128], F32)
        SC = sb("SC", [128, KC, NSLOT], BF16)
        SCF = sb("SCF", [128, KC, NSLOT], F32)
        EPSD = sb("EPSD", [128, 4], F32)
        NV = 64
        VEC = sb("VEC", [128, NV], F32)
        ARENA_ELEMS = 41 * 1024 + 256
        ARENA = sb("ARENA", [128, ARENA_ELEMS], BF16)
        PS = [es.enter_context(nc.psum_tensor("ps%d" % i, [128, 512], F32)) for i in range(8)]

        k = Sched(nc, es)
        out_keys = []

        class Carver:
            def __init__(self, off=0, base=None, limit=None):
                self.off = off
                self.base = ARENA if base is None else base
                self.limit = ARENA_ELEMS if limit is None else limit
            def take(self, shape, dt):
                n = int(np.prod(shape))
                nb = n * (4 if dt == F32 else 2)
                ne = (nb + 1) // 2
                ne = (ne + 15) // 16 * 16
                a = self.base[:, self.off:self.off + ne]
                self.off += ne
                assert self.off <= self.limit, (self.off, self.limit)
                if dt == F32:
                    a = a.bitcast(F32)
                a = a[:, 0:n]
                if len(shape) == 2:
                    return a.rearrange("p (a b) -> p a b", a=shape[0])
                if len(shape) == 3:
                    return a.rearrange("p (a b c) -> p a b c", a=shape[0], b=shape[1])
                return a

        k.dma('sp', IDF[:], ident, writes=['IDF'])
        k.dma('pool', IDB[:], identb_in, writes=['IDB'])
        k.op('dve', lambda e: e.memset(ONESB[:], 1.0), writes=['ONESB'])
        k.op('dve', lambda e: e.memset(ONESF[:], 1.0), writes=['ONESF'])
        k.op('dve', lambda e: e.memset(EPSD[:, 0:1], float(D * EPS)), writes=['EPSD'])
        k.dma('sp', SCF[:], csT, writes=['SCF'])
        k.dma('sp', VEC[:], vec_in, writes=['VEC'])
        k.op('dve', lambda e: e.memset(EPSD[:, 1:2], float(EPS)), writes=['EPSD'])
        k.op('act', lambda e: e.activation(SC[:], SCF[:], AF.Silu), reads=['SCF'], writes=['SC'])

        cv = Carver()
        XT = [cv.take([KC * 128], F32) for _ in range(2)]
        for tt in range(T // 128):
            st = XT[tt % 2]
            k.dma('sp', st, xs[tt * 128:(tt + 1) * 128, :], writes=['XT%d' % (tt % 2)])
            for k4 in range(KC // 4):
                b = k.next_bank()
                def f(e, st=st, k4=k4, b=b):
                    ins = None
                    for q in range(4):
                        kk = k4 * 4 + q
                        ins = e.transpose(PS[b][:, q * 128:(q + 1) * 128], st[:, kk * 128:(kk + 1) * 128], IDF[:])
                    return ins
                k.op('pe', f, reads=['XT%d' % (tt % 2), 'IDF'], writes=['ps%d' % b])
                k.op('dve' if k4 % 2 == 0 else 'act',
                     (lambda e, b=b, k4=k4, tt=tt: e.tensor_copy(
                         X[:, k4 * 4:(k4 + 1) * 4, tt * 128:(tt + 1) * 128],
                         PS[b][:].rearrange("p (q c) -> p q c", q=4))) if k4 % 2 == 0 else
                     (lambda e, b=b, k4=k4, tt=tt: e.activation(
                         X[:, k4 * 4:(k4 + 1) * 4, tt * 128:(tt + 1) * 128],
                         PS[b][:].rearrange("p (q c) -> p q c", q=4), AF.Copy)),
                     reads=['ps%d' % b], writes=['X'])

        def mod_phase(l):
            k.fence()
            cvm = Carver()
            WM = [cvm.take([KC, 512], BF16) for _ in range(2)]
            BM = [cvm.take([512], BF16) for _ in range(2)]
            for g in range(36):
                s = g % 2
                k.dma('pool', WM[s], mod_w[l][:, g * 512:(g + 1) * 512].rearrange("(k p) c -> p k c", p=128),
                      writes=['WM%d' % s])
                k.dma('pool', BM[s][0:1, :], mod_b[l:l + 1, g * 512:(g + 1) * 512], writes=['BM%d' % s])
                b = k.next_bank()
                def f(e, s=s, b=b):
                    ins = None
                    for fc in range(4):
                        o = PS[b][:, fc * 8:fc * 8 + NSLOT]
                        for kk in range(KC):
                            e.matmul(o, WM[s][:, kk, fc * 128:(fc + 1) * 128], SC[:, kk, :], start=(kk == 0), stop=False)
                        ins = e.matmul(o, BM[s][0:1, fc * 128:(fc + 1) * 128], ONESB[0:1, 0:NSLOT], start=False, stop=True)
                    return ins
                k.op('pe', f, reads=['WM%d' % s, 'BM%d' % s, 'SC', 'ONESB'], writes=['ps%d' % b])
                k.op('dve', lambda e, b=b, g=g: e.tensor_copy(
                    MODV[:, g * 4:(g + 1) * 4, :],
                    PS[b][:, 0:32].rearrange("p (a c) -> p a c", a=4)[:, :, 0:NSLOT]),
                    reads=['ps%d' % b], writes=['MODV'])
            for i in (1, 4, 7):
                k.op('dve', lambda e, i=i: e.tensor_scalar_add(MODV[:, i * 16:(i + 1) * 16, :], MODV[:, i * 16:(i + 1) * 16, :], 1.0),
                     reads=['MODV'], writes=['MODV'])
            for i in (2, 8):
                k.op('dve', lambda e, i=i: e.tensor_scalar_mul(MODV[:, i * 16:(i + 1) * 16, :], MODV[:, i * 16:(i + 1) * 16, :], 0.5),
                     reads=['MODV'], writes=['MODV'])

        def prenorm(i, cvp):
            SQ = [cvp.take([512], BF16) for _ in range(2)]
            TMP = [cvp.take([T], F32) for _ in range(2)]
            RS = cvp.take([T], F32)
            for ti, (t0, tn) in enumerate(TBS):
                b = k.next_bank()
                for kk in range(KC):
                    s = kk % 2
                    k.op('act', lambda e, s=s, kk=kk, t0=t0, tn=tn: e.activation(SQ[s][:, 0:tn], X[:, kk, t0:t0 + tn], AF.Square),
                         reads=['X'], writes=['SQ%d' % s])
                    k.op('pe', lambda e, s=s, kk=kk, tn=tn, b=b: e.matmul(PS[b][:, 0:tn], ONESB[:], SQ[s][:, 0:tn], start=(kk == 0), stop=(kk == KC - 1)),
                         reads=['SQ%d' % s, 'ONESB'], writes=['ps%d' % b])
                k.op('act', lambda e, b=b, t0=t0, tn=tn: e.activation(RS[:, t0:t0 + tn], PS[b][:, 0:tn], AF.Sqrt, bias=EPSD[:, 0:1]),
                     reads=['ps%d' % b, 'EPSD'], writes=['RS'])
                k.op('dve', lambda e, t0=t0, tn=tn: e.reciprocal(RS[:, t0:t0 + tn], RS[:, t0:t0 + tn]),
                     reads=['RS'], writes=['RS'])
            for kk in range(KC):
                s = kk % 2
                k.op('dve', lambda e, s=s, kk=kk: e.scalar_tensor_tensor(TMP[s][:], X[:, kk, :], float(math.sqrt(D)), RS[:], ALU.mult, ALU.mult),
                     reads=['X', 'RS'], writes=['TMP%d' % s])
                for sl in range(NSLOT):
                    if sl in (1, 3):
                        k.op('dve', lambda e, s=s, kk=kk, sl=sl: e.tensor_scalar(
                            H[:, kk, sl * L:(sl + 1) * L], TMP[s][:, sl * L:(sl + 1) * L],
                            MODV[:, (3 * i + 1) * 16 + kk, sl:sl + 1], MODV[:, (3 * i) * 16 + kk, sl:sl + 1], ALU.mult, ALU.add),
                            reads=['TMP%d' % s, 'MODV'], writes=['H'])
                    else:
                        k.op('act', lambda e, s=s, kk=kk, sl=sl: e.activation(
                            H[:, kk, sl * L:(sl + 1) * L], TMP[s][:, sl * L:(sl + 1) * L], AF.Identity,
                            bias=MODV[:, (3 * i) * 16 + kk, sl:sl + 1], scale=MODV[:, (3 * i + 1) * 16 + kk, sl:sl + 1]),
                            reads=['TMP%d' % s, 'MODV'], writes=['H'])

        def ffn(l, j, i):
            k.fence()
            cvf = Carver()
            WG = [cvf.take([KC, 256], BF16) for _ in range(2)]
            WU = [cvf.take([KC, 256], BF16) for _ in range(2)]
            WD = [cvf.take([4, D], BF16) for _ in range(2)]
            a_off = cvf.off
            A = [cvf.take([2, T], BF16) for _ in range(3)]
            SIL = [cvf.take([512], F32)]
            NG = NJ // 2
            silc = [0]

            def gu_load(g):
                s = g % 2
                k.dma('pool', WG[s], wg[l, j][:, g * 256:(g + 1) * 256].rearrange("(k p) c -> p k c", p=128), writes=['WG%d' % s])
                k.dma('pool', WU[s], wu[l, j][:, g * 256:(g + 1) * 256].rearrange("(k p) c -> p k c", p=128), writes=['WU%d' % s])

            def dn_load(p):
                s = p % 2
                k.dma('pool', WD[s], wd[l, j][p * 512:(p + 1) * 512, :].rearrange("(c p) n -> p c n", p=128), writes=['WD%d' % s])

            gu_load(0)
            gu_load(1)
            dn_load(0)
            prenorm(i, Carver(off=a_off))
            k.fence()

            def gu(g):
                s = g % 2
                a3 = g % 3
                if g >= 2:
                    gu_load(g)
                for jj in range(2):
                    for (t0, tn) in TBS:
                        bg = k.next_bank()
                        bu = k.next_bank()
                        def fg(e, W=WG[s], b=bg, jj=jj, t0=t0, tn=tn):
                            ins = None
                            for kk in range(KC):
                                ins = e.matmul(PS[b][:, 0:tn], W[:, kk, jj * 128:(jj + 1) * 128], H[:, kk, t0:t0 + tn], start=(kk == 0), stop=(kk == KC - 1))
                            return ins
                        k.op('pe', fg, reads=['WG%d' % s, 'H'], writes=['ps%d' % bg])
                        def fu(e, W=WU[s], b=bu, jj=jj, t0=t0, tn=tn):
                            ins = None
                            for kk in range(KC):
                                ins = e.matmul(PS[b][:, 0:tn], W[:, kk, jj * 128:(jj + 1) * 128], H[:, kk, t0:t0 + tn], start=(kk == 0), stop=(kk == KC - 1))
                            return ins
                        k.op('pe', fu, reads=['WU%d' % s, 'H'], writes=['ps%d' % bu])
                        k.op('act', lambda e, b=bg, tn=tn: e.activation(SIL[0][:, 0:tn], PS[b][:, 0:tn], AF.Silu),
                             reads=['ps%d' % bg], writes=['SIL0'])
                        k.op('dve', lambda e, b=bu, a3=a3, jj=jj, t0=t0, tn=tn: e.tensor_tensor(
                            A[a3][:, jj, t0:t0 + tn], SIL[0][:, 0:tn], PS[b][:, 0:tn], ALU.mult),
                            reads=['SIL0', 'ps%d' % bu], writes=['A%d' % a3])

            def dn(p):
                s = p % 2
                if p >= 1:
                    dn_load(p)
                for oc in range(KC):
                    for (t0, tn) in TBS:
                        b = k.next_bank()
                        def fd(e, s=s, b=b, oc=oc, t0=t0, tn=tn):
                            ins = None
                            for q in range(4):
                                g_ = 2 * p + q // 2
                                ins = e.matmul(PS[b][:, 0:tn], WD[s][:, q, oc * 128:(oc + 1) * 128], A[g_ % 3][:, q % 2, t0:t0 + tn], start=(q == 0), stop=(q == 3))
                            return ins
                        k.op('pe', fd, reads=['WD%d' % s, 'A%d' % ((2 * p) % 3), 'A%d' % ((2 * p + 1) % 3)], writes=['ps%d' % b])
                        for sl in range(t0 // L, (t0 + tn) // L):
                            c0 = sl * L - t0
                            k.op('dve', lambda e, b=b, oc=oc, sl=sl, c0=c0: e.scalar_tensor_tensor(
                                X[:, oc, sl * L:(sl + 1) * L], PS[b][:, c0:c0 + L],
                                MODV[:, (3 * i + 2) * 16 + oc, sl:sl + 1], X[:, oc, sl * L:(sl + 1) * L], ALU.mult, ALU.add),
                                reads=['ps%d' % b, 'MODV', 'X'], writes=['X'])

            gu(0)
            gu(1)
            for p in range(NG // 2):
                if 2 * p + 2 < NG:
                    gu(2 * p + 2)
                dn(p)
                if 2 * p + 3 < NG:
                    gu(2 * p + 3)

        def load_w(slot, dram2d, ncols, key):
            k.dma('pool', slot[:, :, 0:ncols], dram2d.rearrange("(k p) c -> p k c", p=128), writes=[key])

        def fm_unit(W, c0, m, src, nk, t0, tn, rkeys):
            b = k.next_bank()
            def f(e, b=b):
                ins = None
                for kk in range(nk):
                    ins = e.matmul(PS[b][0:m, 0:tn], W[:, kk, c0:c0 + m], src[:, kk, t0:t0 + tn], start=(kk == 0), stop=(kk == nk - 1))
                return ins
            k.op('pe', f, reads=rkeys, writes=['ps%d' % b])
            return b

        def rstd_from(bs, m, tn, nfeat, RSQ, rkey):
            k.op('act', lambda e: e.activation(RSQ[0:m, 0:tn], PS[bs][0:m, 0:tn], AF.Sqrt, bias=EPSD[0:m, 1:2], scale=1.0 / nfeat),
                 reads=['ps%d' % bs, 'EPSD'], writes=[rkey])
            k.op('dve', lambda e: e.reciprocal(RSQ[0:m, 0:tn], RSQ[0:m, 0:tn]), reads=[rkey], writes=[rkey])

        def sumsq_unit(bs, b, m, tn, SQs, sqkey, start, stop):
            k.op('act', lambda e: e.activation(SQs[0:m, 0:tn], PS[b][0:m, 0:tn], AF.Square), reads=['ps%d' % b], writes=[sqkey])
            k.op('pe', lambda e: e.matmul(PS[bs][0:m, 0:tn], ONESB[0:m, 0:m], SQs[0:m, 0:tn], start=start, stop=stop),
                 reads=[sqkey, 'ONESB'], writes=['ps%d' % bs])

        def tok_major_out(srcs, m, tn, t0, dst, width, OSTG, okey, oname):
            for tt in range(tn // 128):
                b = k.next_bank()
                def f(e, b=b, tt=tt):
                    ins = None
                    for c, sr in enumerate(srcs):
                        ins = e.transpose(PS[b][:, c * m:(c + 1) * m], sr[0:m, tt * 128:(tt + 1) * 128], IDF[0:m, 0:m])
                    return ins
                k.op('pe', f, reads=[oname + 'CF', 'IDF'], writes=['ps%d' % b])
                s2 = tok_major_out.cnt % 2
                tok_major_out.cnt += 1
                k.op('dve', lambda e, b=b, s2=s2: e.tensor_copy(OSTG[s2][:, 0:width], PS[b][:, 0:width]),
                     reads=['ps%d' % b], writes=[okey + str(s2)])
                r0 = t0 + tt * 128
                k.dma('sp', dst[r0:r0 + 128, :], OSTG[s2][:, 0:width], reads=[okey + str(s2)], writes=[oname + 'o%d' % r0], key=oname + 'out%d' % s2)
                out_keys.append(oname + 'o%d' % r0)
        tok_major_out.cnt = 0

        def even_attention(l, QA, CKVb, KRb, KG, VG, QG, p_end):
            ei = l // 2
            k.fence()
            HF = H[:].rearrange("p k t -> p (k t)")
            ca = Carver(off=p_end)
            ch = Carver(base=HF, limit=KC * T)
            OM = ch.take([4, T], BF16)
            WO = [ch.take([4, 512], BF16) for _ in range(2)]
            COS = ch.take([1024], F32)
            SIN = ch.take([1024], F32)
            RT = ch.take([128], BF16)
            CKc = ch.take([4, 256], BF16)
            KRc = ch.take([256], BF16)
            KGc = ch.take([2, 256], BF16)
            VGc = ch.take([2, 256], BF16)
            CST = ch.take([2, 512], F32)
            WQ = ch.take([4, 192], BF16)
            WKV = ch.take([4, 256], BF16)
            QN = ca.take([T], BF16)
            QR = ca.take([T], BF16)
            QRr = ca.take([1024], BF16)
            KN = ca.take([T], BF16)
            KNc = ca.take([256], BF16)
            KRr = ca.take([1024], BF16)
            VM = ca.take([12, 128], BF16)
            PT = [ca.take([512], BF16) for _ in range(2)]
            OP = ca.take([T], BF16)
            OS = ca.take([1024], BF16)
            RSQ = ca.take([512], F32)
            SQ = ca.take([512], BF16)
            RD = ca.take([512], F32)
            TMPF = ca.take([512], F32)
            GATE = (3 * 1 + 2) * 16
            st = {'sb': 0, 'acc': 0, 'pt': 0, 'wo': 0}

            def sbank():
                st['sb'] = (st['sb'] + 1) % 4
                return st['sb']

            def accbanks():
                st['acc'] ^= 1
                return (4, 5) if st['acc'] else (6, 7)

            k.dma('sp', CST[:, :, 0:512], c_ckv.rearrange("(t p) f -> p t f", p=128), writes=['CST'])
            for t in range(2):
                b = k.next_bank()
                def f(e, b=b, t=t):
                    ins = None
                    for c in range(4):
                        ins = e.transpose(PS[b][:, c * 128:(c + 1) * 128], CST[:, t, c * 128:(c + 1) * 128], IDF[:])
                    return ins
                k.op('pe', f, reads=['CST', 'IDF'], writes=['ps%d' % b])
                k.op('dve', lambda e, b=b, t=t: e.tensor_copy(CKc[:, :, t * 128:(t + 1) * 128], PS[b][:].rearrange("p (c q) -> p c q", c=4)),
                     reads=['ps%d' % b], writes=['CKc'])
            k.dma('sp', CST[:, :, 0:64], c_kr.rearrange("(t p) f -> p t f", p=128), writes=['CST'])
            b = k.next_bank()
            def f(e, b=b):
                ins = None
                for t in range(2):
                    ins = e.transpose(PS[b][0:64, t * 128:(t + 1) * 128], CST[:, t, 0:64], IDF[:])
                return ins
            k.op('pe', f, reads=['CST', 'IDF'], writes=['ps%d' % b])
            k.op('dve', lambda e, b=b: e.tensor_copy(KRc[0:64, 0:256], PS[b][0:64, 0:256]), reads=['ps%d' % b], writes=['KRc'])
            k.dma('sp', CST[:, :, 0:256], c_k.rearrange("(t p) f -> p t f", p=128), writes=['CST'])
            b = k.next_bank()
            def f(e, b=b):
                ins = None
                for kvh in range(2):
                    for t in range(2):
                        ins = e.transpose(PS[b][:, (kvh * 2 + t) * 128:(kvh * 2 + t + 1) * 128], CST[:, t, kvh * 128:(kvh + 1) * 128], IDF[:])
                return ins
            k.op('pe', f, reads=['CST', 'IDF'], writes=['ps%d' % b])
            k.op('dve', lambda e, b=b: e.tensor_copy(KGc[:, :, :], PS[b][:].rearrange("p (c q) -> p c q", c=2)), reads=['ps%d' % b], writes=['KGc'])
            k.dma('pool', VGc[:, :, :], c_v.rearrange("(t p) f -> p t f", p=128), writes=['VGc'])

            def load_tables(m, cos_in, sin_in, rt_in):
                k.dma('sp', COS[0:m, :], cos_in, writes=['COS'])
                k.dma('sp', SIN[0:m, :], sin_in, writes=['SIN'])
                k.dma('pool', RT[0:m, 0:m], rt_in, writes=['RT'])

            def rope(dst, src, m, rkeys, wkey):
                for blk in range(2):
                    c0 = blk * 512
                    b = k.next_bank()
                    k.op('pe', lambda e, b=b, c0=c0: e.matmul(PS[b][0:m, 0:512], RT[0:m, 0:m], src[0:m, c0:c0 + 512], start=True, stop=True),
                         reads=rkeys + ['RT'], writes=['ps%d' % b])
                    k.op('dve', lambda e, b=b, c0=c0: e.tensor_tensor(TMPF[0:m, :], PS[b][0:m, 0:512], SIN[0:m, c0:c0 + 512], ALU.mult),
                         reads=['ps%d' % b, 'SIN'], writes=['TMPF'])
                    k.op('dve', lambda e, c0=c0: e.tensor_tensor(RD[0:m, :], src[0:m, c0:c0 + 512], COS[0:m, c0:c0 + 512], ALU.mult),
                         reads=rkeys + ['COS'], writes=['RD'])
                    k.op('dve', lambda e, c0=c0: e.tensor_tensor(dst[0:m, c0:c0 + 512], TMPF[0:m, :], RD[0:m, :], ALU.add),
                         reads=['TMPF', 'RD'], writes=[wkey])

            def normed(W, c0, m, src, nk, ntok_list, gcol, nfeat, dst, rkeys, wkey):
                for (t0, tn) in ntok_list:
                    b = fm_unit(W, c0, m, src, nk, t0, tn, rkeys)
                    bs = k.next_bank()
                    sumsq_unit(bs, b, m, tn, SQ, 'SQa', True, True)
                    rstd_from(bs, m, tn, nfeat, RSQ, 'RSQa')
                    k.op('dve', lambda e, b=b, t0=t0, tn=tn: e.scalar_tensor_tensor(dst[0:m, t0:t0 + tn], PS[b][0:m, 0:tn], VEC[0:m, gcol:gcol + 1], RSQ[0:m, 0:tn], ALU.mult, ALU.mult),
                         reads=['ps%d' % b, 'VEC', 'RSQa'], writes=[wkey])

            def attend(nq, q0, parts, ktiles, scale, dst, dkey):
                bo, bd = accbanks()
                nkt = len(ktiles)
                pend = None
                for kt in range(nkt):
                    bS = sbank()
                    def fS(e, bS=bS, kt=kt):
                        ins = None
                        for pi, (lfn, rhs, kd, _) in enumerate(parts):
                            ins = e.matmul(PS[bS][:, 0:nq], lfn(kt), rhs[0:kd, q0:q0 + nq], start=(pi == 0), stop=(pi == len(parts) - 1))
                        return ins
                    rk = []
                    for p_ in parts:
                        rk += p_[3]
                    k.op('pe', fS, reads=rk, writes=['ps%d' % bS])
                    pi_ = st['pt'] % 2
                    st['pt'] += 1
                    k.op('act', lambda e, bS=bS, pi_=pi_: e.activation(PT[pi_][:, 0:nq], PS[bS][:, 0:nq], AF.Exp, scale=float(scale)),
                         reads=['ps%d' % bS], writes=['PT%d' % pi_])
                    V, vkeys = ktiles[kt]
                    def fO(e, V=V, pi_=pi_, kt=kt):
                        e.matmul(PS[bo][:, 0:nq], V, PT[pi_][:, 0:nq], start=(kt == 0), stop=(kt == nkt - 1))
                        return e.matmul(PS[bd][:, 0:nq], ONESB[:], PT[pi_][:, 0:nq], start=(kt == 0), stop=(kt == nkt - 1))
                    if pend is not None:
                        k.op('pe', pend[0], reads=pend[1], writes=['ps%d' % bo, 'ps%d' % bd])
                    pend = (fO, ['PT%d' % pi_, 'ONESB'] + vkeys)
                k.op('pe', pend[0], reads=pend[1], writes=['ps%d' % bo, 'ps%d' % bd])
                k.op('dve', lambda e: e.reciprocal(RD[:, 0:nq], PS[bd][:, 0:nq]), reads=['ps%d' % bd], writes=['RD'])
                k.op('dve', lambda e: e.tensor_tensor(dst, PS[bo][:, 0:nq], RD[:, 0:nq], ALU.mult), reads=['ps%d' % bo, 'RD'], writes=[dkey])

            def combine(hh):
                k.op('dve', lambda e: e.tensor_copy(OM[:, hh, 1024:1280], OP[:, 1024:1280]), reads=['OP'], writes=['OM'])
                k.op('dve', lambda e: e.tensor_scalar_mul(OM[:, hh, 0:1024], OP[:, 0:1024], VEC[:, 14:15]), reads=['OP', 'VEC'], writes=['OM'])
                k.op('dve', lambda e: e.scalar_tensor_tensor(OM[:, hh, 0:1024], OS[:, 0:1024], VEC[:, 15:16], OM[:, hh, 0:1024], ALU.mult, ALU.add),
                     reads=['OS', 'VEC', 'OM'], writes=['OM'])

            def out_proj(w_out, r0):
                for og in range(4):
                    s = st['wo'] % 2
                    st['wo'] += 1
                    k.dma('pool', WO[s], w_out[r0:r0 + 512, og * 512:(og + 1) * 512].rearrange("(h p) c -> p h c", p=128), writes=['WO%d' % s])
                    for oc4 in range(4):
                        oc = og * 4 + oc4
                        for (t0, tn) in TBS:
                            b = k.next_bank()
                            def f(e, s=s, b=b, oc4=oc4, t0=t0, tn=tn):
                                ins = None
                                for hh in range(4):
                                    ins = e.matmul(PS[b][:, 0:tn], WO[s][:, hh, oc4 * 128:(oc4 + 1) * 128], OM[:, hh, t0:t0 + tn], start=(hh == 0), stop=(hh == 3))
                                return ins
                            k.op('pe', f, reads=['WO%d' % s, 'OM'], writes=['ps%d' % b])
                            for sl in range(t0 // L, (t0 + tn) // L):
                                c0 = sl * L - t0
                                k.op('dve', lambda e, b=b, oc=oc, sl=sl, c0=c0: e.scalar_tensor_tensor(
                                    X[:, oc, sl * L:(sl + 1) * L], PS[b][:, c0:c0 + L], MODV[:, GATE + oc, sl:sl + 1], X[:, oc, sl * L:(sl + 1) * L], ALU.mult, ALU.add),
                                    reads=['ps%d' % b, 'MODV', 'X'], writes=['X'])

            load_tables(64, cos64_in, sin64_in, rt64_in)
            rope(KRr, KRb, 64, ['KRb'], 'KRr')
            sc_m = 192.0 ** -0.5
            for h in range(8):
                k.dma('pool', WQ, mla_w_qb[ei][:, h * 192:(h + 1) * 192].rearrange("(k p) c -> p k c", p=128), writes=['WQ'])
                k.dma('pool', WKV, mla_w_kvb[ei][:, h * 256:(h + 1) * 256].rearrange("(k p) c -> p k c", p=128), writes=['WKV'])
                normed(WQ, 0, 128, QA, 4, TBS, 8, 128, QN, ['WQ', 'QA'], 'QN')
                normed(WQ, 128, 64, QA, 4, TBS, 10, 64, QR, ['WQ', 'QA'], 'QR')
                rope(QRr, QR, 64, ['QR'], 'QRr')
                normed(WKV, 0, 128, CKVb, 4, TBS, 9, 128, KN, ['WKV', 'CKVb'], 'KN')
                normed(WKV, 0, 128, CKc, 4, [(0, 256)], 9, 128, KNc, ['WKV', 'CKc'], 'KNc')
                for g3 in range(3):
                    b = k.next_bank()
                    def f(e, b=b, g3=g3):
                        ins = None
                        for q in range(4):
                            ti = g3 * 4 + q
                            for c in range(4):
                                lh = CKc[:, c, ti * 128:(ti + 1) * 128] if ti < 2 else CKVb[:, c, (ti - 2) * 128:(ti - 1) * 128]
                                ins = e.matmul(PS[b][:, q * 128:(q + 1) * 128], lh, WKV[:, c, 128:256], start=(c == 0), stop=(c == 3))
                        return ins
                    k.op('pe', f, reads=['CKc', 'CKVb', 'WKV'], writes=['ps%d' % b])
                    k.op('act', lambda e, b=b, g3=g3: e.activation(VM[:, g3 * 4:(g3 + 1) * 4, :], PS[b][:].rearrange("p (q c) -> p q c", q=4), AF.Copy),
                         reads=['ps%d' % b], writes=['VM'])
                for s_ in range(NSLOT):
                    attend(L, s_ * L,
                           [(lambda kt, s_=s_: KN[:, s_ * L + kt * 128:s_ * L + (kt + 1) * 128], QN, 128, ['KN', 'QN']),
                            (lambda kt, s_=s_: KRb[0:64, s_ * L + kt * 128:s_ * L + (kt + 1) * 128], QR, 64, ['KRb', 'QR'])],
                           [(VM[:, 2 + s_ * 2 + kt, :], ['VM']) for kt in range(2)], sc_m, OP[:, s_ * L:(s_ + 1) * L], 'OP')
                for qb in range(2):
                    attend(512, qb * 512,
                           [(lambda kt: KNc[:, kt * 128:(kt + 1) * 128] if kt < 2 else KN[:, (kt - 2) * 128:(kt - 1) * 128], QN, 128, ['KNc', 'KN', 'QN']),
                            (lambda kt: KRc[0:64, kt * 128:(kt + 1) * 128] if kt < 2 else KRr[0:64, (kt - 2) * 128:(kt - 1) * 128], QRr, 64, ['KRc', 'KRr', 'QRr'])],
                           [(VM[:, kt, :], ['VM']) for kt in range(10)], sc_m, OS[:, qb * 512:(qb + 1) * 512], 'OS')
                combine(h % 4)
                if h % 4 == 3:
                    out_proj(ev_w_out[ei], (h // 4) * 512)
            if dbg == 'e2':
                return
            k.fence()
            load_tables(128, cos128_in, sin128_in, rt128_in)
            KGr = KRr
            QGr = QRr
            sc_g = 128.0 ** -0.5
            for kvh in range(2):
                rope(KGr, KG[:, kvh, :], 128, ['KG'], 'KGr')
                for g in range(4):
                    h = kvh * 4 + g
                    rope(QGr, QG[:, h, :], 128, ['QG'], 'QGr')
                    for s_ in range(NSLOT):
                        attend(L, s_ * L,
                               [(lambda kt, s_=s_, kvh=kvh: KG[:, kvh, s_ * L + kt * 128:s_ * L + (kt + 1) * 128], QG[:, h, :], 128, ['KG', 'QG'])],
                               [(VG[:, s_ * 2 + kt, kvh * 128:(kvh + 1) * 128], ['VG']) for kt in range(2)], sc_g, OP[:, s_ * L:(s_ + 1) * L], 'OP')
                    for qb in range(2):
                        attend(512, qb * 512,
                               [(lambda kt, kvh=kvh: KGc[:, kvh, kt * 128:(kt + 1) * 128] if kt < 2 else KGr[:, (kt - 2) * 128:(kt - 1) * 128], QGr, 128, ['KGc', 'KGr', 'QGr'])],
                               [((VGc[:, kt, kvh * 128:(kvh + 1) * 128] if kt < 2 else VG[:, kt - 2, kvh * 128:(kvh + 1) * 128]), ['VGc', 'VG']) for kt in range(10)],
                               sc_g, OS[:, qb * 512:(qb + 1) * 512], 'OS')
                    combine(g)
                out_proj(ev_w_out[ei], 1024 + kvh * 512)

        def even_mixer(l):
            ei = l // 2
            w_in = ev_w_in[ei]
            k.fence()
            cv = Carver()
            QA = cv.take([4, T], BF16)
            CKVb = cv.take([4, T], BF16)
            KRb = cv.take([T], BF16)
            KG = cv.take([2, T], BF16)
            VG = cv.take([10, 256], BF16)
            qg_off = cv.off
            QG = cv.take([8, T], BF16)
            p_end = cv.off
            WS = cv.take([KC, 512], BF16)
            SQ = [cv.take([512], BF16) for _ in range(2)]
            RSQ = cv.take([512], F32)
            cva = Carver(off=qg_off)
            CF = [cva.take([512], F32) for _ in range(4)]
            OSTG = [cva.take([512], F32) for _ in range(2)]
            assert cva.off <= p_end

            load_w(WS, w_in[:, 0:512], 512, 'WS')
            prenorm(1, Carver())
            k.fence()

            def joint(col0, g0, outb, f32dst, oname, preloaded=False):
                if not preloaded:
                    load_w(WS, w_in[:, col0:col0 + 512], 512, 'WS')
                for (t0, tn) in TBS:
                    bs = k.next_bank()
                    bc = []
                    for c in range(4):
                        b = fm_unit(WS, c * 128, 128, H, KC, t0, tn, ['WS', 'H'])
                        bc.append(b)
                        sumsq_unit(bs, b, 128, tn, SQ[c % 2], 'SQ%d' % (c % 2), c == 0, c == 3)
                    rstd_from(bs, 128, tn, 512, RSQ, 'RSQ')
                    for c in range(4):
                        if f32dst is not None:
                            k.op('dve', lambda e, c=c, b=bc[c], tn=tn: e.scalar_tensor_tensor(CF[c][:, 0:tn], PS[b][:, 0:tn], VEC[:, g0 + c:g0 + c + 1], RSQ[:, 0:tn], ALU.mult, ALU.mult),
                                 reads=['ps%d' % bc[c], 'VEC', 'RSQ'], writes=[oname + 'CF'])
                            k.op('act', lambda e, c=c, t0=t0, tn=tn: e.activation(outb[:, c, t0:t0 + tn], CF[c][:, 0:tn], AF.Copy),
                                 reads=[oname + 'CF'], writes=[oname])
                        else:
                            k.op('dve', lambda e, c=c, b=bc[c], t0=t0, tn=tn: e.scalar_tensor_tensor(outb[:, c, t0:t0 + tn], PS[b][:, 0:tn], VEC[:, g0 + c:g0 + c + 1], RSQ[:, 0:tn], ALU.mult, ALU.mult),
                                 reads=['ps%d' % bc[c], 'VEC', 'RSQ'], writes=[oname])
                    if f32dst is not None:
                        tok_major_out(CF, 128, tn, t0, f32dst, 512, OSTG, 'OSTG', oname)

            joint(0, 0, QA, None, 'QA', True)
            joint(512, 4, CKVb, o_ckv, 'CKVb')
            load_w(WS, w_in[:, 1024:1088], 64, 'WS')
            for (t0, tn) in TBS:
                b = fm_unit(WS, 0, 64, H, KC, t0, tn, ['WS', 'H'])
                bs = k.next_bank()
                sumsq_unit(bs, b, 64, tn, SQ[0], 'SQ0', True, True)
                rstd_from(bs, 64, tn, 64, RSQ, 'RSQ')
                k.op('dve', lambda e, b=b, tn=tn: e.scalar_tensor_tensor(CF[0][0:64, 0:tn], PS[b][0:64, 0:tn], VEC[0:64, 11:12], RSQ[0:64, 0:tn], ALU.mult, ALU.mult),
                     reads=['ps%d' % b, 'VEC', 'RSQ'], writes=['KRbCF'])
                k.op('act', lambda e, t0=t0, tn=tn: e.activation(KRb[0:64, t0:t0 + tn], CF[0][0:64, 0:tn], AF.Copy), reads=['KRbCF'], writes=['KRb'])
                tok_major_out([CF[0]], 64, tn, t0, o_kr, 64, OSTG, 'OSTG', 'KRb')
            k.fence()
            for g in range(2):
                load_w(WS, w_in[:, 1088 + g * 512:1088 + (g + 1) * 512], 512, 'WS')
                for c in range(4):
                    for (t0, tn) in TBS:
                        b = fm_unit(WS, c * 128, 128, H, KC, t0, tn, ['WS', 'H'])
                        bs = k.next_bank()
                        sumsq_unit(bs, b, 128, tn, SQ[0], 'SQ0', True, True)
                        rstd_from(bs, 128, tn, 128, RSQ, 'RSQ')
                        k.op('dve', lambda e, b=b, h=g * 4 + c, t0=t0, tn=tn: e.scalar_tensor_tensor(QG[:, h, t0:t0 + tn], PS[b][:, 0:tn], VEC[:, 12:13], RSQ[:, 0:tn], ALU.mult, ALU.mult),
                             reads=['ps%d' % b, 'VEC', 'RSQ'], writes=['QG'])
            cvb = Carver(off=cv.off)
            CF2 = [cvb.take([512], F32) for _ in range(2)]
            OSTG2 = [cvb.take([512], F32) for _ in range(2)]
            load_w(WS, w_in[:, 2112:2624], 512, 'WS')
            for (t0, tn) in TBS:
                for c in range(2):
                    b = fm_unit(WS, c * 128, 128, H, KC, t0, tn, ['WS', 'H'])
                    bs = k.next_bank()
                    sumsq_unit(bs, b, 128, tn, SQ[0], 'SQ0', True, True)
                    rstd_from(bs, 128, tn, 128, RSQ, 'RSQ')
                    k.op('dve', lambda e, b=b, c=c, tn=tn: e.scalar_tensor_tensor(CF2[c][:, 0:tn], PS[b][:, 0:tn], VEC[:, 13:14], RSQ[:, 0:tn], ALU.mult, ALU.mult),
                         reads=['ps%d' % b, 'VEC', 'RSQ'], writes=['KGCF'])
                    k.op('act', lambda e, c=c, t0=t0, tn=tn: e.activation(KG[:, c, t0:t0 + tn], CF2[c][:, 0:tn], AF.Copy), reads=['KGCF'], writes=['KG'])
                tok_major_out(CF2, 128, tn, t0, o_k, 256, OSTG2, 'OSTGb', 'KG')
            for tile in range(T // 128):
                b = k.next_bank()
                def f(e, b=b, tile=tile):
                    ins = None
                    for kk in range(KC):
                        ins = e.matmul(PS[b][:, 0:256], H[:, kk, tile * 128:(tile + 1) * 128], WS[:, kk, 256:512], start=(kk == 0), stop=(kk == KC - 1))
                    return ins
                k.op('pe', f, reads=['WS', 'H'], writes=['ps%d' % b])
                s2 = tile % 2
                k.op('dve', lambda e, b=b, s2=s2: e.tensor_copy(OSTG2[s2][:, 0:256], PS[b][:, 0:256]), reads=['ps%d' % b], writes=['OSTGb%d' % s2])
                k.op('act', lambda e, s2=s2, tile=tile: e.activation(VG[:, tile, :], OSTG2[s2][:, 0:256], AF.Copy), reads=['OSTGb%d' % s2], writes=['VG'])
                k.dma('sp', o_v[tile * 128:(tile + 1) * 128, :], OSTG2[s2][:, 0:256], reads=['OSTGb%d' % s2], writes=['VGo%d' % tile], key='VGout%d' % s2)
                out_keys.append('VGo%d' % tile)
            if dbg == 'e1':
                return
            even_attention(l, QA, CKVb, KRb, KG, VG, QG, p_end)

        USCR = nc.dram_tensor("u_scr", [56, 128, T], BF16)

        def odd_mixer(l):
            oi = l // 2
            k.fence()
            cv = Carver()
            WSo = [cv.take([KC, 512], BF16) for _ in range(2)]
            STG = [cv.take([T], BF16) for _ in range(4)]
            for g in range(2):
                load_w(WSo[g], od_w_in[oi][:, g * 512:(g + 1) * 512], 512, 'WSo%d' % g)
            prenorm(1, Carver(off=cv.off))
            k.fence()
            kscale = 128.0 ** -0.5
            ev = [0]
            for g in range(14):
                s = g % 2
                if g >= 2:
                    load_w(WSo[s], od_w_in[oi][:, g * 512:(g + 1) * 512], 512, 'WSo%d' % s)
                for c in range(4):
                    chn = g * 4 + c
                    sg = STG[chn % 4]
                    sc = kscale if 8 <= chn < 16 else 1.0
                    for (t0, tn) in TBS:
                        b = fm_unit(WSo[s], c * 128, 128, H, KC, t0, tn, ['WSo%d' % s, 'H'])
                        ev[0] += 1
                        if ev[0] % 2 == 0:
                            k.op('act', lambda e, b=b, sg=sg, t0=t0, tn=tn, sc=sc: e.activation(sg[:, t0:t0 + tn], PS[b][:, 0:tn], AF.Copy, scale=float(sc)),
                                 reads=['ps%d' % b], writes=['STG%d' % (chn % 4)])
                        else:
                            k.op('dve', lambda e, b=b, sg=sg, t0=t0, tn=tn, sc=sc: e.tensor_scalar_mul(sg[:, t0:t0 + tn], PS[b][:, 0:tn], float(sc)),
                                 reads=['ps%d' % b], writes=['STG%d' % (chn % 4)])
                    k.dma('sp', USCR[chn], sg, reads=['STG%d' % (chn % 4)], writes=['U%d' % chn], key='Uw%d' % (chn % 4))
            k.fence()
            OMIX = H
            retention(oi, OMIX)
            if dbg in ('odr', 'odr_only'):
                k.op('dve', lambda e: e.memset(OMIX[:, 8:16, :], 0.0), reads=['OMIX'], writes=['OMIX'])
            else:
                hyena(oi, OMIX)
            k.fence()
            cw = Carver()
            WOo = [cw.take([KC, 256], BF16) for _ in range(2)]
            GATE = (3 * 1 + 2) * 16
            for g in range(8):
                s = g % 2
                k.dma('pool', WOo[s], od_w_out[oi][:, g * 256:(g + 1) * 256].rearrange("(k p) c -> p k c", p=128), writes=['WOo%d' % s])
                for c in range(2):
                    oc = g * 2 + c
                    for (t0, tn) in TBS:
                        b = fm_unit(WOo[s], c * 128, 128, OMIX, KC, t0, tn, ['WOo%d' % s, 'OMIX'])
                        for sl in range(t0 // L, (t0 + tn) // L):
                            c0 = sl * L - t0
                            k.op('dve', lambda e, b=b, oc=oc, sl=sl, c0=c0: e.scalar_tensor_tensor(
                                X[:, oc, sl * L:(sl + 1) * L], PS[b][:, c0:c0 + L], MODV[:, GATE + oc, sl:sl + 1], X[:, oc, sl * L:(sl + 1) * L], ALU.mult, ALU.add),
                                reads=['ps%d' % b, 'MODV', 'X'], writes=['X'])

        def hyena(oi, OMIX):
            k.fence()
            ch_ = Carver()
            DFTT = ch_.take([8, 512], BF16)
            FB = ch_.take([8], F32)
            Z2 = ch_.take([1280], F32)
            cm_ = Carver(off=ch_.off + 2048)
            FE = cm_.take([1280], F32)
            W1 = cm_.take([64], F32)
            W2 = cm_.take([64], F32)
            Z1 = cm_.take([1280], F32)
            RTMP = cm_.take([512], F32)
            VEC2 = ch_.take([128], F32)
            NEGR = ch_.take([16], F32)
            UIN = ch_.take([T], BF16)
            V = ch_.take([T], F32)
            VS = ch_.take([1024], F32)
            XX = ch_.take([T], F32)
            XXS = ch_.take([1024], F32)
            VTKp = ch_.take([10, 128], BF16)
            VTKs = ch_.take([8, 128], BF16)
            HP = ch_.take([4, 128], F32)
            HS = ch_.take([16, 128], F32)
            FTp = ch_.take([2, 128], BF16)
            FTs = ch_.take([8, 128], BF16)
            W3c = ch_.take([128], F32)
            ABSD = ch_.take([128], F32)
            WIN = [ch_.take([128], F32) for _ in range(2)]
            T1 = [ch_.take([4, 128], F32) for _ in range(2)]
            T2 = [ch_.take([4, 128], F32) for _ in range(2)]
            YCp = ch_.take([10, 128], BF16)
            YSp = ch_.take([10, 128], BF16)
            YCf = ch_.take([10, 128], F32)
            YSf = ch_.take([10, 128], F32)
            YCs = ch_.take([10, 128], BF16)
            YSs = ch_.take([10, 128], BF16)
            DT = DFTT.rearrange("p a (r n) -> p a r n", r=2)

            for a_ in range(8):
                k.dma('pool', DT[:, a_, :, :], dft_in[a_].rearrange("(r p) n -> p r n", p=128), writes=['DFTT'])
            k.dma('sp', FE[0:33, :], feat_in, writes=['FE'])
            k.dma('sp', W1[0:33, 0:64], hy_w1[oi], writes=['W1'])
            k.dma('sp', W2[0:64, 0:64], hy_w2[oi], writes=['W2'])
            k.dma('sp', FB[0:64, 0:4], fb_in, writes=['FB'])
            k.dma('sp', VEC2, vec2_in, writes=['VEC2'])
            k.dma('sp', NEGR[:, 0:10], negr_in, writes=['NEGR'])
            k.op('dve', lambda e: e.tensor_tensor(FB[0:64, 4:5], FB[0:64, 0:1], FB[0:64, 2:3], ALU.mult), reads=['FB'], writes=['FB'])
            k.op('dve', lambda e: e.tensor_tensor(FB[0:64, 5:6], FB[0:64, 1:2], FB[0:64, 3:4], ALU.mult), reads=['FB'], writes=['FB'])
            k.op('dve', lambda e: e.memset(FB[0:64, 6:7], -math.pi), reads=['FB'], writes=['FB'])
            for (src, W, kin, fcol, dst, skey, dkey) in ((FE, W1, 33, 2, Z1, 'FE', 'Z1'), (Z1, W2, 64, 3, Z2, 'Z1', 'Z2')):
                for (c0, cn) in ((0, 256), (256, 512), (768, 512)):
                    b = k.next_bank()
                    k.op('pe', lambda e, b=b, W=W, kin=kin, src=src, c0=c0, cn=cn: e.matmul(PS[b][0:64, 0:cn], W[0:kin, 0:64], src[0:kin, c0:c0 + cn], start=True, stop=True),
                         reads=[skey, 'W1', 'W2'], writes=['ps%d' % b])
                    k.op('dve', lambda e, b=b, dst=dst, c0=c0, cn=cn, fcol=fcol: e.tensor_scalar(dst[0:64, c0:c0 + cn], PS[b][0:64, 0:cn], FB[0:64, fcol:fcol + 1], FB[0:64, fcol + 2:fcol + 3], ALU.mult, ALU.add),
                         reads=['ps%d' % b, 'FB'], writes=[dkey])
                    for rep in range(2):
                        for (cmp_, thr, adj) in ((ALU.is_gt, math.pi, -2.0 * math.pi), (ALU.is_lt, -math.pi, 2.0 * math.pi)):
                            k.op('dve', lambda e, dst=dst, c0=c0, cn=cn, cmp_=cmp_, thr=thr, adj=adj: e.tensor_scalar(RTMP[0:64, 0:cn], dst[0:64, c0:c0 + cn], float(thr), float(adj), cmp_, ALU.mult),
                                 reads=[dkey], writes=['RTMP'])
                            k.op('dve', lambda e, dst=dst, c0=c0, cn=cn: e.tensor_tensor(dst[0:64, c0:c0 + cn], dst[0:64, c0:c0 + cn], RTMP[0:64, 0:cn], ALU.add),
                                 reads=[dkey, 'RTMP'], writes=[dkey])
                    k.op('act', lambda e, dst=dst, c0=c0, cn=cn: e.activation(dst[0:64, c0:c0 + cn], dst[0:64, c0:c0 + cn], AF.Sin),
                         reads=[dkey, 'FB'], writes=[dkey])
            cnt = {'w': 0, 't': 0}
            k.fence()

            def short_conv(cidx, OUTP, OUTS, pkey, skey):
                w0 = VEC2[:, cidx:cidx + 1]
                w1 = VEC2[:, 24 + cidx:25 + cidx]
                w2 = VEC2[:, 48 + cidx:49 + cidx]
                bb = VEC2[:, 72 + cidx:73 + cidx]
                k.dma('sp', UIN, USCR[32 + cidx], reads=['U%d' % (32 + cidx)], writes=['UIN'])
                k.op('dve', lambda e: e.tensor_scalar(OUTP, UIN, w1, bb, ALU.mult, ALU.add), reads=['UIN', 'VEC2'], writes=[pkey])
                for s_ in range(NSLOT):
                    a, z_ = s_ * L, (s_ + 1) * L
                    k.op('dve', lambda e, a=a, z_=z_: e.scalar_tensor_tensor(OUTP[:, a + 1:z_], UIN[:, a:z_ - 1], w0, OUTP[:, a + 1:z_], ALU.mult, ALU.add),
                         reads=['UIN', 'VEC2', pkey], writes=[pkey])
                    k.op('dve', lambda e, a=a, z_=z_: e.scalar_tensor_tensor(OUTP[:, a:z_ - 1], UIN[:, a + 1:z_], w2, OUTP[:, a:z_ - 1], ALU.mult, ALU.add),
                         reads=['UIN', 'VEC2', pkey], writes=[pkey])
                k.op('dve', lambda e: e.tensor_scalar(OUTS, UIN[:, 0:1024], w1, bb, ALU.mult, ALU.add), reads=['UIN', 'VEC2'], writes=[skey])
                k.op('dve', lambda e: e.scalar_tensor_tensor(OUTS[:, 1:1024], UIN[:, 0:1023], w0, OUTS[:, 1:1024], ALU.mult, ALU.add), reads=['UIN', 'VEC2', skey], writes=[skey])
                k.op('dve', lambda e: e.scalar_tensor_tensor(OUTS[:, 0:1023], UIN[:, 1:1024], w2, OUTS[:, 0:1023], ALU.mult, ALU.add), reads=['UIN', 'VEC2', skey], writes=[skey])

            def filters(o, c):
                chn = o * 1024 + c * 128
                k.dma('sp', W3c[0:64, :], hy_w3[oi][:, chn:chn + 128], writes=['W3c'])
                k.dma('sp', ABSD, decb_in[:, chn:chn + 128], writes=['ABSD'])
                k.op('act', lambda e: e.activation(ABSD, ABSD, AF.Abs), reads=['ABSD'], writes=['ABSD'])
                for tile in range(10):
                    zc0 = tile * 128
                    if tile < 2:
                        dst = FTp[:, tile, :]
                        dkey = 'FTp'
                    else:
                        t_idx = tile - 2
                        dst = FTs[:, (t_idx % 2) * 4 + t_idx // 2, :]
                        dkey = 'FTs'
                    b = k.next_bank()
                    k.op('pe', lambda e, b=b, zc0=zc0: e.matmul(PS[b][:, 0:128], Z2[0:64, zc0:zc0 + 128], W3c[0:64, :], start=True, stop=True),
                         reads=['Z2', 'W3c'], writes=['ps%d' % b])
                    wi = cnt['w'] % 2
                    cnt['w'] += 1
                    k.op('act', lambda e, wi=wi, tile=tile: e.activation(WIN[wi], ABSD, AF.Exp, scale=NEGR[:, tile:tile + 1]), reads=['ABSD', 'NEGR'], writes=['WIN%d' % wi])
                    k.op('dve', lambda e, b=b, wi=wi, dst=dst: e.tensor_tensor(dst, PS[b][:, 0:128], WIN[wi], ALU.mult), reads=['ps%d' % b, 'WIN%d' % wi], writes=[dkey])
                b = k.next_bank()
                def f(e, b=b):
                    ins = None
                    for cs in range(2):
                        for kc in range(2):
                            o_ = PS[b][:, (cs * 2 + kc) * 128:(cs * 2 + kc + 1) * 128]
                            for tt in range(2):
                                ins = e.matmul(o_, DT[:, cs, tt, kc * 128:(kc + 1) * 128], FTp[:, tt, :], start=(tt == 0), stop=(tt == 1))
                    return ins
                k.op('pe', f, reads=['DFTT', 'FTp'], writes=['ps%d' % b])
                k.op('act', lambda e, b=b: e.activation(HP, PS[b][:].rearrange("p (a c) -> p a c", a=4), AF.Copy), reads=['ps%d' % b], writes=['HP'])
                for cs in range(2):
                    for kc in range(2):
                        b = k.next_bank()
                        def f(e, b=b, cs=cs, kc=kc):
                            ins = None
                            for tt in range(2):
                                ins = e.matmul(PS[b][:, 0:512], DT[:, cs, tt, kc * 128:(kc + 1) * 128], FTs[:, tt * 4:(tt + 1) * 4, :], start=(tt == 0), stop=(tt == 1))
                            return ins
                        k.op('pe', f, reads=['DFTT', 'FTs'], writes=['ps%d' % b])
                        k.op('act', lambda e, b=b, cs=cs, kc=kc: e.activation(HS[:, (cs * 2 + kc) * 4:(cs * 2 + kc + 1) * 4, :], PS[b][:].rearrange("p (a c) -> p a c", a=4), AF.Copy),
                             reads=['ps%d' % b], writes=['HS'])

            def to_tokmajor(SRC, nblk, DST, skey, dkey):
                blocks = [(tt, bl) for tt in range(2) for bl in range(nblk)]
                for g0 in range(0, len(blocks), 4):
                    grp = blocks[g0:g0 + 4]
                    b = k.next_bank()
                    def f(e, b=b, grp=grp):
                        ins = None
                        for q, (tt, bl) in enumerate(grp):
                            ins = e.transpose(PS[b][:, q * 128:(q + 1) * 128], SRC[:, bl * 256 + tt * 128:bl * 256 + (tt + 1) * 128], IDF[:])
                        return ins
                    k.op('pe', f, reads=[skey, 'IDF'], writes=['ps%d' % b])
                    n = len(grp)
                    k.op('act', lambda e, b=b, g0=g0, n=n: e.activation(DST[:, g0:g0 + n, :], PS[b][:, 0:n * 128].rearrange("p (q c) -> p q c", q=n), AF.Copy),
                         reads=['ps%d' % b], writes=[dkey])

            def tmp():
                i = cnt['t'] % 2
                cnt['t'] += 1
                return i

            def cmul(Ap, Bp, Ah, Bh, n, psk, hkey, outC, outS, ckey, accumulate):
                AhB = Ah[:, None, :].broadcast_to([128, n, 128])
                BhB = Bh[:, None, :].broadcast_to([128, n, 128])
                i = tmp()
                t1, t2 = T1[i][:, 0:n, :], T2[i][:, 0:n, :]
                k.op('dve', lambda e: e.tensor_tensor(t1, Ap, AhB, ALU.mult), reads=psk + [hkey], writes=['T1%d' % i])
                k.op('dve', lambda e: e.tensor_tensor(t2, Bp, BhB, ALU.mult), reads=psk + [hkey], writes=['T2%d' % i])
                if accumulate:
                    k.op('dve', lambda e: e.tensor_tensor(t1, t1, t2, ALU.subtract), reads=['T1%d' % i, 'T2%d' % i], writes=['T1%d' % i])
                    k.op('dve', lambda e: e.tensor_tensor(outC, outC, t1, ALU.add), reads=['T1%d' % i, ckey], writes=[ckey])
                else:
                    k.op('dve', lambda e: e.tensor_tensor(outC, t1, t2, ALU.subtract), reads=['T1%d' % i, 'T2%d' % i], writes=[ckey])
                i2 = tmp()
                u1, u2 = T1[i2][:, 0:n, :], T2[i2][:, 0:n, :]
                k.op('dve', lambda e: e.tensor_tensor(u1, Ap, BhB, ALU.mult), reads=psk + [hkey], writes=['T1%d' % i2])
                k.op('dve', lambda e: e.tensor_tensor(u2, Bp, AhB, ALU.mult), reads=psk + [hkey], writes=['T2%d' % i2])
                if accumulate:
                    k.op('dve', lambda e: e.tensor_tensor(u1, u1, u2, ALU.add), reads=['T1%d' % i2, 'T2%d' % i2], writes=['T1%d' % i2])
                    k.op('dve', lambda e: e.tensor_tensor(outS, outS, u1, ALU.add), reads=['T1%d' % i2, ckey], writes=[ckey])
                else:
                    k.op('dve', lambda e: e.tensor_tensor(outS, u1, u2, ALU.add), reads=['T1%d' % i2, 'T2%d' % i2], writes=[ckey])

            def long_conv(skipcol):
                sk = VEC2[:, skipcol:skipcol + 1]
                to_tokmajor(V, NSLOT, VTKp, 'V', 'VTKp')
                to_tokmajor(VS, 4, VTKs, 'VS', 'VTKs')
                for kc in range(2):
                    banks = []
                    for cs in range(2):
                        b1 = k.next_bank()
                        b2 = k.next_bank()
                        def f(e, b1=b1, b2=b2, cs=cs, kc=kc):
                            ins = None
                            for tt in range(2):
                                e.matmul(PS[b1][:, 0:512], DT[:, cs, tt, kc * 128:(kc + 1) * 128], VTKp[:, tt * 5:tt * 5 + 4, :], start=(tt == 0), stop=(tt == 1))
                            for tt in range(2):
                                ins = e.matmul(PS[b2][:, 0:128], DT[:, cs, tt, kc * 128:(kc + 1) * 128], VTKp[:, tt * 5 + 4, :], start=(tt == 0), stop=(tt == 1))
                            return ins
                        k.op('pe', f, reads=['DFTT', 'VTKp'], writes=['ps%d' % b1, 'ps%d' % b2])
                        banks.append((b1, b2))
                    psk = ['ps%d' % banks[0][0], 'ps%d' % banks[0][1], 'ps%d' % banks[1][0], 'ps%d' % banks[1][1]]
                    v4 = lambda bq: PS[bq][:, 0:512].rearrange("p (a c) -> p a c", c=128)
                    v1 = lambda bq: PS[bq][:, 0:128].rearrange("p (a c) -> p a c", c=128)
                    cmul(v4(banks[0][0]), v4(banks[1][0]), HP[:, kc, :], HP[:, 2 + kc, :], 4, psk, 'HP', YCp[:, kc * 5:kc * 5 + 4, :], YSp[:, kc * 5:kc * 5 + 4, :], 'YP', False)
                    cmul(v1(banks[0][1]), v1(banks[1][1]), HP[:, kc, :], HP[:, 2 + kc, :], 1, psk, 'HP', YCp[:, kc * 5 + 4:kc * 5 + 5, :], YSp[:, kc * 5 + 4:kc * 5 + 5, :], 'YP', False)
                for s_ in range(NSLOT):
                    b = k.next_bank()
                    def f(e, b=b, s_=s_):
                        ins = None
                        for kc in range(2):
                            e.matmul(PS[b][:, 0:256], YCp[:, kc * 5 + s_, :], DT[:, 2, kc, :], start=(kc == 0), stop=False)
                            ins = e.matmul(PS[b][:, 0:256], YSp[:, kc * 5 + s_, :], DT[:, 3, kc, :], start=False, stop=(kc == 1))
                        return ins
                    k.op('pe', f, reads=['YP', 'DFTT'], writes=['ps%d' % b])
                    k.op('dve', lambda e, b=b, s_=s_: e.scalar_tensor_tensor(V[:, s_ * L:(s_ + 1) * L], V[:, s_ * L:(s_ + 1) * L], sk, PS[b][:, 0:256], ALU.mult, ALU.add),
                         reads=['ps%d' % b, 'V', 'VEC2'], writes=['V'])
                for kc in range(2):
                    bA = k.next_bank()
                    bB = k.next_bank()
                    def f(e, bA=bA, bB=bB, kc=kc):
                        ins = None
                        for cs, bq in ((0, bA), (1, bB)):
                            for tt in range(2):
                                ins = e.matmul(PS[bq][:, 0:512], DT[:, cs, tt, kc * 128:(kc + 1) * 128], VTKs[:, tt * 4:(tt + 1) * 4, :], start=(tt == 0), stop=(tt == 1))
                        return ins
                    k.op('pe', f, reads=['DFTT', 'VTKs'], writes=['ps%d' % bA, 'ps%d' % bB])
                    if kc == 0:
                        k.op('dve', lambda e: e.memset(YCf, 0.0), reads=['YF'], writes=['YF'])
                        k.op('dve', lambda e: e.memset(YSf, 0.0), reads=['YF'], writes=['YF'])
                    for j_ in range(4):
                        i_lo, i_hi = max(0, 1 - j_), min(3, 5 - j_)
                        n_ = i_hi - i_lo + 1
                        o0 = kc * 5 + (i_lo + j_) - 1
                        va = lambda bq: PS[bq][:, i_lo * 128:(i_hi + 1) * 128].rearrange("p (a c) -> p a c", c=128)
                        cmul(va(bA), va(bB), HS[:, kc * 4 + j_, :], HS[:, (2 + kc) * 4 + j_, :], n_, ['ps%d' % bA, 'ps%d' % bB], 'HS',
                             YCf[:, o0:o0 + n_, :], YSf[:, o0:o0 + n_, :], 'YF', True)
                    k.op('act', lambda e, kc=kc: e.activation(YCs[:, kc * 5:kc * 5 + 5, :], YCf[:, kc * 5:kc * 5 + 5, :], AF.Copy), reads=['YF'], writes=['YS_'])
                    k.op('act', lambda e, kc=kc: e.activation(YSs[:, kc * 5:kc * 5 + 5, :], YSf[:, kc * 5:kc * 5 + 5, :], AF.Copy), reads=['YF'], writes=['YS_'])
                for r in range(4):
                    b = k.next_bank()
                    def f(e, b=b, r=r):
                        ins = None
                        first = True
                        for kc in range(2):
                            for (Y, s_, tab) in ((YCs, r + 2, 4), (YSs, r + 2, 5), (YCs, r + 1, 6), (YSs, r + 1, 7)):
                                ins = e.matmul(PS[b][:, 0:256], Y[:, kc * 5 + s_ - 1, :], DT[:, tab, kc, :], start=first, stop=(kc == 1 and tab == 7))
                                first = False
                        return ins
                    k.op('pe', f, reads=['YS_', 'DFTT'], writes=['ps%d' % b])
                    k.op('dve', lambda e, b=b, r=r: e.scalar_tensor_tensor(VS[:, r * L:(r + 1) * L], VS[:, r * L:(r + 1) * L], sk, PS[b][:, 0:256], ALU.mult, ALU.add),
                         reads=['ps%d' % b, 'VS', 'VEC2'], writes=['VS'])

            for c in range(8):
                short_conv(c, V, VS, 'V', 'VS')
                short_conv(8 + c, XX, XXS, 'XX', 'XXS')
                filters(0, c)
                long_conv(96 + c)
                k.op('dve', lambda e: e.tensor_tensor(V, V, XX, ALU.mult), reads=['V', 'XX'], writes=['V'])
                k.op('dve', lambda e: e.tensor_tensor(VS, VS, XXS, ALU.mult), reads=['VS', 'XXS'], writes=['VS'])
                short_conv(16 + c, XX, XXS, 'XX', 'XXS')
                filters(1, c)
                long_conv(104 + c)
                k.op('dve', lambda e: e.tensor_tensor(V, V, XX, ALU.mult), reads=['V', 'XX'], writes=['V'])
                k.op('dve', lambda e: e.tensor_tensor(VS, VS, XXS, ALU.mult), reads=['VS', 'XXS'], writes=['VS'])
                k.op('dve', lambda e: e.tensor_scalar_mul(V[:, 0:1024], V[:, 0:1024], VEC[:, 14:15]), reads=['V', 'VEC'], writes=['V'])
                k.op('dve', lambda e: e.scalar_tensor_tensor(V[:, 0:1024], VS, VEC[:, 15:16], V[:, 0:1024], ALU.mult, ALU.add), reads=['V', 'VS', 'VEC'], writes=['V'])
                k.op('act', lambda e, c=c: e.activation(OMIX[:, 8 + c, :], V, AF.Copy), reads=['V'], writes=['OMIX'])

        def retention(oi, OMIX):
            cr = Carver()
            RC = cr.take([6, 128], F32)
            PC = cr.take([4], F32)
            LG = cr.take([16], F32)
            k.dma('sp', RC, rc_in.rearrange("a p n -> p a n"), writes=['RC'])
            k.dma('sp', PC, pc_in, writes=['PC'])
            k.dma('sp', LG, lgt_in, writes=['LG'])
            k.op('act', lambda e: e.activation(LG, LG, AF.Exp, scale=-1.0), reads=['LG'], writes=['LG'])
            k.op('dve', lambda e: e.tensor_scalar_add(LG, LG, 1.0), reads=['LG'], writes=['LG'])
            k.op('act', lambda e: e.activation(LG, LG, AF.Ln), reads=['LG'], writes=['LG'])
            k.op('dve', lambda e: e.tensor_scalar_mul(LG, LG, -1.0), reads=['LG'], writes=['LG'])
            QT = cr.take([T], BF16)
            KT = cr.take([T], BF16)
            GT = cr.take([T], BF16)
            VT = cr.take([T], BF16)
            KTOK = cr.take([10, 128], BF16)
            VTOK = cr.take([10, 128], BF16)
            Mh = cr.take([128], F32)
            E2 = cr.take([128], F32)
            QDF = cr.take([128], F32)
            QDB = cr.take([128], F32)
            KD = cr.take([4], F32)
            PST = cr.take([10, 256], F32)
            PF0b = cr.take([5, 128], BF16)
            PB1b = cr.take([5, 128], BF16)
            SOUT = [cr.take([128], F32) for _ in range(2)]
            SFb = cr.take([8, 128], BF16)
            SBb = cr.take([8, 128], BF16)
            SF = cr.take([128], F32)
            SB = cr.take([128], F32)
            KFB = [cr.take([2, 128], BF16) for _ in range(2)]
            AM = [cr.take([128], BF16) for _ in range(2)]
            QFB = [cr.take([2, 128], BF16) for _ in range(2)]
            OR = cr.take([T], F32)
            ORS = cr.take([1024], F32)
            SQ = cr.take([512], BF16)
            RSQ = cr.take([512], F32)
            SG = cr.take([T], F32)
            for h in range(8):
                for (tl, chn, key) in ((QT, h, 'QT'), (KT, 8 + h, 'KT'), (VT, 16 + h, 'VT'), (GT, 24 + h, 'GT')):
                    k.dma('sp', tl, USCR[chn], reads=['U%d' % chn], writes=[key])
                for (src, dst, skey, dkey) in ((KT, KTOK, 'KT', 'KTOK'), (VT, VTOK, 'VT', 'VTOK')):
                    for g3 in range(3):
                        nt = 4 if g3 < 2 else 2
                        b = k.next_bank()
                        pb = PS[b][:].bitcast(BF16)
                        def f(e, pb=pb, src=src, g3=g3, nt=nt):
                            ins = None
                            for q in range(nt):
                                ti = g3 * 4 + q
                                ins = e.transpose(pb[:, q * 128:(q + 1) * 128], src[:, ti * 128:(ti + 1) * 128], IDB[:])
                            return ins
                        k.op('pe', f, reads=[skey, 'IDB'], writes=['ps%d' % b])
                        k.op('dve', lambda e, pb=pb, dst=dst, g3=g3, nt=nt: e.tensor_copy(dst[:, g3 * 4:g3 * 4 + nt, :], pb[:, 0:nt * 128].rearrange("p (q c) -> p q c", q=nt)),
                             reads=['ps%d' % b], writes=[dkey])
                lgf = LG[:, h:h + 1]
                lgb = LG[:, 8 + h:9 + h]
                k.op('act', lambda e, lgf=lgf: e.activation(Mh, RC[:, 0, :], AF.Exp, scale=lgf), reads=['RC', 'LG'], writes=['Mh'])
                k.op('dve', lambda e: e.tensor_tensor(Mh, Mh, RC[:, 2, :], ALU.mult), reads=['Mh', 'RC'], writes=['Mh'])
                k.op('act', lambda e, lgb=lgb: e.activation(E2, RC[:, 1, :], AF.Exp, scale=lgb), reads=['RC', 'LG'], writes=['E2'])
                k.op('dve', lambda e: e.tensor_tensor(E2, E2, RC[:, 3, :], ALU.mult), reads=['E2', 'RC'], writes=['E2'])
                k.op('dve', lambda e: e.tensor_tensor(Mh, Mh, E2, ALU.add), reads=['E2', 'Mh'], writes=['Mh'])
                k.op('act', lambda e, lgf=lgf: e.activation(QDF, RC[:, 4, :], AF.Exp, scale=lgf), reads=['RC', 'LG'], writes=['QDF'])
                k.op('act', lambda e, lgb=lgb: e.activation(QDB, RC[:, 5, :], AF.Exp, scale=lgb), reads=['RC', 'LG'], writes=['QDB'])
                k.op('act', lambda e, lgf=lgf: e.activation(KD[:, 0:1], PC[:, 0:1], AF.Exp, scale=lgf), reads=['PC', 'LG'], writes=['KD'])
                k.op('act', lambda e, lgb=lgb: e.activation(KD[:, 1:2], PC[:, 1:2], AF.Exp, scale=lgb), reads=['PC', 'LG'], writes=['KD'])
                k.op('act', lambda e, lgf=lgf: e.activation(KD[:, 2:3], PC[:, 2:3], AF.Exp, scale=lgf), reads=['PC', 'LG'], writes=['KD'])
                k.op('act', lambda e, lgb=lgb: e.activation(KD[:, 3:4], PC[:, 2:3], AF.Exp, scale=lgb), reads=['PC', 'LG'], writes=['KD'])
                for ci in range(10):
                    s2 = ci % 2
                    k.op('dve', lambda e, ci=ci, s2=s2: e.tensor_scalar_mul(KFB[s2][:, 0, :], KTOK[:, ci, :], KD[:, 0:1]), reads=['KTOK', 'KD'], writes=['KFB%d' % s2])
                    k.op('dve', lambda e, ci=ci, s2=s2: e.tensor_scalar_mul(KFB[s2][:, 1, :], KTOK[:, ci, :], KD[:, 1:2]), reads=['KTOK', 'KD'], writes=['KFB%d' % s2])
                    b = k.next_bank()
                    def f(e, b=b, ci=ci, s2=s2):
                        e.matmul(PS[b][:, 0:128], KFB[s2][:, 0, :], VTOK[:, ci, :], start=True, stop=True)
                        return e.matmul(PS[b][:, 128:256], KFB[s2][:, 1, :], VTOK[:, ci, :], start=True, stop=True)
                    k.op('pe', f, reads=['KFB%d' % s2, 'VTOK'], writes=['ps%d' % b])
                    k.op('act', lambda e, b=b, ci=ci: e.activation(PST[:, ci, :], PS[b][:, 0:256], AF.Copy), reads=['ps%d' % b], writes=['PST'])
                for s_ in range(NSLOT):
                    c0_, c1_ = 2 * s_, 2 * s_ + 1
                    k.op('act', lambda e, s_=s_, c0_=c0_: e.activation(PF0b[:, s_, :], PST[:, c0_, 0:128], AF.Copy), reads=['PST'], writes=['PF0b'])
                    k.op('act', lambda e, s_=s_, c1_=c1_: e.activation(PB1b[:, s_, :], PST[:, c1_, 128:256], AF.Copy), reads=['PST'], writes=['PB1b'])
                    k.op('dve', lambda e, c0_=c0_, c1_=c1_: e.scalar_tensor_tensor(SOUT[0], PST[:, c0_, 0:128], KD[:, 2:3], PST[:, c1_, 0:128], ALU.mult, ALU.add),
                         reads=['PST', 'KD'], writes=['SOUT0'])
                    k.dma('sp', o_ret[s_, 0, h], SOUT[0], reads=['SOUT0'], writes=['ret%d_0_%d' % (s_, h)], key='retout0')
                    k.op('dve', lambda e, c0_=c0_, c1_=c1_: e.scalar_tensor_tensor(SOUT[1], PST[:, c1_, 128:256], KD[:, 3:4], PST[:, c0_, 128:256], ALU.mult, ALU.add),
                         reads=['PST', 'KD'], writes=['SOUT1'])
                    k.dma('sp', o_ret[s_, 1, h], SOUT[1], reads=['SOUT1'], writes=['ret%d_1_%d' % (s_, h)], key='retout1')
                    out_keys.append('ret%d_0_%d' % (s_, h))
                    out_keys.append('ret%d_1_%d' % (s_, h))
                k.dma('sp', SF, s0_in[0, h], writes=['SF'])
                k.dma('sp', SB, s0_in[1, h], writes=['SB'])
                for j in range(8):
                    k.op('act', lambda e, j=j: e.activation(SFb[:, j, :], SF, AF.Copy), reads=['SF'], writes=['SFb'])
                    k.op('dve', lambda e, j=j: e.scalar_tensor_tensor(SF, SF, KD[:, 2:3], PST[:, j, 0:128], ALU.mult, ALU.add), reads=['SF', 'KD', 'PST'], writes=['SF'])
                for j in range(7, -1, -1):
                    k.op('act', lambda e, j=j: e.activation(SBb[:, j, :], SB, AF.Copy), reads=['SB'], writes=['SBb'])
                    k.op('dve', lambda e, j=j: e.scalar_tensor_tensor(SB, SB, KD[:, 3:4], PST[:, j, 128:256], ALU.mult, ALU.add), reads=['SB', 'KD', 'PST'], writes=['SB'])
                pendD = None
                for ci in range(10):
                    s2 = ci % 2
                    cs_ = slice(ci * 128, (ci + 1) * 128)
                    sl_, jj = ci // 2, ci % 2
                    bA = k.next_bank()
                    k.op('pe', lambda e, bA=bA, cs_=cs_: e.matmul(PS[bA][:, 0:128], KT[:, cs_], QT[:, cs_], start=True, stop=True), reads=['KT', 'QT'], writes=['ps%d' % bA])
                    k.op('dve', lambda e, bA=bA, s2=s2: e.tensor_tensor(AM[s2], PS[bA][:, 0:128], Mh, ALU.mult), reads=['ps%d' % bA, 'Mh'], writes=['AM%d' % s2])
                    k.op('dve', lambda e, s2=s2, cs_=cs_: e.tensor_tensor(QFB[s2][:, 0, :], QT[:, cs_], QDF, ALU.mult), reads=['QT', 'QDF'], writes=['QFB%d' % s2])
                    k.op('dve', lambda e, s2=s2, cs_=cs_: e.tensor_tensor(QFB[s2][:, 1, :], QT[:, cs_], QDB, ALU.mult), reads=['QT', 'QDB'], writes=['QFB%d' % s2])
                    if pendD is not None:
                        pendD()
                    def emitO(ci=ci, s2=s2, sl_=sl_, jj=jj, cs_=cs_):
                      bO = k.next_bank()
                      def f(e, bO=bO, ci=ci, s2=s2, sl_=sl_, jj=jj):
                          e.matmul(PS[bO][:, 0:128], VTOK[:, ci, :], AM[s2], start=True, stop=False)
                          if jj == 1:
                              ins = e.matmul(PS[bO][:, 0:128], PF0b[:, sl_, :], QFB[s2][:, 0, :], start=False, stop=True)
                          else:
                              ins = e.matmul(PS[bO][:, 0:128], PB1b[:, sl_, :], QFB[s2][:, 1, :], start=False, stop=True)
                          if ci < 8:
                              e.matmul(PS[bO][:, 128:256], VTOK[:, ci, :], AM[s2], start=True, stop=False)
                              e.matmul(PS[bO][:, 128:256], SFb[:, ci, :], QFB[s2][:, 0, :], start=False, stop=False)
                              ins = e.matmul(PS[bO][:, 128:256], SBb[:, ci, :], QFB[s2][:, 1, :], start=False, stop=True)
                          return ins
                      k.op('pe', f, reads=['VTOK', 'AM%d' % s2, 'QFB%d' % s2, 'PF0b', 'PB1b', 'SFb', 'SBb'], writes=['ps%d' % bO])
                      k.op('act', lambda e, bO=bO, cs_=cs_: e.activation(OR[:, cs_], PS[bO][:, 0:128], AF.Copy), reads=['ps%d' % bO], writes=['OR'])
                      if ci < 8:
                          k.op('act', lambda e, bO=bO, cs_=cs_: e.activation(ORS[:, cs_], PS[bO][:, 128:256], AF.Copy), reads=['ps%d' % bO], writes=['ORS'])
                    pendD = emitO
                pendD()
                k.op('dve', lambda e: e.tensor_scalar_mul(OR[:, 0:1024], OR[:, 0:1024], VEC[:, 14:15]), reads=['OR', 'VEC'], writes=['OR'])
                k.op('dve', lambda e: e.scalar_tensor_tensor(OR[:, 0:1024], ORS[:, 0:1024], VEC[:, 15:16], OR[:, 0:1024], ALU.mult, ALU.add), reads=['ORS', 'OR', 'VEC'], writes=['OR'])
                k.op('act', lambda e: e.activation(SG, GT, AF.Silu), reads=['GT'], writes=['SG'])
                for (t0, tn) in TBS:
                    bs = k.next_bank()
                    k.op('act', lambda e, t0=t0, tn=tn: e.activation(SQ[:, 0:tn], OR[:, t0:t0 + tn], AF.Square), reads=['OR'], writes=['SQr'])
                    k.op('pe', lambda e, bs=bs, tn=tn: e.matmul(PS[bs][:, 0:tn], ONESB[:], SQ[:, 0:tn], start=True, stop=True), reads=['SQr', 'ONESB'], writes=['ps%d' % bs])
                    rstd_from(bs, 128, tn, 128, RSQ, 'RSQr')
                    k.op('dve', lambda e, t0=t0, tn=tn, h=h: e.scalar_tensor_tensor(OR[:, t0:t0 + tn], OR[:, t0:t0 + tn], VEC[:, 16 + h:17 + h], RSQ[:, 0:tn], ALU.mult, ALU.mult),
                         reads=['OR', 'VEC', 'RSQr'], writes=['OR'])
                    k.op('dve', lambda e, t0=t0, tn=tn, h=h: e.tensor_tensor(OMIX[:, h, t0:t0 + tn], OR[:, t0:t0 + tn], SG[:, t0:t0 + tn], ALU.mult),
                         reads=['OR', 'SG'], writes=['OMIX'])

        nl = 2
        for l in ([1] if dbg in ('odr_only', 'od_only') else range(nl)):
            if dbg == 'load':
                break
            mod_phase(l)
            if dbg == 'mod':
                for sl in range(NSLOT):
                    k.op('dve', lambda e, sl=sl: e.tensor_copy(X[:, 0:9, sl * L:sl * L + 16], MODV[:, :, sl].rearrange("p (a b) -> p a b", a=9)),
                         reads=['MODV', 'X'], writes=['X'])
                break
            if dbg == 'pre':
                cvq = Carver()
                prenorm(0, cvq)
                k.op('dve', lambda e: e.tensor_copy(X[:], H[:]), reads=['H', 'X'], writes=['X'])
                break
            ffn(l, 0, 0)
            if dbg == 'ffn0':
                break
            if l % 2 == 0:
                even_mixer(l)
            else:
                odd_mixer(l)
            if dbg in ('odr_only', 'od_only', 'odr', 'od'):
                break
            if dbg in ('e1', 'e2', 'ev'):
                break
            ffn(l, 1, 2)

        k.fence()
        cvo = Carver()
        YT = [cvo.take([D], F32) for _ in range(2)]
        for tt in range(T // 128):
            st = YT[tt % 2]
            for k4 in range(KC // 4):
                b = k.next_bank()
                def f(e, tt=tt, k4=k4, b=b):
                    ins = None
                    for q in range(4):
                        kk = k4 * 4 + q
                        ins = e.transpose(PS[b][:, q * 128:(q + 1) * 128], X[:, kk, tt * 128:(tt + 1) * 128], IDF[:])
                    return ins
                k.op('pe', f, reads=['X', 'IDF'], writes=['ps%d' % b])
                if k4 % 2 == 0:
                    k.op('dve', lambda e, b=b, k4=k4, st=st: e.tensor_copy(st[:, k4 * 512:(k4 + 1) * 512], PS[b][:]),
                         reads=['ps%d' % b], writes=['YT%d' % (tt % 2)])
                else:
                    k.op('act', lambda e, b=b, k4=k4, st=st: e.activation(st[:, k4 * 512:(k4 + 1) * 512], PS[b][:], AF.Copy),
                         reads=['ps%d' % b], writes=['YT%d' % (tt % 2)])
            k.dma('sp', y[tt * 128:(tt + 1) * 128, :], st, reads=['YT%d' % (tt % 2)], writes=['y%d' % tt], key='yout%d' % (tt % 2))
        k.final_wait('sp', ['y%d' % tt for tt in range(T // 128)] + out_keys)

        with nc.Block() as block:
            @block.tensor
            def _(e):
                for f in k.prog['pe']:
                    f(e)

            @block.scalar
            def _(e):
                for f in k.prog['act']:
                    f(e)

            @block.vector
            def _(e):
                for f in k.prog['dve']:
                    f(e)

            @block.gpsimd
            def _(e):
                for f in k.prog['pool']:
                    f(e)

            @block.sync
            def _(e):
                for f in k.prog['sp']:
                    f(e)
    return nc


def _slot_assign():
    out = []
    for c in range(NCORES):
        if c < 2:
            out.append([('s', c, q) for q in range(4)] + [('p', c)])
        else:
            out.append([('p', 2 + 5 * (c - 2) + s) for s in range(5)])
    return out


def _rope_tables(dim):
    f32 = np.float32
    n_rows = 1024 // 64
    row = np.repeat(np.arange(n_rows, dtype=f32), 64)
    col = np.tile(np.arange(64, dtype=f32), n_rows)
    half = dim // 2
    freq = (f32(10000.0) ** (-np.arange(0, half, 2, dtype=f32) / f32(half))).astype(f32)
    ang = np.concatenate([row[:, None] * freq[None], col[:, None] * freq[None]], axis=-1).astype(f32)
    cos = np.cos(ang).astype(f32)
    sin = np.sin(ang).astype(f32)
    cosT = np.ascontiguousarray(np.concatenate([cos, cos], axis=1).T)
    sinT = np.ascontiguousarray(np.concatenate([sin, sin], axis=1).T)
    rt = np.zeros((dim, dim), f32)
    for m_ in range(dim):
        if m_ < half:
            rt[m_ + half, m_] = -1.0
        else:
            rt[m_ - half, m_] = 1.0
    return cosT, sinT, rt


def _const_tables():
    c64, s64, rt64 = _rope_tables(64)
    c128, s128, rt128 = _rope_tables(128)
    f32 = np.float32
    ii = np.arange(128, dtype=f32)
    dm = ii[None, :] - ii[:, None]
    rc = np.stack([np.maximum(dm, 0), np.maximum(-dm, 0), (dm >= 0).astype(f32), (dm <= 0).astype(f32),
                   np.broadcast_to(ii[None, :] + 1, (128, 128)), np.broadcast_to(128 - ii[None, :], (128, 128))]).astype(f32)
    pc = np.stack([127 - ii, ii, np.full(128, 128.0, f32), np.zeros(128, f32)], axis=1).astype(f32)
    tt_ = np.arange(256, dtype=np.float64)
    om = 2.0 * np.pi * (np.arange(256, dtype=np.float64) + 0.5) / 512.0
    fc = np.cos(np.outer(tt_, om)); fs = np.sin(np.outer(tt_, om))
    def g(off):
        ph = np.outer(om, tt_ + off)
        return (2.0 / 512.0) * np.cos(ph), (2.0 / 512.0) * np.sin(ph)
    gcm, gsm = g(128); gcl, gsl = g(0); gch, gsh = g(256)
    dft = np.stack([fc, fs, gcm, gsm, gcl, gsl, gch, gsh]).astype(f32)
    def feat(Lq):
        t = np.arange(Lq, dtype=f32); tn = (t / f32(Lq)).astype(f32)
        bands = np.arange(1, 17, dtype=f32)
        ang = (f32(2.0 * math.pi) * tn[:, None] * bands[None, :]).astype(f32)
        return np.concatenate([tn[:, None], np.sin(ang), np.cos(ang)], axis=-1).astype(f32).T
    featT = np.ascontiguousarray(np.concatenate([feat(256), feat(1024)], axis=1))
    def negr(Lq):
        t = np.arange(Lq, dtype=f32)
        return (-(np.abs(t - Lq // 2) / f32(Lq / 2))).astype(f32).reshape(Lq // 128, 128).T
    negr_t = np.ascontiguousarray(np.concatenate([negr(256), negr(1024)], axis=1))
    return {'dft': dft, 'feat': featT, 'negr': negr_t, 'rc': np.ascontiguousarray(rc), 'pc': np.ascontiguousarray(pc), 'identb': np.eye(128, dtype=f32), 'cos64': c64, 'sin64': s64, 'rt64': rt64, 'cos128': c128, 'sin128': s128, 'rt128': rt128}


def make_in_maps(inp):
    f32 = np.float32
    assign = _slot_assign()
    maps = []
    ident = np.eye(128, dtype=f32)
    consts = _const_tables()
    for c in range(NCORES):
        xs = np.empty((T, D), f32)
        cs = np.empty((NSLOT, D), f32)
        for s, a in enumerate(assign[c]):
            if a[0] == 's':
                xs[s * L:(s + 1) * L] = inp['x_sample'][a[1], a[2] * L:(a[2] + 1) * L]
                cs[s] = inp['c'][a[1]]
            else:
                xs[s * L:(s + 1) * L] = inp['x_prompt'][a[1]]
                cs[s] = inp['c_ctx']
        csT = np.ascontiguousarray(cs.reshape(NSLOT, KC, 128).transpose(2, 1, 0))
        b_ = c if c < 2 else 0
        vec = np.zeros((128, 64), f32)
        vec[:, 0:4] = inp['mla_q_norm'][0].reshape(4, 128).T
        vec[:, 4:8] = inp['mla_kv_norm'][0].reshape(4, 128).T
        vec[:, 8] = inp['mla_nope_norm'][0, 0]
        vec[:, 9] = inp['mla_nope_norm'][0, 1]
        vec[0:64, 10] = inp['mla_rope_norm'][0, 0]
        vec[0:64, 11] = inp['mla_rope_norm'][0, 1]
        vec[:, 12] = inp['gqa_qk_norm'][0, 0]
        vec[:, 13] = inp['gqa_qk_norm'][0, 1]
        vec[:, 14] = 0.0 if c < 2 else 1.0
        vec[:, 15] = 1.0 if c < 2 else 0.0
        vec[:, 16:24] = inp['ret_norm'][0].reshape(8, 128).T
        vec2 = np.zeros((128, 128), f32)
        cw = inp['hy_conv_w'][0]
        for kk in range(3):
            vec2[:, kk * 24:(kk + 1) * 24] = cw[kk].reshape(24, 128).T
        vec2[:, 72:96] = inp['hy_conv_b'][0].reshape(24, 128).T
        vec2[:, 96:112] = inp['hy_skip'][0].reshape(16, 128).T
        fb = np.ascontiguousarray(np.stack([inp['hy_filt_b1'][0], inp['hy_filt_b2'][0], inp['hy_filt_freq'][0, 0], inp['hy_filt_freq'][0, 1]], axis=1))
        m = {
            'vec2': vec2, 'fb': fb, 'decb': np.ascontiguousarray(np.broadcast_to(inp['hy_decay'][0][None, :], (128, 2048))),
            'hy_filt_w1': inp['hy_filt_w1'], 'hy_filt_w2': inp['hy_filt_w2'], 'hy_filt_w3': inp['hy_filt_w3'],
            'od_w_in': inp['od_w_in'], 'od_w_out': inp['od_w_out'],
            'lgt': np.ascontiguousarray(np.broadcast_to(inp['ret_decay_logit'][0].reshape(1, 16), (128, 16))),
            's0': np.ascontiguousarray(inp['state_ret'][b_, 0]),
            'vec': vec,
            'ev_w_in': inp['ev_w_in'], 'mla_w_qb': inp['mla_w_qb'], 'mla_w_kvb': inp['mla_w_kvb'], 'ev_w_out': inp['ev_w_out'],
            'c_ckv': np.ascontiguousarray(inp['cache_mla_ckv'][b_, 0]), 'c_kr': np.ascontiguousarray(inp['cache_mla_krope'][b_, 0]),
            'c_k': np.ascontiguousarray(inp['cache_gqa_k'][b_, 0].reshape(256, 256)), 'c_v': np.ascontiguousarray(inp['cache_gqa_v'][b_, 0].reshape(256, 256)),
            **consts,
            'xs': xs, 'csT': csT, 'ident': ident,
            'mod_w': inp['mod_w'], 'mod_b': inp['mod_b'],
            'ffn_w_gate': inp['ffn_w_gate'], 'ffn_w_up': inp['ffn_w_up'], 'ffn_w_down': inp['ffn_w_down'],
        }
        maps.append(m)
    return maps


def kernel(**inputs):
    inp = {k_: np.asarray(v) for k_, v in inputs.items()}
    nc = build_program(dbg=STOP_AFTER)
    maps = make_in_maps(inp)
    res = run_bass_kernel_spmd(nc, maps, core_ids=list(range(NCORES)))
    r = res.results
    f32 = np.float32
    assign = _slot_assign()
    y_prompt = np.zeros((32, 256, D), f32)
    y_sample = np.zeros((2, 1024, D), f32)
    n_ckv = np.zeros((32, 1, 256, 512), f32)
    n_kr = np.zeros((32, 1, 256, 64), f32)
    n_k = np.zeros((32, 1, 256, 2, 128), f32)
    n_v = np.zeros((32, 1, 256, 2, 128), f32)
    n_ret = np.zeros((32, 1, 2, 8, 128, 128), f32)
    for c in range(NCORES):
        yc = np.asarray(r[c]['y'])
        for s_, a in enumerate(assign[c]):
            rows = slice(s_ * L, (s_ + 1) * L)
            if a[0] == 's':
                y_sample[a[1], a[2] * L:(a[2] + 1) * L] = yc[rows]
            else:
                y_prompt[a[1]] = yc[rows]
                for name, dst, shp in (('o_ckv', n_ckv, (256, 512)), ('o_kr', n_kr, (256, 64)),
                                       ('o_k', n_k, (256, 2, 128)), ('o_v', n_v, (256, 2, 128))):
                    if name in r[c]:
                        dst[a[1], 0] = np.asarray(r[c][name])[rows].reshape(shp)
                if 'o_ret' in r[c]:
                    n_ret[a[1], 0] = np.asarray(r[c]['o_ret'])[s_]
    return (y_prompt, y_sample, n_ckv, n_kr, n_k, n_v, n_ret)
```
